# Optimizing a Trainium2 kernel written in Bass

```python
import jax, jax.numpy as jnp
from jax import lax
import numpy as np

D_MODEL = 1024
BATCH = 16
SEQ = 2048
DEPTH = 1

D_MIX = 2 * D_MODEL
A_HEADS = 16
A_HEAD_DIM = 64
A_WIDTH = A_HEADS * A_HEAD_DIM
A_ROT_DIM = A_HEAD_DIM // 4
DILATED_PATTERNS = ((128, 1), (512, 4), (2048, 16))
MLA_HEADS = 8
MLA_Q_RANK = 256
MLA_KV_RANK = 128
MLA_NOPE_DIM = 64
MLA_ROPE_DIM = 32
MLA_V_DIM = 64
MLA_WIDTH = MLA_HEADS * MLA_V_DIM
N_MEM = 256
MEM_HEADS = 4
MEM_HEAD_DIM = 128
MEM_WIDTH = MEM_HEADS * MEM_HEAD_DIM

ROPE_THETA = 500000.0
Q_BLOCK = 128
NORM_EPS = 1e-5
NEG_INF = -1e30
DEEPNORM_ALPHA = (2 * DEPTH) ** 0.25
DEEPNORM_BETA = (8 * DEPTH) ** -0.25

IN_SPLITS = (A_WIDTH, A_WIDTH, A_WIDTH, A_WIDTH,
             MLA_Q_RANK, MLA_KV_RANK, MLA_ROPE_DIM, MLA_WIDTH,
             MEM_WIDTH, MEM_WIDTH)
D_IN = sum(IN_SPLITS)

kernel_name = "hymba_dilated_mla_memory_deepnorm"


def _layer_norm(x, g, b):
    xf = x.astype(jnp.float32)
    mu = jnp.mean(xf, axis=-1, keepdims=True)
    var = jnp.mean(jnp.square(xf - mu), axis=-1, keepdims=True)
    return ((xf - mu) * lax.rsqrt(var + NORM_EPS) * g.astype(jnp.float32) + b.astype(jnp.float32)).astype(x.dtype)


def _rms_norm(x, g, out_dtype):
    xf = x.astype(jnp.float32)
    ms = jnp.mean(jnp.square(xf), axis=-1, keepdims=True)
    return (xf * lax.rsqrt(ms + NORM_EPS) * g.astype(jnp.float32)).astype(out_dtype)


def _rope(x, pos):
    r = x.shape[-1]
    inv_freq = ROPE_THETA ** (-(jnp.arange(0, r, 2, dtype=jnp.float32) / r))
    ang = pos.astype(jnp.float32)[..., None] * inv_freq
    cos, sin = jnp.cos(ang)[:, :, None, :], jnp.sin(ang)[:, :, None, :]
    xf = x.astype(jnp.float32)
    x1, x2 = xf[..., : r // 2], xf[..., r // 2:]
    return jnp.concatenate([x1 * cos - x2 * sin, x2 * cos + x1 * sin], axis=-1).astype(x.dtype)


def _partial_rope(x, pos):
    return jnp.concatenate([_rope(x[..., :A_ROT_DIM], pos), x[..., A_ROT_DIM:]], axis=-1)


def _window_attn(q, k, v, n_side):
    n, length, h, e = q.shape
    blk = n_side
    nb = -(-length // blk)
    pad = nb * blk - length
    qb = jnp.pad(q, ((0, 0), (0, pad), (0, 0), (0, 0))).reshape(n, nb, blk, h, e).astype(jnp.float32)

    def bands(t):
        tb = jnp.pad(t, ((0, 0), (blk, pad + blk), (0, 0), (0, 0))).reshape(n, nb + 2, blk, h, t.shape[-1])
        return jnp.concatenate([tb[:, :-2], tb[:, 1:-1], tb[:, 2:]], axis=2).astype(jnp.float32)

    kb, vb = bands(k), bands(v)
    qpos = jnp.arange(nb)[:, None] * blk + jnp.arange(blk)[None, :]
    kpos = (jnp.arange(nb)[:, None] - 1) * blk + jnp.arange(3 * blk)[None, :]
    off = kpos[:, None, :] - qpos[:, :, None]
    valid = (jnp.abs(off) <= n_side) & (kpos[:, None, :] >= 0) & (kpos[:, None, :] < length)
    s = jnp.einsum('nbqhe,nbkhe->nbhqk', qb, kb) * (e ** -0.5)
    s = jnp.where(valid[None, :, None], s, NEG_INF)
    m = jnp.max(s, axis=-1, keepdims=True)
    p = jnp.exp(s - m)
    den = jnp.sum(p, axis=-1, keepdims=True)
    o = jnp.einsum('nbhqk,nbkhe->nbqhe', p / den, vb).reshape(n, nb * blk, h, vb.shape[-1])[:, :length]
    lse = (m + jnp.log(den))[..., 0]
    lse = lse.transpose(0, 1, 3, 2).reshape(n, nb * blk, h)[:, :length]
    return o, lse


def _dilated_attention(q, k, v):
    b, s, h, e = q.shape
    outs, lses = [], []
    for window, dil in DILATED_PATTERNS:
        n_side = window // (2 * dil)
        length = s // dil

        def to_sub(t):
            return t.reshape(b, length, dil, h, t.shape[-1]).transpose(0, 2, 1, 3, 4).reshape(b * dil, length, h, t.shape[-1])

        o, lse = _window_attn(to_sub(q), to_sub(k), to_sub(v), n_side)
        outs.append(o.reshape(b, dil, length, h, e).transpose(0, 2, 1, 3, 4).reshape(b, s, h, e))
        lses.append(lse.reshape(b, dil, length, h).transpose(0, 2, 1, 3).reshape(b, s, h))
    w = jax.nn.softmax(jnp.stack(lses, axis=0), axis=0)
    return jnp.einsum('gbsh,gbshe->bshe', w, jnp.stack(outs, axis=0))


def _mla_attention(q_nope, q_rope, k_nope, k_rope, v):
    b, s, h, _ = q_nope.shape
    scale = (MLA_NOPE_DIM + MLA_ROPE_DIM) ** -0.5
    nq = s // Q_BLOCK
    kn, kr, vf = k_nope.astype(jnp.float32), k_rope.astype(jnp.float32), v.astype(jnp.float32)

    def blocks(t):
        return t.reshape((b, nq, Q_BLOCK) + t.shape[2:]).swapaxes(0, 1)

    def one_block(args):
        qn, qr = args
        sc = (jnp.einsum('bqhe,bkhe->bhqk', qn.astype(jnp.float32), kn)
              + jnp.einsum('bqhr,bkr->bhqk', qr.astype(jnp.float32), kr)) * scale
        p = jax.nn.softmax(sc, axis=-1)
        return jnp.einsum('bhqk,bkhe->bqhe', p, vf)

    o = lax.map(one_block, (blocks(q_nope), blocks(q_rope)))
    return o.swapaxes(0, 1).reshape(b, s, h, v.shape[-1])


def _memory_attention(q, k, v):
    sc = jnp.einsum('bshe,bmhe->bhsm', q.astype(jnp.float32), k.astype(jnp.float32)) * (q.shape[-1] ** -0.5)
    p = jax.nn.softmax(sc, axis=-1)
    return jnp.einsum('bhsm,bmhe->bshe', p, v.astype(jnp.float32))


def _hybrid_layer(h, pos, mem, w_in, g_cq, g_ckv, w_uq, w_ukv, w_mem_kv,
                  g_out_a, g_out_b, g_out_m, w_out, g_post, b_post):
    b, s, _ = h.shape
    dt = h.dtype
    idx = [int(i) for i in np.cumsum(IN_SPLITS)[:-1]]
    proj = h @ w_in
    a_q, a_k, a_v, a_g, c_q, c_kv, b_kr, b_g, m_q, m_g = jnp.split(proj, idx, axis=-1)

    hd = (b, s, A_HEADS, A_HEAD_DIM)
    y_a = _dilated_attention(_partial_rope(a_q.reshape(hd), pos),
                             _partial_rope(a_k.reshape(hd), pos),
                             a_v.reshape(hd)).reshape(b, s, A_WIDTH)

    q = (_rms_norm(c_q, g_cq, dt) @ w_uq).reshape(b, s, MLA_HEADS, MLA_NOPE_DIM + MLA_ROPE_DIM)
    q_nope, q_rope = q[..., :MLA_NOPE_DIM], _rope(q[..., MLA_NOPE_DIM:], pos)
    kv = (_rms_norm(c_kv, g_ckv, dt) @ w_ukv).reshape(b, s, MLA_HEADS, MLA_NOPE_DIM + MLA_V_DIM)
    k_nope, v = kv[..., :MLA_NOPE_DIM], kv[..., MLA_NOPE_DIM:]
    k_rope = _rope(b_kr[:, :, None, :], pos)[:, :, 0]
    y_b = _mla_attention(q_nope, q_rope, k_nope, k_rope, v).reshape(b, s, MLA_WIDTH)

    mkv = mem @ w_mem_kv
    mk = mkv[..., :MEM_WIDTH].reshape(b, -1, MEM_HEADS, MEM_HEAD_DIM)
    mv = mkv[..., MEM_WIDTH:].reshape(b, -1, MEM_HEADS, MEM_HEAD_DIM)
    y_m = _memory_attention(m_q.reshape(b, s, MEM_HEADS, MEM_HEAD_DIM), mk, mv).reshape(b, s, MEM_WIDTH)

    y = jnp.concatenate([_rms_norm(y_a, g_out_a, dt) * jax.nn.silu(a_g),
                         _rms_norm(y_b, g_out_b, dt) * jax.nn.silu(b_g),
                         _rms_norm(y_m, g_out_m, dt) * jax.nn.silu(m_g)], axis=-1)
    sub = y @ w_out
    return _layer_norm(DEEPNORM_ALPHA * h + sub, g_post, b_post)


def setup_inputs(seed: int = 0) -> dict:
    key = jax.random.key(seed)
    ks = jax.random.split(key, 20)
    f32 = jnp.float32

    def nrm(k, shape, fan_in, scale=1.0):
        return jax.random.normal(k, shape, f32) * (fan_in ** -0.5) * scale

    def gain(k, shape):
        return 1.0 + 0.02 * jax.random.normal(k, shape, f32)

    x = jax.random.normal(ks[0], (BATCH, SEQ, D_MODEL), f32)
    mem = jax.random.normal(ks[1], (BATCH, N_MEM, D_MODEL), f32)
    offsets = jax.random.randint(ks[2], (BATCH, 1), 0, 4096, dtype=jnp.int32)
    positions = offsets + jnp.arange(SEQ, dtype=jnp.int32)[None, :]
    return {
        "x": x,
        "mem": mem,
        "positions": positions,
        "g_emb": gain(ks[3], (D_MODEL,)),
        "b_emb": 0.02 * jax.random.normal(ks[4], (D_MODEL,), f32),
        "w_in": nrm(ks[5], (DEPTH, D_MODEL, D_IN), D_MODEL),
        "g_cq": gain(ks[6], (DEPTH, MLA_Q_RANK)),
        "g_ckv": gain(ks[7], (DEPTH, MLA_KV_RANK)),
        "w_uq": nrm(ks[8], (DEPTH, MLA_Q_RANK, MLA_HEADS * (MLA_NOPE_DIM + MLA_ROPE_DIM)), MLA_Q_RANK),
        "w_ukv": nrm(ks[9], (DEPTH, MLA_KV_RANK, MLA_HEADS * (MLA_NOPE_DIM + MLA_V_DIM)), MLA_KV_RANK),
        "w_mem_kv": nrm(ks[10], (DEPTH, D_MODEL, 2 * MEM_WIDTH), D_MODEL),
        "g_out_a": gain(ks[11], (DEPTH, A_WIDTH)),
        "g_out_b": gain(ks[12], (DEPTH, MLA_WIDTH)),
        "g_out_m": gain(ks[13], (DEPTH, MEM_WIDTH)),
        "w_out": nrm(ks[14], (DEPTH, D_MIX, D_MODEL), D_MIX, DEEPNORM_BETA),
        "g_post": gain(ks[15], (DEPTH, D_MODEL)),
        "b_post": 0.02 * jax.random.normal(ks[16], (DEPTH, D_MODEL), f32),
    }


def reference(x, mem, positions, g_emb, b_emb, w_in, g_cq, g_ckv, w_uq, w_ukv, w_mem_kv,
              g_out_a, g_out_b, g_out_m, w_out, g_post, b_post):
    h = _layer_norm(x, g_emb, b_emb)
    for l in range(DEPTH):
        h = _hybrid_layer(h, positions, mem, w_in[l], g_cq[l], g_ckv[l], w_uq[l], w_ukv[l], w_mem_kv[l],
                          g_out_a[l], g_out_b[l], g_out_m[l], w_out[l], g_post[l], b_post[l])
    return h
```

```python
import numpy as np
from contextlib import ExitStack

import concourse.bass as bass
import concourse.mybir as mybir
from concourse.bass_utils import run_bass_kernel_spmd

F32 = mybir.dt.float32
BF16 = mybir.dt.bfloat16
I32 = mybir.dt.int32
ALU = mybir.AluOpType
AF = mybir.ActivationFunctionType

N_CORES = 8
NB = 2
S = 2048
D = 1024
D_IN = 6048
NMEM = 256
EPS = 1e-5
ALPHA = 2.0 ** 0.25
THETA = 500000.0
TWO_PI = float(2 * np.pi)
C1 = 6.28125
C2 = float(2 * np.pi - 6.28125)
MASK_W = 1408
MASK_X0 = 640

ENGS = ("pe", "act", "dve", "pool", "sp")


class Op:
    __slots__ = ("eng", "fn", "deps", "signal", "sigval", "is_dma", "dsem", "dval", "dprev",
                 "epoch", "esem")

    def __init__(self, eng, fn, is_dma):
        self.eng = eng
        self.fn = fn
        self.deps = {}
        self.signal = False
        self.sigval = 0
        self.is_dma = is_dma
        self.dsem = None
        self.dval = 0
        self.dprev = 0
        self.epoch = 0
        self.esem = None


class Prog:
    N_EPOCH_SEMS = 6
    N_DMA_SEMS = 28
    N_SW_SEMS = 10

    def __init__(self, nc):
        self.nc = nc
        self.items = {e: [] for e in ENGS}
        self.last_writer = {}
        self.readers = {}
        self.epoch = 0
        self.barriers = []
        self.dma_count = 0
        self.sw_count = 0
        self.dma_sem_counts = [0] * self.N_DMA_SEMS
        self.epoch_ops = {e: [] for e in ENGS}

    def add(self, eng, fn, reads=(), writes=(), dma=False):
        op = Op(eng, fn, dma)
        op.epoch = self.epoch
        for r in reads:
            w = self.last_writer.get(r)
            if w is not None:
                op.deps[w] = True
            self.readers.setdefault(r, []).append(op)
        for r in writes:
            w = self.last_writer.get(r)
            if w is not None and w is not op:
                op.deps.setdefault(w, False)
            for rd in self.readers.get(r, ()):
                if rd is not op:
                    op.deps.setdefault(rd, False)
            self.last_writer[r] = op
            self.readers[r] = []
        if dma:
            if eng == "pool":
                k = self.N_DMA_SEMS - self.N_SW_SEMS + (self.sw_count % self.N_SW_SEMS)
                self.sw_count += 1
            else:
                k = self.dma_count % (self.N_DMA_SEMS - self.N_SW_SEMS)
                self.dma_count += 1
            op.dsem = k
            op.dprev = 16 * self.dma_sem_counts[k]
            self.dma_sem_counts[k] += 1
            op.dval = 16 * self.dma_sem_counts[k]
        self.items[eng].append(op)
        self.epoch_ops[eng].append(op)
        return op

    def barrier(self):
        lasts = {}
        for e in ENGS:
            ops = [o for o in self.epoch_ops[e] if not o.is_dma]
            lasts[e] = ops[-1] if ops else None
            if lasts[e] is not None:
                lasts[e].signal = True
        self.barriers.append((lasts, list(self.dma_sem_counts)))
        for e in ENGS:
            self.items[e].append(("barrier", len(self.barriers) - 1))
            self.epoch_ops[e] = []
        self.last_writer = {}
        self.readers = {}
        self.epoch += 1

    def emit(self, esems, dsems):
        nc = self.nc
        for e in ENGS:
            for it in self.items[e]:
                if isinstance(it, tuple):
                    continue
                for d, raw in it.deps.items():
                    if d.is_dma or d.epoch != it.epoch:
                        continue
                    if d.eng == it.eng and not it.is_dma:
                        if it.eng == "pe" or not raw:
                            continue
                    d.signal = True
        cum = {e: [0] * self.N_EPOCH_SEMS for e in ENGS}
        for e in ENGS:
            for it in self.items[e]:
                if isinstance(it, tuple) or it.is_dma:
                    continue
                k = it.epoch % self.N_EPOCH_SEMS
                it.esem = esems[e][k]
                if it.signal:
                    cum[e][k] += 1
                    it.sigval = cum[e][k]
        stats = {e: [0, 0] for e in ENGS}

        def run_engine(e, eng):
            waited = {}

            def wait(sem, val):
                if val <= 0:
                    return
                key = id(sem)
                if waited.get(key, 0) >= val:
                    return
                eng.wait_ge(sem, val)
                waited[key] = val
                stats[e][1] += 1

            for it in self.items[e]:
                if isinstance(it, tuple):
                    lasts, dcounts = self.barriers[it[1]]
                    for e2 in ENGS:
                        lo = lasts[e2]
                        if lo is not None and e2 != e:
                            wait(lo.esem, lo.sigval)
                    for k, c in enumerate(dcounts):
                        wait(dsems[k], 16 * c)
                    continue
                op = it
                for d, raw in op.deps.items():
                    if d.epoch != op.epoch:
                        continue
                    if d.is_dma:
                        wait(dsems[d.dsem], d.dval)
                        continue
                    if d.eng == op.eng and not op.is_dma:
                        if op.eng == "pe" or not raw:
                            continue
                    wait(d.esem, d.sigval)
                if op.is_dma:
                    wait(dsems[op.dsem], op.dprev)
                ins = op.fn(eng)
                stats[e][0] += 1
                if op.is_dma:
                    ins.then_inc(dsems[op.dsem], 16)
                elif op.signal:
                    ins.then_inc(op.esem, 1)
            if e == "sp":
                for k, c in enumerate(self.dma_sem_counts):
                    wait(dsems[k], 16 * c)

        with nc.Block() as block:
            @block.tensor
            def _(eng):
                run_engine("pe", eng)

            @block.scalar
            def _(eng):
                run_engine("act", eng)

            @block.vector
            def _(eng):
                run_engine("dve", eng)

            @block.gpsimd
            def _(eng):
                run_engine("pool", eng)

            @block.sync
            def _(eng):
                run_engine("sp", eng)
        return stats


class Builder:
    SB_BASE = 16512
    SB_LIMIT = 229344

    def __init__(self, nc):
        self.nc = nc
        self.P = Prog(nc)
        self.cur = self.SB_BASE
        self.uid = 0
        self.cache = {}

    def T(self, name, shape, dt):
        n = 1
        for s in shape[1:]:
            n *= s
        nbytes = n * mybir.dt.size(dt)
        nbytes = (nbytes + 31) // 32 * 32
        assert self.cur + nbytes <= self.SB_LIMIT, (name, self.cur, nbytes)
        key = (self.cur, tuple(shape), str(dt))
        t = self.cache.get(key)
        if t is None:
            self.uid += 1
            t = self.nc.alloc_sbuf_tensor_at(f"{name}_{self.uid}", shape, dt, offset=self.cur)
            self.cache[key] = t
        self.cur += nbytes
        return t

    def mark(self):
        return self.cur

    def release(self, m):
        self.cur = m

    def mm(self, out, lhsT, rhs, start, stop, reads, writes):
        self.P.add("pe", lambda e: e.matmul(out, lhsT=lhsT, rhs=rhs, start=start, stop=stop), reads, writes)

    def tr(self, out, in_, ident, reads, writes):
        self.P.add("pe", lambda e: e.transpose(out=out, in_=in_, identity=ident), reads, writes)

    def act(self, out, in_, func, reads, writes, scale=None, bias=None):
        kw = {}
        if scale is not None:
            kw["scale"] = scale
        if bias is not None:
            kw["bias"] = bias
        self.P.add("act", lambda e: e.activation(out=out, in_=in_, func=func, **kw), reads, writes)

    def tt(self, eng, out, in0, in1, op, reads, writes):
        self.P.add(eng, lambda e: e.tensor_tensor(out=out, in0=in0, in1=in1, op=op), reads, writes)

    def ts(self, eng, out, in0, s1, s2, op0, op1, reads, writes):
        if s2 is None:
            self.P.add(eng, lambda e: e.tensor_scalar(out=out, in0=in0, scalar1=s1, scalar2=None, op0=op0), reads, writes)
        else:
            self.P.add(eng, lambda e: e.tensor_scalar(out=out, in0=in0, scalar1=s1, scalar2=s2, op0=op0, op1=op1), reads, writes)

    def stt(self, out, in0, scalar, in1, op0, op1, reads, writes):
        self.P.add("dve", lambda e: e.scalar_tensor_tensor(out=out, in0=in0, scalar=scalar, in1=in1, op0=op0, op1=op1),
                   reads, writes)

    def cp(self, eng, out, in_, reads, writes):
        if eng == "act":
            self.P.add("act", lambda e: e.activation(out=out, in_=in_, func=AF.Copy), reads, writes)
        else:
            self.P.add(eng, lambda e: e.tensor_copy(out=out, in_=in_), reads, writes)

    def memset(self, eng, ap, val, writes):
        self.P.add(eng, lambda e: e.memset(ap, val), (), writes)

    def dma(self, q, out, in_, reads, writes, slow=False):
        if slow:
            self.P.add(q, lambda e: e.dma_start(out=out, in_=in_, allow_slow_non_contiguous=True), reads, writes, dma=True)
        else:
            self.P.add(q, lambda e: e.dma_start(out=out, in_=in_), reads, writes, dma=True)


def blk(i, n=512):
    return slice(i * n, (i + 1) * n)


def build_program():
    nc = bass.Bass("TRN2", target_bir_lowering=False)

    def din(name, shape, dt=F32):
        return nc.dram_tensor(name, shape, dt, kind="ExternalInput").ap()

    x = din("x", [NB, S, D])
    mem = din("mem", [NB, NMEM, D])
    pos = din("positions", [NB, S], I32)
    g_emb = din("g_emb", [D])
    b_emb = din("b_emb", [D])
    w_in = din("w_in", [D, D_IN])
    g_cq = din("g_cq", [256])
    g_ckv = din("g_ckv", [128])
    w_uq = din("w_uq", [256, 768])
    w_ukv = din("w_ukv", [128, 1024])
    w_mem_kv = din("w_mem_kv", [D, 1024])
    g_out_a = din("g_out_a", [1024])
    g_out_b = din("g_out_b", [512])
    g_out_m = din("g_out_m", [512])
    w_out = din("w_out", [2048, D])
    g_post = din("g_post", [D])
    b_post = din("b_post", [D])
    cst = din("cst", [128, 8])
    maskd = din("maskd", [128, MASK_W + 128])
    out = nc.dram_tensor("out", [NB, S, D], F32, kind="ExternalOutput").ap()
    hscr = nc.dram_tensor("hscr", [NB, S, D], F32, kind="Internal").ap()

    w_in_v = w_in.rearrange("(k p) c -> p k c", p=128)

    with ExitStack() as st:
        esems = {e: [st.enter_context(nc.semaphore(f"s_{e}_{i}")) for i in range(Prog.N_EPOCH_SEMS)]
                 for e in ENGS}
        dsems = [st.enter_context(nc.semaphore(f"d_{i}")) for i in range(Prog.N_DMA_SEMS)]
        psT = st.enter_context(nc.psum_tensor("psT", [128, 1024], BF16))
        ssqP = st.enter_context(nc.psum_tensor("ssqP", [128, 512], F32))
        W = [st.enter_context(nc.psum_tensor(f"W{i}", [128, 512], F32)) for i in range(6)]
        WN = [f"W{i}" for i in range(6)]

        B = Builder(nc)
        P = B.P

        hT = B.T("hT", [128, 8, S], BF16)
        yT = B.T("yT", [128, 16, S], BF16)
        ident = B.T("ident", [128, 128], BF16)
        ones_bf = B.T("ones_bf", [128, 128], BF16)
        onesf = B.T("onesf", [128, 128], F32)
        sel_o = B.T("sel_o", [128, 128], F32)
        mhalf = B.T("mhalf", [128, 4], F32)
        sel_ob = B.T("sel_ob", [128, 128], BF16)
        cst_t = B.T("cst", [128, 8], F32)
        gcq_t = B.T("gcq", [128, 2], F32)
        gckv_t = B.T("gckv", [128, 1], F32)
        goa_t = B.T("goa", [128, 8], F32)
        gob_t = B.T("gob", [128, 4], F32)
        gom_t = B.T("gom", [128, 4], F32)
        wst = [B.T(f"wst{i}", [128, 8, 512], BF16) for i in range(2)]

        B.memset("pool", ident[:], 0.0, ["ident"])
        P.add("pool", lambda e: e.affine_select(out=ident[:], in_=ident[:], compare_op=ALU.not_equal, fill=1.0,
                                                base=0, pattern=[[-1, 128]], channel_multiplier=1),
              ["ident"], ["ident"])
        B.memset("pool", ones_bf[:], 1.0, ["ones_bf"])
        B.memset("pool", onesf[:], 1.0, ["onesf"])
        B.memset("pool", sel_o[:, 0:64], 0.0, ["sel_o"])
        B.memset("pool", sel_o[:, 64:128], 1.0, ["sel_o"])
        B.memset("pool", mhalf[:], -0.5, ["mhalf"])
        B.memset("pool", sel_ob[:, 0:64], 0.0, ["sel_ob"])
        B.memset("pool", sel_ob[:, 64:128], 1.0, ["sel_ob"])
        B.dma("sp", cst_t[:], cst, [], ["cst"])
        B.dma("sp", gcq_t[:], g_cq.rearrange("(c p) -> p c", p=128), [], ["gvec"], slow=True)
        B.dma("sp", gckv_t[:], g_ckv.rearrange("(c p) -> p c", p=128), [], ["gvec"], slow=True)
        B.dma("sp", goa_t[:], g_out_a.rearrange("(c p) -> p c", p=128), [], ["gvec"], slow=True)
        B.dma("sp", gob_t[:], g_out_b.rearrange("(c p) -> p c", p=128), [], ["gvec"], slow=True)
        B.dma("sp", gom_t[:], g_out_m.rearrange("(c p) -> p c", p=128), [], ["gvec"], slow=True)
        P.barrier()

        base_mark = B.mark()

        def pipeline(T, stages):
            maxlag = max(l for l, _ in stages)
            stages = sorted(stages, key=lambda lf: -lf[0])
            for tau in range(T + maxlag):
                for lag, fn in stages:
                    t = tau - lag
                    if 0 <= t < T:
                        fn(t)

        def ln_alloc(nsm, nbig):
            return dict(st=[B.T("lst", [128, 2, 6], F32) for _ in range(nsm)],
                        mv=[B.T("lmv", [128, 2], F32) for _ in range(nsm)],
                        rs=[B.T("lrs", [128, 1], F32) for _ in range(nsm)],
                        nmr=[B.T("lnm", [128, 1], F32) for _ in range(nsm)],
                        xn=[B.T("lxn", [128, 1024], F32) for _ in range(nbig)],
                        t1=[B.T("lt1", [128, 1024], F32) for _ in range(nbig + 1)])

        def ln_stages(L, src_fn, gbc, gres, bbc, bres, lag0):
            nsm, nxn, nt1 = len(L["st"]), len(L["xn"]), len(L["t1"])

            def sB(t):
                k = t % nsm
                src, sres = src_fn(t)
                for hh in range(2):
                    P.add("dve", lambda e, hh=hh: e.bn_stats(out=L["st"][k][:, hh, :], in_=src[:, hh * 512:(hh + 1) * 512]),
                          [sres], [f"lst{k}"])
                P.add("dve", lambda e: e.bn_aggr(out=L["mv"][k][:], in_=L["st"][k][:].rearrange("p a b -> p (a b)")),
                      [f"lst{k}"], [f"lmv{k}"])
                B.ts("dve", L["rs"][k][:], L["mv"][k][:, 1:2], EPS, None, ALU.add, None, [f"lmv{k}"], [f"lrs{k}"])

            def sC(t):
                k = t % nsm
                B.tt("pool", L["rs"][k][:], L["rs"][k][:], mhalf[:, 0:1], ALU.pow, [f"lrs{k}", "mhalf"], [f"lrs{k}"])

            def sD(t):
                k = t % nsm
                B.stt(L["nmr"][k][:], L["mv"][k][:, 0:1], -1.0, L["rs"][k][:], ALU.mult, ALU.mult,
                      [f"lmv{k}", f"lrs{k}"], [f"lnm{k}"])

            def sE(t):
                k = t % nsm
                src, sres = src_fn(t)
                B.act(L["xn"][t % nxn][:], src, AF.Identity, [sres, f"lnm{k}", f"lrs{k}"], [f"lxn{t % nxn}"],
                      scale=L["rs"][k][:, 0:1], bias=L["nmr"][k][:, 0:1])

            def sF(t):
                B.tt("dve", L["t1"][t % nt1][:], L["xn"][t % nxn][:], gbc[:], ALU.mult, [f"lxn{t % nxn}", gres], [f"lt1_{t % nt1}"])

            def sG(t):
                B.tt("pool", L["t1"][t % nt1][:], L["t1"][t % nt1][:], bbc[:], ALU.add, [f"lt1_{t % nt1}", bres], [f"lt1_{t % nt1}"])

            return [(lag0, sB), (lag0, sC), (lag0 + 1, sD), (lag0 + 1, sE), (lag0 + 2, sF), (lag0 + 2, sG)]

        wcount = [0]

        def next_w():
            i = wcount[0] % 5
            wcount[0] += 1
            return W[i], WN[i]

        def proj_fm(lhs_fn, K, rhs_fn, nrows, reads):
            bank, bn = next_w()
            for k in range(K):
                B.mm(bank[0:nrows, :], lhs_fn(k), rhs_fn(k), k == 0, k == K - 1, reads, [bn])
            return bank, bn

        def rope_tables(b, fr_col, sgn_col, cosT, sinT, tag):
            m = B.mark()
            posb = B.T("posb", [128, S], I32)
            ang = B.T("ang", [128, S], F32)
            ki = B.T("ki", [128, S], I32)
            kf = B.T("kf", [128, S], F32)
            a = B.T("a", [128, S], F32)
            B.dma("sp", posb[:], pos[b].partition_broadcast(128), [], ["posb"])
            B.cp("dve", ang[:], posb[:], ["posb"], ["ang"])
            B.ts("dve", ang[:], ang[:], cst_t[:, fr_col:fr_col + 1], None, ALU.mult, None, ["ang", "cst"], ["ang"])
            for which in range(2):
                if which == 1:
                    B.ts("dve", ang[:], ang[:], float(np.pi / 2), None, ALU.add, None, ["ang"], ["ang"])
                B.ts("dve", ki[:], ang[:], float(1.0 / TWO_PI), None, ALU.mult, None, ["ang"], ["ki"])
                B.cp("dve", kf[:], ki[:], ["ki"], ["kf"])
                B.stt(a[:], kf[:], -C1, ang[:], ALU.mult, ALU.add, ["kf", "ang"], ["a"])
                B.stt(a[:], kf[:], -C2, a[:], ALU.mult, ALU.add, ["kf", "a"], ["a"])
                B.ts("dve", a[:], a[:], float(-np.pi), float(np.pi), ALU.max, ALU.min, ["a"], ["a"])
                if which == 0:
                    B.act(sinT[:], a[:], AF.Sin, ["a", "cst"], [f"sin{tag}"], scale=cst_t[:, sgn_col:sgn_col + 1])
                else:
                    B.act(cosT[:], a[:], AF.Sin, ["a"], [f"cos{tag}"])
            P.barrier()
            B.release(m)

        def attn_bufs():
            d = {}
            d["PT"] = [B.T("PT", [128, 512], BF16) for _ in range(4)]
            d["RD"] = [B.T("RD", [128, 512], F32) for _ in range(1)]
            d["BCS"] = [B.T("BCS", [128, 512], F32) for _ in range(2)]
            d["ON"] = [B.T("ON", [128, 512], F32) for _ in range(2)]
            for t in d["RD"]:
                B.memset("pool", t[:], 1.0, ["RD0"])
            d["RH"] = [B.T("RH", [128, 512], BF16)]
            d["RL"] = [B.T("RL", [128, 512], BF16)]
            d["step"] = 0
            d["defer"] = []
            d["bj"] = 0
            d["epi"] = 0
            return d

        def attention(ab, KT, QT, V, vcols, nkt, r0, r1, den, scale, maskfn, sg, sgres, gvec, ychunk, sqT, reads, o3=None, bulk=None):
            SB = (0, 1)
            OB = (2, 3)
            steps = []
            for qb in range(4):
                kts = [kt for kt in range(nkt) if maskfn is None or maskfn(kt, qb) is not None]
                for j, kt in enumerate(kts):
                    steps.append((qb, kt, j == 0, j == len(kts) - 1))
            base = ab["step"]

            def emit_S(i):
                qb, kt, _, _ = steps[i]
                s = SB[(base + i) % 2]
                if maskfn is None:
                    B.mm(W[s][:, :], KT(kt), QT(qb), True, True, reads, [WN[s]])
                else:
                    B.mm(W[s][:, :], KT(kt), QT(qb), True, False, reads, [WN[s]])
                    B.mm(W[s][:, :], ident[:, :], maskfn(kt, qb), False, True, ["ident", "maskT"], [WN[s]])

            def epilogue(qb, ob, cur):
                run_deferred(1 << 60)
                j = ab["epi"] % 2
                ab["epi"] += 1
                rd, bcs, on = ab["RD"][0], ab["BCS"][j], ab["ON"][j]
                if den[0] == "row":
                    dr = den[1]
                    den_ap = W[ob][dr:dr + 1, :]
                    den_res = WN[ob]
                else:
                    dr = 0
                    den_ap = W[4][0:1, :]
                    den_res = WN[4]
                if o3 is None:
                    P.add("dve", lambda e: e.reciprocal(out=rd[dr:dr + 1, :], in_=den_ap), [den_res], ["RD0"])
                else:
                    B.tt("dve", rd[dr:dr + 1, :], den_ap, o3[0][dr:dr + 1, blk(qb)], ALU.add, [den_res, o3[1]], ["RD0"])
                    P.add("dve", lambda e: e.reciprocal(out=rd[dr:dr + 1, :], in_=rd[dr:dr + 1, :]), ["RD0"], ["RD0"])

                def st1():
                    if r0 == 0 and r1 == 64:
                        B.mm(W[5][0:64, :], onesf[dr:dr + 1, 0:64], rd[dr:dr + 1, :], True, True, ["RD0", "onesf"], [WN[5]])
                    elif r0 == 64:
                        B.mm(W[5][0:128, :], sel_o[dr:dr + 1, 0:128], rd[dr:dr + 1, :], True, True, ["RD0", "sel_o"], [WN[5]])
                    else:
                        B.mm(W[5][0:128, :], onesf[dr:dr + 1, 0:128], rd[dr:dr + 1, :], True, True, ["RD0", "onesf"], [WN[5]])

                def st2():
                    B.cp("act", bcs[r0:r1, :], W[5][r0:r1, :], [WN[5]], [f"BCS{j}"])

                def st3():
                    if o3 is None:
                        B.tt("dve", on[r0:r1, :], W[ob][r0:r1, :], bcs[r0:r1, :], ALU.mult, [WN[ob], f"BCS{j}"], [f"ON{j}"])
                    else:
                        B.tt("dve", on[r0:r1, :], W[ob][r0:r1, :], o3[0][r0:r1, blk(qb)], ALU.add, [WN[ob], o3[1]], [f"ON{j}"])
                        B.tt("dve", on[r0:r1, :], on[r0:r1, :], bcs[r0:r1, :], ALU.mult, [f"ON{j}", f"BCS{j}"], [f"ON{j}"])

                def st4():
                    B.act(sqT[r0:r1, blk(qb)], on[r0:r1, :], AF.Square, [f"ON{j}"], ["sqT"])
                    B.stt(yT[r0:r1, ychunk, blk(qb)], on[r0:r1, :], gvec, sg[r0:r1, blk(qb)], ALU.mult, ALU.mult,
                          [f"ON{j}", "gvec", sgres], [f"yT{ychunk}"])

                for dly, fn in ((6, st1), (8, st2), (9, st3), (10, st4)):
                    ab["defer"].append((cur + dly, fn))

            def run_deferred(upto):
                q = ab["defer"]
                while q and q[0][0] <= upto:
                    q.pop(0)[1]()

            def emit_rest(i):
                qb, kt, first, last = steps[i]
                s = SB[(base + i) % 2]
                pi = (base + i) % 4
                pt = ab["PT"][pi]
                ob = OB[(ab["epi"]) % 2]
                B.act(pt[:], W[s][:, :], AF.Exp, [WN[s]], [f"PT{pi}"], scale=scale)
                B.mm(W[ob][0:vcols, :], V(kt), pt[:], first, last, [f"PT{pi}"] + reads, [WN[ob]])
                if den[0] == "sep":
                    B.mm(W[4][0:1, :], ones_bf[:, 0:1], pt[:], first, last, [f"PT{pi}", "ones_bf"], [WN[4]])
                if last:
                    if bulk is None:
                        epilogue(qb, ob, base + i)
                    elif den[0] == "sep":
                        B.cp("dve", bulk[0][:, blk(qb)], W[ob][:, :], [WN[ob]], [bulk[1]])
                        B.cp("act", bulk[2][0:1, blk(qb)], W[4][0:1, :], [WN[4]], [bulk[3]])
                        ab["epi"] += 1
                    else:
                        brows = slice(0, 128) if r0 == 64 else slice(0, 65)
                        B.tt("dve", bulk[0][brows, blk(qb)], W[ob][brows, :], bulk[0][brows, blk(qb)], ALU.add,
                             [WN[ob], bulk[1]], [bulk[1]])
                        ab["epi"] += 1

            n = len(steps)
            emit_S(0)
            for i in range(n):
                if i + 1 < n:
                    emit_S(i + 1)
                emit_rest(i)
                run_deferred(base + i)
            if bulk is None:
                run_deferred(1 << 60)
            ab["step"] = base + n

        def run_deferred_ab(ab, upto, maxn=1 << 30):
            q = ab["defer"]
            n = 0
            while q and q[0][0] <= upto and n < maxn:
                q.pop(0)[1]()
                n += 1

        def bulk_epilogue(ab, acc, accres, r0, r1, dr, sg, sgres, gvec, ychunk, sqT, start, dtile=None, dres=None):
            items = []
            if dtile is None:
                dtile, dres = acc, accres
            for qb in range(4):
                t0 = start + 6 * qb
                j = ab["bj"] % 2
                ab["bj"] += 1
                bcs, on = ab["BCS"][j], ab["ON"][j]

                def f_rec(qb=qb):
                    P.add("dve", lambda e: e.reciprocal(out=dtile[dr:dr + 1, blk(qb)], in_=dtile[dr:dr + 1, blk(qb)]),
                          [dres], [dres])

                rh, rl = ab["RH"][0], ab["RL"][0]

                def f_hl(qb=qb, rh=rh, rl=rl):
                    B.cp("pool", rh[dr:dr + 1, :], dtile[dr:dr + 1, blk(qb)], [dres], ["RH0"])
                    B.tt("pool", rl[dr:dr + 1, :], dtile[dr:dr + 1, blk(qb)], rh[dr:dr + 1, :], ALU.subtract,
                         [dres, "RH0"], ["RL0"])

                def f_bc(qb=qb, rh=rh, rl=rl):
                    if r0 == 0 and r1 == 64:
                        o_ap, l_ap, lres = W[5][0:64, :], ones_bf[dr:dr + 1, 0:64], "ones_bf"
                    elif r0 == 64:
                        o_ap, l_ap, lres = W[5][0:128, :], sel_ob[dr:dr + 1, 0:128], "sel_ob"
                    else:
                        o_ap, l_ap, lres = W[5][0:128, :], ones_bf[dr:dr + 1, 0:128], "ones_bf"
                    B.mm(o_ap, l_ap, rh[dr:dr + 1, :], True, False, ["RH0", lres], [WN[5]])
                    B.mm(o_ap, l_ap, rl[dr:dr + 1, :], False, True, ["RL0", lres], [WN[5]])

                def f_cp(bcs=bcs, j=j):
                    B.cp("act", bcs[r0:r1, :], W[5][r0:r1, :], [WN[5]], [f"BCS{j}"])

                def f_mul(qb=qb, bcs=bcs, on=on, j=j):
                    B.tt("dve", on[r0:r1, :], acc[r0:r1, blk(qb)], bcs[r0:r1, :], ALU.mult, [accres, f"BCS{j}"], [f"ON{j}"])

                def f_fin(qb=qb, on=on, j=j):
                    B.act(sqT[r0:r1, blk(qb)], on[r0:r1, :], AF.Square, [f"ON{j}"], ["sqT"])
                    B.stt(yT[r0:r1, ychunk, blk(qb)], on[r0:r1, :], gvec, sg[r0:r1, blk(qb)], ALU.mult, ALU.mult,
                          [f"ON{j}", "gvec", sgres], [f"yT{ychunk}"])

                items += [(t0, f_rec), (t0 + 5, f_hl), (t0 + 7, f_bc), (t0 + 9, f_cp), (t0 + 10, f_mul), (t0 + 11, f_fin)]
            ab["defer"] = sorted(ab["defer"] + items, key=lambda x: x[0])

        def attention_p3(ab, KTx, QTc_, V3, odd, mask3, O3s, reads):
            SB = (0, 1)
            OB = (2, 3)
            rows = slice(0, 128) if odd else slice(0, 65)
            vsl = slice(64, 192) if odd else slice(0, 128)
            base = ab["step"]

            def csl(r):
                return slice(r, r + 16 * 127 + 1, 16)

            def emit_S(r):
                s = SB[(base + r) % 2]
                B.mm(W[s][:, 0:128], KTx[:, csl(r)], QTc_[:, csl(r)], True, False, reads, [WN[s]])
                B.mm(W[s][:, 0:128], ident[:, :], mask3, False, True, ["ident", "maskT"], [WN[s]])

            def emit_rest(r):
                s = SB[(base + r) % 2]
                pi = (base + r) % 4
                pt = ab["PT"][pi]
                ob = OB[ab["epi"] % 2]
                B.act(pt[:, 0:128], W[s][:, 0:128], AF.Exp, [WN[s]], [f"PT{pi}"], scale=0.125)
                q = r % 4
                B.mm(W[ob][0:128, q * 128:(q + 1) * 128], V3[:, r, vsl], pt[:, 0:128], True, True, [f"PT{pi}", "V3"], [WN[ob]])
                if q == 3:
                    b3 = r // 4
                    dst = O3s[0][:].rearrange("p (j r) -> p r j", r=16)[rows, 4 * b3:4 * b3 + 4, :]
                    src = W[ob][rows, :].rearrange("p (q j) -> p q j", q=4)
                    B.cp("dve", dst, src, [WN[ob]], [O3s[1]])
                    ab["epi"] += 1

            emit_S(0)
            for r in range(16):
                if r + 1 < 16:
                    emit_S(r + 1)
                emit_rest(r)
                run_deferred_ab(ab, base + r)
            ab["step"] = base + 16

        def ssq_mms(sqT, chunk):
            for tt_ in range(16):
                col = tt_ * 16 + chunk
                B.mm(ssqP[:, col:col + 1], sqT[:, tt_ * 128:(tt_ + 1) * 128], ones_bf[:, 0:1], True, True,
                     ["sqT", "ones_bf"], ["ssqP"])

        def load_w_in(buf, col0, ncols, dst0=0):
            B.dma("pool", wst[buf][:, :, dst0:dst0 + ncols], w_in_v[:, :, col0:col0 + ncols], [], [f"wst{buf}"])

        for b in range(NB):
            m = B.mark()
            gbc = B.T("gbc", [128, 1024], F32)
            bbc = B.T("bbc", [128, 1024], F32)
            xt = [B.T("xt", [128, 1024], F32) for _ in range(4)]
            hb = [B.T("hb", [128, 1024], BF16) for _ in range(3)]
            ha = [B.T("ha", [128, 1024], F32) for _ in range(3)]
            L = ln_alloc(4, 2)
            B.dma("sp", gbc[:], g_emb.partition_broadcast(128), [], ["gbc"])
            B.dma("sp", bbc[:], b_emb.partition_broadcast(128), [], ["bbc"])

            def tsl_(t):
                return slice(t * 128, (t + 1) * 128)

            def p1_load(t):
                B.dma("sp", xt[t % 4][:], x[b, tsl_(t), :], [], [f"xt{t % 4}"])

            def p1_H(t):
                hfr = f"lt1_{t % 3}"
                hf_ = L["t1"][t % 3]
                B.act(hb[t % 3][:], hf_[:], AF.Copy, [hfr], [f"hb{t % 3}"])
                B.act(ha[t % 3][:], hf_[:], AF.Identity, [hfr], [f"ha{t % 3}"], scale=ALPHA)
                B.dma("sp", hscr[b, tsl_(t), :], ha[t % 3][:], [f"ha{t % 3}"], [f"hscr{t % 3}"])
                for k in range(8):
                    B.tr(psT[:, k * 128:(k + 1) * 128], hb[t % 3][:, k * 128:(k + 1) * 128], ident[:],
                         [f"hb{t % 3}", "ident"], ["psT"])

            def p1_J(t):
                B.cp("act", hT[:, :, tsl_(t)], psT[:].rearrange("p (k t) -> p k t", k=8), ["psT"], ["hT"])

            pipeline(16, [(0, p1_load)] + ln_stages(L, lambda t: (xt[t % 4][:], f"xt{t % 4}"), gbc, "gbc", bbc, "bbc", 1)
                     + [(4, p1_H), (5, p1_J)])
            P.barrier()
            B.release(m)

            m = B.mark()
            memb = B.T("memb", [128, 2, 1024], BF16)
            memT = B.T("memT", [128, 8, NMEM], BF16)
            mkT = B.T("mkT", [128, 4, NMEM], BF16)
            mv = B.T("mv", [128, 2, 512], BF16)
            qTm = [B.T("qTm", [128, S], BF16) for _ in range(2)]
            sgm = [B.T("sgm", [128, S], BF16) for _ in range(2)]
            accm = [B.T("accm", [128, S], F32) for _ in range(2)]
            denm = [B.T("denm", [128, S], F32) for _ in range(2)]
            sqT = B.T("sqT", [128, S], BF16)
            ab = attn_bufs()
            wmk = w_mem_kv.rearrange("(k p) c -> p k c", p=128)
            B.dma("pool", memb[:], mem[b].rearrange("(t p) d -> p t d", p=128), [], ["memb"])
            B.dma("pool", wst[0][:], wmk[:, :, 0:512], [], ["wst0"])
            B.dma("pool", wst[1][:], wmk[:, :, 512:1024], [], ["wst1"])
            for t in range(2):
                for k in range(8):
                    B.tr(psT[:, k * 128:(k + 1) * 128], memb[:, t, k * 128:(k + 1) * 128], ident[:], ["memb", "ident"], ["psT"])
                B.cp("act", memT[:, :, t * 128:(t + 1) * 128], psT[:].rearrange("p (k t) -> p k t", k=8), ["psT"], ["memT"])
            for h in range(4):
                bank, bn = next_w()
                for k in range(8):
                    B.mm(bank[:, 0:NMEM], wst[0][:, k, h * 128:(h + 1) * 128], memT[:, k, :], k == 0, k == 7,
                         ["wst0", "memT"], [bn])
                B.cp("dve", mkT[:, h, :], bank[:, 0:NMEM], [bn], ["mkT"])
            for t in range(2):
                bank, bn = next_w()
                for k in range(8):
                    B.mm(bank[:, :], memT[:, k, t * 128:(t + 1) * 128], wst[1][:, k, :], k == 0, k == 7, ["wst1", "memT"], [bn])
                B.cp("dve", mv[:, t, :], bank[:, :], [bn], ["mv"])
            load_w_in(0, 5024, 512)
            load_w_in(1, 5536, 512)
            for h in range(4):
                sl = h % 2
                for qb in range(4):
                    bank, bn = proj_fm(lambda k: wst[0][:, k, h * 128:(h + 1) * 128], 8, lambda k: hT[:, k, blk(qb)], 128,
                                       ["wst0", "hT"])
                    B.cp("dve", qTm[sl][:, blk(qb)], bank[:, :], [bn], [f"qTm{sl}"])
                    run_deferred_ab(ab, 1 << 60, maxn=3)
                    bank, bn = proj_fm(lambda k: wst[1][:, k, h * 128:(h + 1) * 128], 8, lambda k: hT[:, k, blk(qb)], 128,
                                       ["wst1", "hT"])
                    B.act(sgm[sl][:, blk(qb)], bank[:, :], AF.Silu, [bn], [f"sgm{sl}"])
                    run_deferred_ab(ab, 1 << 60, maxn=3)
                run_deferred_ab(ab, 1 << 60)
                if h > 0:
                    ssq_mms(sqT, 12 + h - 1)
                attention(ab, KT=lambda kt: mkT[:, h, kt * 128:(kt + 1) * 128], QT=lambda qb: qTm[sl][:, blk(qb)],
                          V=lambda kt: mv[:, kt, h * 128:(h + 1) * 128], vcols=128, nkt=2, r0=0, r1=128, den=("sep",),
                          scale=128.0 ** -0.5, maskfn=None, sg=sgm[sl], sgres=f"sgm{sl}", gvec=gom_t[:, h:h + 1],
                          ychunk=12 + h, sqT=sqT, reads=["mkT", "mv", f"qTm{sl}"],
                          bulk=(accm[sl], f"accm{sl}", denm[sl], f"denm{sl}"))
                bulk_epilogue(ab, accm[sl], f"accm{sl}", 0, 128, 0, sgm[sl], f"sgm{sl}", gom_t[:, h:h + 1], 12 + h, sqT,
                              ab["step"], dtile=denm[sl], dres=f"denm{sl}")
            run_deferred_ab(ab, 1 << 60)
            ssq_mms(sqT, 15)
            P.barrier()
            B.release(m)

            m = B.mark()
            cosB = B.T("cosB", [128, S], F32)
            sinB = B.T("sinB", [128, S], F32)
            rope_tables(b, 2, 3, cosB, sinB, "B")
            wuq = B.T("wuq", [128, 2, 768], BF16)
            wuq_sw = B.T("wuq_sw", [128, 2, 768], BF16)
            wukv = B.T("wukv", [128, 1024], BF16)
            wkr_sw = B.T("wkr_sw", [128, 8, 96], BF16)
            cqn = B.T("cqn", [128, 2, S], BF16)
            ckvn = B.T("ckvn", [128, S], BF16)
            kr = B.T("kr", [128, S], BF16)
            V2B = B.T("V2B", [128, 16, 192], BF16)
            sqt = [B.T("sqt", [128, 512], BF16) for _ in range(2)]
            R = [B.T("R", [128, 512], F32) for _ in range(2)]
            rt1 = B.T("rt1", [128, 512], F32)
            rt2 = B.T("rt2", [128, 512], F32)
            QTh = B.T("QTh", [128, S], BF16)
            KTh = B.T("KTh", [128, S], BF16)
            sgb = B.T("sgb", [128, S], BF16)
            sqT = B.T("sqT", [128, S], BF16)
            ab = attn_bufs()
            B.dma("pool", wuq[:], w_uq.rearrange("(k p) c -> p k c", p=128), [], ["wuq"])
            B.dma("pool", wukv[:], w_ukv, [], ["wukv"])
            load_w_in(0, 4096, 512)
            load_w_in(1, 4512, 512)
            B.memset("pool", wuq_sw[:], 0.0, ["wuq_sw"])
            wuq4 = wuq[:].rearrange("p k (h t) -> p k h t", t=96)
            wsw4 = wuq_sw[:].rearrange("p k (h t) -> p k h t", t=96)
            for kc in range(2):
                B.cp("pool", wsw4[:, kc, :, 64:80], wuq4[:, kc, :, 80:96], ["wuq", "wuq_sw"], ["wuq_sw"])
                B.cp("pool", wsw4[:, kc, :, 80:96], wuq4[:, kc, :, 64:80], ["wuq", "wuq_sw"], ["wuq_sw"])
            B.memset("pool", wkr_sw[:], 0.0, ["wkr_sw"])
            B.cp("pool", wkr_sw[:, :, 64:80], wst[0][:, :, 400:416], ["wst0", "wkr_sw"], ["wkr_sw"])
            B.cp("pool", wkr_sw[:, :, 80:96], wst[0][:, :, 384:400], ["wst0", "wkr_sw"], ["wkr_sw"])
            B.memset("pool", V2B[:, :, 64:65], 1.0, ["V2B"])
            B.memset("pool", V2B[:, :, 65:128], 0.0, ["V2B"])
            rc = 0
            for qb in range(4):
                hrhs = lambda k: hT[:, k, blk(qb)]
                cb = []
                for c in range(2):
                    cb.append(proj_fm(lambda k: wst[0][:, k, c * 128:(c + 1) * 128], 8, hrhs, 128, ["wst0", "hT"]))
                for c in range(2):
                    B.act(sqt[c][:], cb[c][0][:, :], AF.Square, [cb[c][1]], [f"sqt{c}"])
                sbank, sbn = next_w()
                for c in range(2):
                    B.mm(sbank[:, :], ones_bf[:, :], sqt[c][:], c == 0, c == 1, [f"sqt{c}", "ones_bf"], [sbn])
                j = rc % 2
                rc += 1
                B.ts("dve", R[j][:], sbank[:, :], 1.0 / 256, EPS, ALU.mult, ALU.add, [sbn], [f"R{j}"])
                B.act(R[j][:], R[j][:], AF.Sqrt, [f"R{j}"], [f"R{j}"])
                P.add("dve", lambda e, j=j: e.reciprocal(out=R[j][:], in_=R[j][:]), [f"R{j}"], [f"R{j}"])
                for c in range(2):
                    B.stt(cqn[:, c, blk(qb)], cb[c][0][:, :], gcq_t[:, c:c + 1], R[j][:], ALU.mult, ALU.mult,
                          [cb[c][1], "gvec", f"R{j}"], ["cqn"])
                kb, kbn = proj_fm(lambda k: wst[0][:, k, 256:384], 8, hrhs, 128, ["wst0", "hT"])
                B.act(sqt[0][:], kb[:, :], AF.Square, [kbn], ["sqt0"])
                sbank, sbn = next_w()
                B.mm(sbank[:, :], ones_bf[:, :], sqt[0][:], True, True, ["sqt0", "ones_bf"], [sbn])
                j = rc % 2
                rc += 1
                B.ts("dve", R[j][:], sbank[:, :], 1.0 / 128, EPS, ALU.mult, ALU.add, [sbn], [f"R{j}"])
                B.act(R[j][:], R[j][:], AF.Sqrt, [f"R{j}"], [f"R{j}"])
                P.add("dve", lambda e, j=j: e.reciprocal(out=R[j][:], in_=R[j][:]), [f"R{j}"], [f"R{j}"])
                B.stt(ckvn[:, blk(qb)], kb[:, :], gckv_t[:, 0:1], R[j][:], ALU.mult, ALU.mult, [kbn, "gvec", f"R{j}"], ["ckvn"])
                pa, pan = proj_fm(lambda k: wst[0][:, k, 320:416], 8, hrhs, 96, ["wst0", "hT"])
                pb, pbn = proj_fm(lambda k: wkr_sw[:, k, 0:96], 8, hrhs, 96, ["wkr_sw", "hT"])
                B.tt("dve", rt1[64:96, :], pa[64:96, :], cosB[64:96, blk(qb)], ALU.mult, [pan, "cosB"], ["rt1"])
                B.tt("dve", rt2[64:96, :], pb[64:96, :], sinB[64:96, blk(qb)], ALU.mult, [pbn, "sinB"], ["rt2"])
                B.tt("pool", kr[64:96, blk(qb)], rt1[64:96, :], rt2[64:96, :], ALU.add, ["rt1", "rt2"], ["kr"])
            wv3 = wukv[:].rearrange("p (h t) -> p h t", t=128)
            for h in range(8):
                pr = h // 2
                odd = h % 2
                if not odd:
                    for qb in range(4):
                        bank, bn = proj_fm(lambda k: wst[1][:, k, pr * 128:(pr + 1) * 128], 8, lambda k: hT[:, k, blk(qb)], 128,
                                           ["wst1", "hT"])
                        B.act(sgb[:, blk(qb)], bank[:, :], AF.Silu, [bn], ["sgb"])
                    for t4 in range(4):
                        bank, bn = next_w()
                        for tq in range(4):
                            tt_ = t4 * 4 + tq
                            B.mm(bank[:, tq * 128:(tq + 1) * 128], ckvn[:, tt_ * 128:(tt_ + 1) * 128],
                                 wv3[:, 2 * pr:2 * pr + 2, 64:128], True, True, ["ckvn", "wukv"], [bn])
                        bv = bank[:, :].rearrange("p (q e t) -> p q e t", q=4, e=2)
                        B.cp("act", V2B[:, t4 * 4:(t4 + 1) * 4, 0:64], bv[:, :, 0, :], [bn, "V2B"], ["V2B"])
                        B.cp("dve", V2B[:, t4 * 4:(t4 + 1) * 4, 128:192], bv[:, :, 1, :], [bn, "V2B"], ["V2B"])
                for qb in range(4):
                    pa, pan = proj_fm(lambda k: wuq[:, k, h * 96:(h + 1) * 96], 2, lambda k: cqn[:, k, blk(qb)], 96, ["wuq", "cqn"])
                    pb, pbn = proj_fm(lambda k: wuq_sw[:, k, h * 96:(h + 1) * 96], 2, lambda k: cqn[:, k, blk(qb)], 96,
                                      ["wuq_sw", "cqn"])
                    B.cp("act", QTh[0:64, blk(qb)], pa[0:64, :], [pan], ["QTh"])
                    B.tt("dve", rt1[64:96, :], pa[64:96, :], cosB[64:96, blk(qb)], ALU.mult, [pan, "cosB"], ["rt1"])
                    B.tt("dve", rt2[64:96, :], pb[64:96, :], sinB[64:96, blk(qb)], ALU.mult, [pbn, "sinB"], ["rt2"])
                    B.tt("pool", QTh[64:96, blk(qb)], rt1[64:96, :], rt2[64:96, :], ALU.add, ["rt1", "rt2", "QTh"], ["QTh"])
                    pc, pcn = proj_fm(lambda k: wukv[:, h * 128:h * 128 + 64], 1, lambda k: ckvn[:, blk(qb)], 64, ["wukv", "ckvn"])
                    B.cp("act", KTh[0:64, blk(qb)], pc[0:64, :], [pcn], ["KTh"])
                B.cp("pool", KTh[64:96, :], kr[64:96, :], ["kr", "KTh"], ["KTh"])
                if not odd:
                    V = lambda kt: V2B[:, kt, 0:128]
                    vcols, r0, r1, den = 128, 0, 64, ("row", 64)
                else:
                    V = lambda kt: V2B[:, kt, 64:192]
                    vcols, r0, r1, den = 128, 64, 128, ("row", 0)
                attention(ab, KT=lambda kt: KTh[0:96, kt * 128:(kt + 1) * 128], QT=lambda qb: QTh[0:96, blk(qb)],
                          V=V, vcols=vcols, nkt=16, r0=r0, r1=r1, den=den, scale=96.0 ** -0.5, maskfn=None,
                          sg=sgb, sgres="sgb", gvec=gob_t[r0:r1, pr:pr + 1], ychunk=8 + pr, sqT=sqT,
                          reads=["V2B", "QTh", "KTh"])
                if odd:
                    ssq_mms(sqT, 8 + pr)
            P.barrier()
            B.release(m)

            m = B.mark()
            cosA = B.T("cosA", [128, S], F32)
            sinA = B.T("sinA", [128, S], F32)
            rope_tables(b, 0, 1, cosA, sinA, "A")
            maskT = B.T("maskT", [128, MASK_W + 128], BF16)
            V3 = B.T("V3", [128, 16, 192], BF16)
            O3 = [B.T("O3s", [128, S], F32) for _ in range(2)]
            wswq = B.T("wswq", [128, 8, 128], BF16)
            wswk = B.T("wswk", [128, 8, 128], BF16)
            QTc = B.T("QTc", [128, S], BF16)
            KTc = B.T("KTc", [128, S], BF16)
            KTo = B.T("KTo", [128, S], BF16)
            V2A = B.T("V2A", [128, 16, 192], BF16)
            sga = [B.T("sga", [128, S], BF16) for _ in range(2)]
            rt1 = B.T("rt1", [128, 512], F32)
            rt2 = B.T("rt2", [128, 512], F32)
            sqT = B.T("sqT", [128, S], BF16)
            ab = dict(PT=[B.T("PT", [128, 512], BF16) for _ in range(4)],
                      BCS=[B.T("BCS", [128, 512], F32) for _ in range(2)],
                      ON=[B.T("ON", [128, 512], F32) for _ in range(2)],
                      RH=[B.T("RH", [128, 512], BF16)], RL=[B.T("RL", [128, 512], BF16)],
                      step=0, epi=0, defer=[], bj=0)
            B.dma("pool", maskT[:], maskd, [], ["maskT"])
            B.memset("pool", KTc[64:128, :], 0.0, ["KTc"])
            B.memset("pool", KTo[0:64, :], 0.0, ["KTc"])
            B.memset("pool", wswq[:], 0.0, ["wswq"])
            B.memset("pool", wswk[:], 0.0, ["wswk"])
            B.memset("pool", V2A[:, :, 64:65], 1.0, ["V2A"])
            B.memset("pool", V2A[:, :, 65:128], 0.0, ["V2A"])
            B.memset("pool", V3[:, :, 64:65], 1.0, ["V3"])
            B.memset("pool", V3[:, :, 65:128], 0.0, ["V3"])

            def maskfn(kt, qb):
                dmin = 128 * kt - 512 * qb - 511
                dmax = 128 * kt + 127 - 512 * qb
                if dmin > 256 or dmax < -256:
                    return None
                off = MASK_X0 - 128 * kt + 512 * qb
                assert 0 <= off and off + 512 <= MASK_W
                return maskT[:, off:off + 512]

            for c in range(8):
                sl = c % 2
                wb = wst[sl]
                sgc = sga[sl]
                sgn = f"sga{sl}"
                for i in range(4):
                    load_w_in(sl, 1024 * i + c * 128, 128, dst0=128 * i)
                wq4 = wb[:, :, 0:128].rearrange("p k (e t) -> p k e t", e=2)
                wk4 = wb[:, :, 128:256].rearrange("p k (e t) -> p k e t", e=2)
                sq4 = wswq[:].rearrange("p k (e t) -> p k e t", e=2)
                sk4 = wswk[:].rearrange("p k (e t) -> p k e t", e=2)
                B.cp("pool", sq4[:, :, :, 0:8], wq4[:, :, :, 8:16], [f"wst{sl}", "wswq"], ["wswq"])
                B.cp("pool", sq4[:, :, :, 8:16], wq4[:, :, :, 0:8], [f"wst{sl}", "wswq"], ["wswq"])
                B.cp("pool", sk4[:, :, :, 0:8], wk4[:, :, :, 8:16], [f"wst{sl}", "wswk"], ["wswk"])
                B.cp("pool", sk4[:, :, :, 8:16], wk4[:, :, :, 0:8], [f"wst{sl}", "wswk"], ["wswk"])
                for qb in range(4):
                    hrhs = lambda k: hT[:, k, blk(qb)]
                    for (dst, dname, c0, wsw, wswn) in ((QTc, "QTc", 0, wswq, "wswq"), (KTc, "KTc", 128, wswk, "wswk")):
                        pa, pan = proj_fm(lambda k: wb[:, k, c0:c0 + 128], 8, hrhs, 128, [f"wst{sl}", "hT"])
                        pb, pbn = proj_fm(lambda k: wsw[:, k, :], 8, hrhs, 128, [wswn, "hT"])
                        B.tt("dve", rt1[:], pa[:, :], cosA[:, blk(qb)], ALU.mult, [pan, "cosA"], ["rt1"])
                        B.tt("dve", rt2[:], pb[:, :], sinA[:, blk(qb)], ALU.mult, [pbn, "sinA"], ["rt2"])
                        if dname == "QTc":
                            B.tt("pool", dst[:, blk(qb)], rt1[:], rt2[:], ALU.add, ["rt1", "rt2"], [dname])
                        else:
                            B.tt("pool", KTc[0:64, blk(qb)], rt1[0:64, :], rt2[0:64, :], ALU.add, ["rt1", "rt2"], [dname])
                            B.tt("pool", KTo[64:128, blk(qb)], rt1[64:128, :], rt2[64:128, :], ALU.add, ["rt1", "rt2"], [dname])
                        run_deferred_ab(ab, 1 << 60, maxn=3)
                    bank, bn = proj_fm(lambda k: wb[:, k, 384:512], 8, hrhs, 128, [f"wst{sl}", "hT"])
                    B.act(sgc[:, blk(qb)], bank[:, :], AF.Silu, [bn], [sgn])
                for t4 in range(4):
                    bank, bn = next_w()
                    for tq in range(4):
                        tt_ = t4 * 4 + tq
                        for k in range(8):
                            B.mm(bank[:, tq * 128:(tq + 1) * 128], hT[:, k, tt_ * 128:(tt_ + 1) * 128], wb[:, k, 256:384],
                                 k == 0, k == 7, [f"wst{sl}", "hT"], [bn])
                    bv = bank[:, :].rearrange("p (q e t) -> p q e t", q=4, e=2)
                    B.cp("act", V2A[:, t4 * 4:(t4 + 1) * 4, 0:64], bv[:, :, 0, :], [bn, "V2A"], ["V2A"])
                    B.cp("dve", V2A[:, t4 * 4:(t4 + 1) * 4, 128:192], bv[:, :, 1, :], [bn, "V2A"], ["V2A"])
                for r4 in range(4):
                    bank, bn = next_w()
                    for tq in range(4):
                        r = r4 * 4 + tq
                        for k in range(8):
                            B.mm(bank[:, tq * 128:(tq + 1) * 128], hT[:, k, r:r + 16 * 127 + 1:16], wb[:, k, 256:384],
                                 k == 0, k == 7, [f"wst{sl}", "hT"], [bn])
                    bv = bank[:, :].rearrange("p (q e t) -> p q e t", q=4, e=2)
                    B.cp("act", V3[:, r4 * 4:(r4 + 1) * 4, 0:64], bv[:, :, 0, :], [bn, "V3"], ["V3"])
                    B.cp("dve", V3[:, r4 * 4:(r4 + 1) * 4, 128:192], bv[:, :, 1, :], [bn, "V3"], ["V3"])
                run_deferred_ab(ab, 1 << 60)
                if c > 0:
                    ssq_mms(sqT, c - 1)
                for odd in range(2):
                    if not odd:
                        V = lambda kt: V2A[:, kt, 0:128]
                        vcols, r0, r1, den = 128, 0, 64, ("row", 64)
                    else:
                        V = lambda kt: V2A[:, kt, 64:192]
                        vcols, r0, r1, den = 128, 64, 128, ("row", 0)
                    KTx = KTo if odd else KTc
                    acc, accres = O3[odd], f"O3s{odd}"
                    attention_p3(ab, KTx, QTc, V3, odd, maskT[:, MASK_W:MASK_W + 128], (acc, accres), ["QTc", "KTc"])
                    attention(ab, KT=lambda kt: KTx[:, kt * 128:(kt + 1) * 128], QT=lambda qb: QTc[:, blk(qb)],
                              V=V, vcols=vcols, nkt=16, r0=r0, r1=r1, den=den, scale=0.125, maskfn=maskfn,
                              sg=sgc, sgres=sgn, gvec=goa_t[r0:r1, c:c + 1], ychunk=c, sqT=sqT,
                              reads=["V2A", "QTc", "KTc"], bulk=(acc, accres))
                    bulk_epilogue(ab, acc, accres, r0, r1, den[1], sgc, sgn, goa_t[r0:r1, c:c + 1], c, sqT,
                                  ab["step"] + (17 if not odd else 0))
            run_deferred_ab(ab, 1 << 60)
            ssq_mms(sqT, 7)
            P.barrier()
            B.release(m)

            m = B.mark()
            wo = B.T("wo", [128, 16, D], BF16)
            gpc = B.T("gpc", [128, 1024], F32)
            bpc = B.T("bpc", [128, 1024], F32)
            hf = [B.T("hf", [128, 1024], F32) for _ in range(3)]
            acc = [B.T("acc", [128, 1024], F32) for _ in range(3)]
            rg = [B.T("rg", [128, 4], F32) for _ in range(3)]
            L = ln_alloc(4, 2)
            wo_v = w_out.rearrange("(k p) c -> p k c", p=128)
            for q4 in range(4):
                B.dma("pool", wo[:, q4 * 4:(q4 + 1) * 4, :], wo_v[:, q4 * 4:(q4 + 1) * 4, :], [], ["wo"])
            B.dma("sp", gpc[:], g_post.partition_broadcast(128), [], ["gpc"])
            B.dma("sp", bpc[:], b_post.partition_broadcast(128), [], ["bpc"])
            groups = ((0, 8, 1024.0), (8, 12, 512.0), (12, 16, 512.0))

            def tsl_(t):
                return slice(t * 128, (t + 1) * 128)

            def f_load(t):
                B.dma("sp", hf[t % 3][:], hscr[b, tsl_(t), :], [], [f"hf{t % 3}"])

            def f_mm(t):
                for n in range(2):
                    for gi, (c0, c1, width) in enumerate(groups):
                        bi = n * 3 + gi
                        for c in range(c0, c1):
                            B.mm(W[bi][:, :], yT[:, c, tsl_(t)], wo[:, c, blk(n)], c == c0, c == c1 - 1, ["yT", "wo"],
                                 [f"{WN[bi]}"])

            def f_rg(t):
                k = t % 3
                for gi, (c0, c1, width) in enumerate(groups):
                    P.add("dve", lambda e, gi=gi, c0=c0, c1=c1: e.reduce_sum(
                        out=rg[k][:, gi:gi + 1], in_=ssqP[:, t * 16 + c0:t * 16 + c1], axis=mybir.AxisListType.X),
                        ["ssqP"], [f"rg{k}"])
                    B.ts("dve", rg[k][:, gi:gi + 1], rg[k][:, gi:gi + 1], 1.0 / width, EPS, ALU.mult, ALU.add,
                         [f"rg{k}"], [f"rg{k}"])
                B.tt("pool", rg[k][:, 0:3], rg[k][:, 0:3], mhalf[:, 0:3], ALU.pow, [f"rg{k}", "mhalf"], [f"rg{k}"])

            def f_stt(t):
                k = t % 3
                for n in range(2):
                    for gi in range(3):
                        bi = n * 3 + gi
                        src = hf[k][:, blk(n)] if gi == 0 else acc[k][:, blk(n)]
                        srcn = f"hf{k}" if gi == 0 else f"acc{k}"
                        B.stt(acc[k][:, blk(n)], W[bi][:, :], rg[k][:, gi:gi + 1], src, ALU.mult, ALU.add,
                              [WN[bi], f"rg{k}", srcn], [f"acc{k}"])

            def f_store(t):
                B.dma("sp", out[b, tsl_(t), :], L["t1"][t % 3][:], [f"lt1_{t % 3}"], [f"out{t % 3}"])

            pipeline(16, [(0, f_load), (0, f_mm), (0, f_rg), (1, f_stt)]
                     + ln_stages(L, lambda t: (acc[t % 3][:], f"acc{t % 3}"), gpc, "gpc", bpc, "bpc", 2)
                     + [(5, f_store)])
            P.barrier()
            B.release(m)

        stats = P.emit(esems, dsems)
    return nc, stats


def _host_consts():
    cst = np.zeros((128, 8), np.float32)
    for p in range(128):
        j = p % 64
        if j < 16:
            cst[p, 0] = THETA ** (-(2.0 * (j % 8)) / 16.0)
            cst[p, 1] = -1.0 if j < 8 else 1.0
        else:
            cst[p, 0] = 0.0
            cst[p, 1] = 1.0
        if 64 <= p < 96:
            jj = p - 64
            cst[p, 2] = THETA ** (-(2.0 * (jj % 16)) / 32.0)
            cst[p, 3] = -1.0 if jj < 16 else 1.0
        else:
            cst[p, 2] = 0.0
            cst[p, 3] = 1.0
    kl = np.arange(128)[:, None]
    xx = np.arange(MASK_W)[None, :]
    d = kl - xx + MASK_X0
    f = ((np.abs(d) <= 64).astype(np.float32)
         + ((d % 4 == 0) & (np.abs(d) <= 256)).astype(np.float32))
    mb = np.where(f > 0, np.log(np.maximum(f, 1.0)) / 0.125, -320.0)
    d3 = np.arange(128)[:, None] - np.arange(128)[None, :]
    m3 = np.where(np.abs(d3) <= 64, 0.0, -320.0)
    return cst, np.concatenate([mb, m3], axis=1).astype(np.float32)


_CACHE = {}


def kernel(x, mem, positions, g_emb, b_emb, w_in, g_cq, g_ckv, w_uq, w_ukv, w_mem_kv,
           g_out_a, g_out_b, g_out_m, w_out, g_post, b_post):
    f32 = lambda a: np.ascontiguousarray(np.asarray(a), dtype=np.float32)
    if "nc" not in _CACHE:
        _CACHE["nc"] = build_program()[0]
    nc = _CACHE["nc"]
    cst, maskd = _host_consts()
    x = f32(x)
    mem = f32(mem)
    positions = np.ascontiguousarray(np.asarray(positions), dtype=np.int32)
    shared = dict(
        g_emb=f32(g_emb), b_emb=f32(b_emb), w_in=f32(w_in).reshape(D, D_IN), g_cq=f32(g_cq).reshape(256),
        g_ckv=f32(g_ckv).reshape(128), w_uq=f32(w_uq).reshape(256, 768), w_ukv=f32(w_ukv).reshape(128, 1024),
        w_mem_kv=f32(w_mem_kv).reshape(D, 1024), g_out_a=f32(g_out_a).reshape(1024), g_out_b=f32(g_out_b).reshape(512),
        g_out_m=f32(g_out_m).reshape(512), w_out=f32(w_out).reshape(2048, D), g_post=f32(g_post).reshape(D),
        b_post=f32(b_post).reshape(D), cst=cst, maskd=maskd)
    in_maps = []
    for c in range(N_CORES):
        d = dict(shared)
        d["x"] = x[c * NB:(c + 1) * NB]
        d["mem"] = mem[c * NB:(c + 1) * NB]
        d["positions"] = positions[c * NB:(c + 1) * NB]
        in_maps.append(d)
    res = run_bass_kernel_spmd(nc, in_maps, core_ids=list(range(N_CORES)))
    return np.concatenate([r["out"] for r in res.results], axis=0)
```

```python
import numpy as np
from contextlib import ExitStack

import concourse.bass as bass
import concourse.mybir as mybir
from concourse.bass_utils import run_bass_kernel_spmd

F32 = mybir.dt.float32
BF16 = mybir.dt.bfloat16
I32 = mybir.dt.int32
ALU = mybir.AluOpType
AF = mybir.ActivationFunctionType

N_CORES = 8
NB = 2
S = 2048
D = 1024
D_IN = 6048
NMEM = 256
EPS = 1e-5
ALPHA = 2.0 ** 0.25
THETA = 500000.0
TWO_PI = float(2 * np.pi)
C1 = 6.28125
C2 = float(2 * np.pi - 6.28125)
MASK_W = 1408
MASK_X0 = 640

ENGS = ("pe", "act", "dve", "pool", "sp")


class Op:
    __slots__ = ("eng", "fn", "deps", "signal", "sigval", "is_dma", "dsem", "dval", "dprev",
                 "epoch", "esem")

    def __init__(self, eng, fn, is_dma):
        self.eng = eng
        self.fn = fn
        self.deps = {}
        self.signal = False
        self.sigval = 0
        self.is_dma = is_dma
        self.dsem = None
        self.dval = 0
        self.dprev = 0
        self.epoch = 0
        self.esem = None


class Prog:
    N_EPOCH_SEMS = 6
    N_DMA_SEMS = 28
    N_SW_SEMS = 10

    def __init__(self, nc):
        self.nc = nc
        self.items = {e: [] for e in ENGS}
        self.last_writer = {}
        self.readers = {}
        self.epoch = 0
        self.barriers = []
        self.dma_count = 0
        self.sw_count = 0
        self.dma_sem_counts = [0] * self.N_DMA_SEMS
        self.epoch_ops = {e: [] for e in ENGS}

    def add(self, eng, fn, reads=(), writes=(), dma=False):
        op = Op(eng, fn, dma)
        op.epoch = self.epoch
        for r in reads:
            w = self.last_writer.get(r)
            if w is not None:
                op.deps[w] = True
            self.readers.setdefault(r, []).append(op)
        for r in writes:
            w = self.last_writer.get(r)
            if w is not None and w is not op:
                op.deps.setdefault(w, False)
            for rd in self.readers.get(r, ()):
                if rd is not op:
                    op.deps.setdefault(rd, False)
            self.last_writer[r] = op
            self.readers[r] = []
        if dma:
            if eng == "pool":
                k = self.N_DMA_SEMS - self.N_SW_SEMS + (self.sw_count % self.N_SW_SEMS)
                self.sw_count += 1
            else:
                k = self.dma_count % (self.N_DMA_SEMS - self.N_SW_SEMS)
                self.dma_count += 1
            op.dsem = k
            op.dprev = 16 * self.dma_sem_counts[k]
            self.dma_sem_counts[k] += 1
            op.dval = 16 * self.dma_sem_counts[k]
        self.items[eng].append(op)
        self.epoch_ops[eng].append(op)
        return op

    def barrier(self):
        lasts = {}
        for e in ENGS:
            ops = [o for o in self.epoch_ops[e] if not o.is_dma]
            lasts[e] = ops[-1] if ops else None
            if lasts[e] is not None:
                lasts[e].signal = True
        self.barriers.append((lasts, list(self.dma_sem_counts)))
        for e in ENGS:
            self.items[e].append(("barrier", len(self.barriers) - 1))
            self.epoch_ops[e] = []
        self.last_writer = {}
        self.readers = {}
        self.epoch += 1

    def emit(self, esems, dsems):
        nc = self.nc
        for e in ENGS:
            for it in self.items[e]:
                if isinstance(it, tuple):
                    continue
                for d, raw in it.deps.items():
                    if d.is_dma or d.epoch != it.epoch:
                        continue
                    if d.eng == it.eng and not it.is_dma:
                        if it.eng == "pe" or not raw:
                            continue
                    d.signal = True
        cum = {e: [0] * self.N_EPOCH_SEMS for e in ENGS}
        for e in ENGS:
            for it in self.items[e]:
                if isinstance(it, tuple) or it.is_dma:
                    continue
                k = it.epoch % self.N_EPOCH_SEMS
                it.esem = esems[e][k]
                if it.signal:
                    cum[e][k] += 1
                    it.sigval = cum[e][k]
        stats = {e: [0, 0] for e in ENGS}

        def run_engine(e, eng):
            waited = {}

            def wait(sem, val):
                if val <= 0:
                    return
                key = id(sem)
                if waited.get(key, 0) >= val:
                    return
                eng.wait_ge(sem, val)
                waited[key] = val
                stats[e][1] += 1

            for it in self.items[e]:
                if isinstance(it, tuple):
                    lasts, dcounts = self.barriers[it[1]]
                    for e2 in ENGS:
                        lo = lasts[e2]
                        if lo is not None and e2 != e:
                            wait(lo.esem, lo.sigval)
                    for k, c in enumerate(dcounts):
                        wait(dsems[k], 16 * c)
                    continue
                op = it
                for d, raw in op.deps.items():
                    if d.epoch != op.epoch:
                        continue
                    if d.is_dma:
                        wait(dsems[d.dsem], d.dval)
                        continue
                    if d.eng == op.eng and not op.is_dma:
                        if op.eng == "pe" or not raw:
                            continue
                    wait(d.esem, d.sigval)
                if op.is_dma:
                    wait(dsems[op.dsem], op.dprev)
                ins = op.fn(eng)
                stats[e][0] += 1
                if op.is_dma:
                    ins.then_inc(dsems[op.dsem], 16)
                elif op.signal:
                    ins.then_inc(op.esem, 1)
            if e == "sp":
                for k, c in enumerate(self.dma_sem_counts):
                    wait(dsems[k], 16 * c)

        with nc.Block() as block:
            @block.tensor
            def _(eng):
                run_engine("pe", eng)

            @block.scalar
            def _(eng):
                run_engine("act", eng)

            @block.vector
            def _(eng):
                run_engine("dve", eng)

            @block.gpsimd
            def _(eng):
                run_engine("pool", eng)

            @block.sync
            def _(eng):
                run_engine("sp", eng)
        return stats


class Builder:
    SB_BASE = 16512
    SB_LIMIT = 229344

    def __init__(self, nc):
        self.nc = nc
        self.P = Prog(nc)
        self.cur = self.SB_BASE
        self.uid = 0
        self.cache = {}

    def T(self, name, shape, dt):
        n = 1
        for s in shape[1:]:
            n *= s
        nbytes = n * mybir.dt.size(dt)
        nbytes = (nbytes + 31) // 32 * 32
        assert self.cur + nbytes <= self.SB_LIMIT, (name, self.cur, nbytes)
        key = (self.cur, tuple(shape), str(dt))
        t = self.cache.get(key)
        if t is None:
            self.uid += 1
            t = self.nc.alloc_sbuf_tensor_at(f"{name}_{self.uid}", shape, dt, offset=self.cur)
            self.cache[key] = t
        self.cur += nbytes
        return t

    def mark(self):
        return self.cur

    def release(self, m):
        self.cur = m

    def mm(self, out, lhsT, rhs, start, stop, reads, writes):
        self.P.add("pe", lambda e: e.matmul(out, lhsT=lhsT, rhs=rhs, start=start, stop=stop), reads, writes)

    def tr(self, out, in_, ident, reads, writes):
        self.P.add("pe", lambda e: e.transpose(out=out, in_=in_, identity=ident), reads, writes)

    def act(self, out, in_, func, reads, writes, scale=None, bias=None):
        kw = {}
        if scale is not None:
            kw["scale"] = scale
        if bias is not None:
            kw["bias"] = bias
        self.P.add("act", lambda e: e.activation(out=out, in_=in_, func=func, **kw), reads, writes)

    def tt(self, eng, out, in0, in1, op, reads, writes):
        self.P.add(eng, lambda e: e.tensor_tensor(out=out, in0=in0, in1=in1, op=op), reads, writes)

    def ts(self, eng, out, in0, s1, s2, op0, op1, reads, writes):
        if s2 is None:
            self.P.add(eng, lambda e: e.tensor_scalar(out=out, in0=in0, scalar1=s1, scalar2=None, op0=op0), reads, writes)
        else:
            self.P.add(eng, lambda e: e.tensor_scalar(out=out, in0=in0, scalar1=s1, scalar2=s2, op0=op0, op1=op1), reads, writes)

    def stt(self, out, in0, scalar, in1, op0, op1, reads, writes):
        self.P.add("dve", lambda e: e.scalar_tensor_tensor(out=out, in0=in0, scalar=scalar, in1=in1, op0=op0, op1=op1),
                   reads, writes)

    def cp(self, eng, out, in_, reads, writes):
        if eng == "act":
            self.P.add("act", lambda e: e.activation(out=out, in_=in_, func=AF.Copy), reads, writes)
        else:
            self.P.add(eng, lambda e: e.tensor_copy(out=out, in_=in_), reads, writes)

    def memset(self, eng, ap, val, writes):
        self.P.add(eng, lambda e: e.memset(ap, val), (), writes)

    def dma(self, q, out, in_, reads, writes, slow=False):
        if slow:
            self.P.add(q, lambda e: e.dma_start(out=out, in_=in_, allow_slow_non_contiguous=True), reads, writes, dma=True)
        else:
            self.P.add(q, lambda e: e.dma_start(out=out, in_=in_), reads, writes, dma=True)


def blk(i, n=512):
    return slice(i * n, (i + 1) * n)


def build_program():
    nc = bass.Bass("TRN2", target_bir_lowering=False)

    def din(name, shape, dt=F32):
        return nc.dram_tensor(name, shape, dt, kind="ExternalInput").ap()

    x = din("x", [NB, S, D])
    mem = din("mem", [NB, NMEM, D])
    pos = din("positions", [NB, S], I32)
    g_emb = din("g_emb", [D])
    b_emb = din("b_emb", [D])
    w_in = din("w_in", [D, D_IN])
    g_cq = din("g_cq", [256])
    g_ckv = din("g_ckv", [128])
    w_uq = din("w_uq", [256, 768])
    w_ukv = din("w_ukv", [128, 1024])
    w_mem_kv = din("w_mem_kv", [D, 1024])
    g_out_a = din("g_out_a", [1024])
    g_out_b = din("g_out_b", [512])
    g_out_m = din("g_out_m", [512])
    w_out = din("w_out", [2048, D])
    g_post = din("g_post", [D])
    b_post = din("b_post", [D])
    cst = din("cst", [128, 8])
    maskd = din("maskd", [128, MASK_W + 128])
    out = nc.dram_tensor("out", [NB, S, D], F32, kind="ExternalOutput").ap()
    hscr = nc.dram_tensor("hscr", [NB, S, D], F32, kind="Internal").ap()

    w_in_v = w_in.rearrange("(k p) c -> p k c", p=128)

    with ExitStack() as st:
        esems = {e: [st.enter_context(nc.semaphore(f"s_{e}_{i}")) for i in range(Prog.N_EPOCH_SEMS)]
                 for e in ENGS}
        dsems = [st.enter_context(nc.semaphore(f"d_{i}")) for i in range(Prog.N_DMA_SEMS)]
        psT = st.enter_context(nc.psum_tensor("psT", [128, 1024], BF16))
        ssqP = st.enter_context(nc.psum_tensor("ssqP", [128, 512], F32))
        W = [st.enter_context(nc.psum_tensor(f"W{i}", [128, 512], F32)) for i in range(6)]
        WN = [f"W{i}" for i in range(6)]

        B = Builder(nc)
        P = B.P

        hT = B.T("hT", [128, 8, S], BF16)
        yT = B.T("yT", [128, 16, S], BF16)
        ident = B.T("ident", [128, 128], BF16)
        ones_bf = B.T("ones_bf", [128, 128], BF16)
        onesf = B.T("onesf", [128, 128], F32)
        sel_o = B.T("sel_o", [128, 128], F32)
        mhalf = B.T("mhalf", [128, 512], F32)
        cst_t = B.T("cst", [128, 8], F32)
        gcq_t = B.T("gcq", [128, 2], F32)
        gckv_t = B.T("gckv", [128, 1], F32)
        goa_t = B.T("goa", [128, 8], F32)
        gob_t = B.T("gob", [128, 4], F32)
        gom_t = B.T("gom", [128, 4], F32)
        wst = [B.T(f"wst{i}", [128, 8, 512], BF16) for i in range(2)]

        B.memset("pool", ident[:], 0.0, ["ident"])
        P.add("pool", lambda e: e.affine_select(out=ident[:], in_=ident[:], compare_op=ALU.not_equal, fill=1.0,
                                                base=0, pattern=[[-1, 128]], channel_multiplier=1),
              ["ident"], ["ident"])
        B.memset("pool", ones_bf[:], 1.0, ["ones_bf"])
        B.memset("pool", onesf[:], 1.0, ["onesf"])
        B.memset("pool", sel_o[:, 0:64], 0.0, ["sel_o"])
        B.memset("pool", sel_o[:, 64:128], 1.0, ["sel_o"])
        B.memset("pool", mhalf[:], -0.5, ["mhalf"])
        B.dma("sp", cst_t[:], cst, [], ["cst"])
        B.dma("sp", gcq_t[:], g_cq.rearrange("(c p) -> p c", p=128), [], ["gvec"], slow=True)
        B.dma("sp", gckv_t[:], g_ckv.rearrange("(c p) -> p c", p=128), [], ["gvec"], slow=True)
        B.dma("sp", goa_t[:], g_out_a.rearrange("(c p) -> p c", p=128), [], ["gvec"], slow=True)
        B.dma("sp", gob_t[:], g_out_b.rearrange("(c p) -> p c", p=128), [], ["gvec"], slow=True)
        B.dma("sp", gom_t[:], g_out_m.rearrange("(c p) -> p c", p=128), [], ["gvec"], slow=True)
        P.barrier()

        base_mark = B.mark()

        def pipeline(T, stages):
            maxlag = max(l for l, _ in stages)
            stages = sorted(stages, key=lambda lf: -lf[0])
            for tau in range(T + maxlag):
                for lag, fn in stages:
                    t = tau - lag
                    if 0 <= t < T:
                        fn(t)

        def ln_alloc(nsm, nbig):
            return dict(st=[B.T("lst", [128, 2, 6], F32) for _ in range(nsm)],
                        mv=[B.T("lmv", [128, 2], F32) for _ in range(nsm)],
                        rs=[B.T("lrs", [128, 1], F32) for _ in range(nsm)],
                        nmr=[B.T("lnm", [128, 1], F32) for _ in range(nsm)],
                        xn=[B.T("lxn", [128, 1024], F32) for _ in range(nbig)],
                        t1=[B.T("lt1", [128, 1024], F32) for _ in range(nbig + 1)])

        def ln_stages(L, src_fn, gbc, gres, bbc, bres, lag0):
            nsm, nxn, nt1 = len(L["st"]), len(L["xn"]), len(L["t1"])

            def sB(t):
                k = t % nsm
                src, sres = src_fn(t)
                for hh in range(2):
                    P.add("dve", lambda e, hh=hh: e.bn_stats(out=L["st"][k][:, hh, :], in_=src[:, hh * 512:(hh + 1) * 512]),
                          [sres], [f"lst{k}"])
                P.add("dve", lambda e: e.bn_aggr(out=L["mv"][k][:], in_=L["st"][k][:].rearrange("p a b -> p (a b)")),
                      [f"lst{k}"], [f"lmv{k}"])
                B.ts("dve", L["rs"][k][:], L["mv"][k][:, 1:2], EPS, None, ALU.add, None, [f"lmv{k}"], [f"lrs{k}"])

            def sC(t):
                k = t % nsm
                B.tt("pool", L["rs"][k][:], L["rs"][k][:], mhalf[:, 0:1], ALU.pow, [f"lrs{k}", "mhalf"], [f"lrs{k}"])

            def sD(t):
                k = t % nsm
                B.stt(L["nmr"][k][:], L["mv"][k][:, 0:1], -1.0, L["rs"][k][:], ALU.mult, ALU.mult,
                      [f"lmv{k}", f"lrs{k}"], [f"lnm{k}"])

            def sE(t):
                k = t % nsm
                src, sres = src_fn(t)
                B.act(L["xn"][t % nxn][:], src, AF.Identity, [sres, f"lnm{k}", f"lrs{k}"], [f"lxn{t % nxn}"],
                      scale=L["rs"][k][:, 0:1], bias=L["nmr"][k][:, 0:1])

            def sF(t):
                B.tt("dve", L["t1"][t % nt1][:], L["xn"][t % nxn][:], gbc[:], ALU.mult, [f"lxn{t % nxn}", gres], [f"lt1_{t % nt1}"])

            def sG(t):
                B.tt("pool", L["t1"][t % nt1][:], L["t1"][t % nt1][:], bbc[:], ALU.add, [f"lt1_{t % nt1}", bres], [f"lt1_{t % nt1}"])

            return [(lag0, sB), (lag0, sC), (lag0 + 1, sD), (lag0 + 1, sE), (lag0 + 2, sF), (lag0 + 2, sG)]

        wcount = [0]

        def next_w():
            i = wcount[0] % 5
            wcount[0] += 1
            return W[i], WN[i]

        def proj_fm(lhs_fn, K, rhs_fn, nrows, reads):
            bank, bn = next_w()
            for k in range(K):
                B.mm(bank[0:nrows, :], lhs_fn(k), rhs_fn(k), k == 0, k == K - 1, reads, [bn])
            return bank, bn

        def rope_tables(b, fr_col, sgn_col, cosT, sinT, tag):
            m = B.mark()
            posb = B.T("posb", [128, S], I32)
            ang = B.T("ang", [128, S], F32)
            ki = B.T("ki", [128, S], I32)
            kf = B.T("kf", [128, S], F32)
            a = B.T("a", [128, S], F32)
            B.dma("sp", posb[:], pos[b].partition_broadcast(128), [], ["posb"])
            B.cp("dve", ang[:], posb[:], ["posb"], ["ang"])
            B.ts("dve", ang[:], ang[:], cst_t[:, fr_col:fr_col + 1], None, ALU.mult, None, ["ang", "cst"], ["ang"])
            for which in range(2):
                if which == 1:
                    B.ts("dve", ang[:], ang[:], float(np.pi / 2), None, ALU.add, None, ["ang"], ["ang"])
                B.ts("dve", ki[:], ang[:], float(1.0 / TWO_PI), None, ALU.mult, None, ["ang"], ["ki"])
                B.cp("dve", kf[:], ki[:], ["ki"], ["kf"])
                B.stt(a[:], kf[:], -C1, ang[:], ALU.mult, ALU.add, ["kf", "ang"], ["a"])
                B.stt(a[:], kf[:], -C2, a[:], ALU.mult, ALU.add, ["kf", "a"], ["a"])
                B.ts("dve", a[:], a[:], float(-np.pi), float(np.pi), ALU.max, ALU.min, ["a"], ["a"])
                if which == 0:
                    B.act(sinT[:], a[:], AF.Sin, ["a", "cst"], [f"sin{tag}"], scale=cst_t[:, sgn_col:sgn_col + 1])
                else:
                    B.act(cosT[:], a[:], AF.Sin, ["a"], [f"cos{tag}"])
            P.barrier()
            B.release(m)

        def attn_bufs():
            d = {}
            d["PT"] = [B.T("PT", [128, 512], BF16) for _ in range(4)]
            d["RD"] = [B.T("RD", [128, 512], F32) for _ in range(1)]
            d["BCS"] = [B.T("BCS", [128, 512], F32) for _ in range(2)]
            d["ON"] = [B.T("ON", [128, 512], F32) for _ in range(2)]
            for t in d["RD"]:
                B.memset("pool", t[:], 1.0, ["RD0"])
            d["step"] = 0
            d["defer"] = []
            d["bj"] = 0
            d["epi"] = 0
            return d

        def attention(ab, KT, QT, V, vcols, nkt, r0, r1, den, scale, maskfn, sg, sgres, gvec, ychunk, sqT, reads, o3=None, bulk=None):
            SB = (0, 1)
            OB = (2, 3)
            steps = []
            for qb in range(4):
                kts = [kt for kt in range(nkt) if maskfn is None or maskfn(kt, qb) is not None]
                for j, kt in enumerate(kts):
                    steps.append((qb, kt, j == 0, j == len(kts) - 1))
            base = ab["step"]

            def emit_S(i):
                qb, kt, _, _ = steps[i]
                s = SB[(base + i) % 2]
                if maskfn is None:
                    B.mm(W[s][:, :], KT(kt), QT(qb), True, True, reads, [WN[s]])
                else:
                    B.mm(W[s][:, :], KT(kt), QT(qb), True, False, reads, [WN[s]])
                    B.mm(W[s][:, :], ident[:, :], maskfn(kt, qb), False, True, ["ident", "maskT"], [WN[s]])

            def epilogue(qb, ob, cur):
                run_deferred(1 << 60)
                j = ab["epi"] % 2
                ab["epi"] += 1
                rd, bcs, on = ab["RD"][0], ab["BCS"][j], ab["ON"][j]
                if den[0] == "row":
                    dr = den[1]
                    den_ap = W[ob][dr:dr + 1, :]
                    den_res = WN[ob]
                else:
                    dr = 0
                    den_ap = W[4][0:1, :]
                    den_res = WN[4]
                if o3 is None:
                    P.add("dve", lambda e: e.reciprocal(out=rd[dr:dr + 1, :], in_=den_ap), [den_res], ["RD0"])
                else:
                    B.tt("dve", rd[dr:dr + 1, :], den_ap, o3[0][dr:dr + 1, blk(qb)], ALU.add, [den_res, o3[1]], ["RD0"])
                    P.add("dve", lambda e: e.reciprocal(out=rd[dr:dr + 1, :], in_=rd[dr:dr + 1, :]), ["RD0"], ["RD0"])

                def st1():
                    if r0 == 0 and r1 == 64:
                        B.mm(W[5][0:64, :], onesf[dr:dr + 1, 0:64], rd[dr:dr + 1, :], True, True, ["RD0", "onesf"], [WN[5]])
                    elif r0 == 64:
                        B.mm(W[5][0:128, :], sel_o[dr:dr + 1, 0:128], rd[dr:dr + 1, :], True, True, ["RD0", "sel_o"], [WN[5]])
                    else:
                        B.mm(W[5][0:128, :], onesf[dr:dr + 1, 0:128], rd[dr:dr + 1, :], True, True, ["RD0", "onesf"], [WN[5]])

                def st2():
                    B.cp("act", bcs[r0:r1, :], W[5][r0:r1, :], [WN[5]], [f"BCS{j}"])

                def st3():
                    if o3 is None:
                        B.tt("dve", on[r0:r1, :], W[ob][r0:r1, :], bcs[r0:r1, :], ALU.mult, [WN[ob], f"BCS{j}"], [f"ON{j}"])
                    else:
                        B.tt("dve", on[r0:r1, :], W[ob][r0:r1, :], o3[0][r0:r1, blk(qb)], ALU.add, [WN[ob], o3[1]], [f"ON{j}"])
                        B.tt("dve", on[r0:r1, :], on[r0:r1, :], bcs[r0:r1, :], ALU.mult, [f"ON{j}", f"BCS{j}"], [f"ON{j}"])

                def st4():
                    B.act(sqT[r0:r1, blk(qb)], on[r0:r1, :], AF.Square, [f"ON{j}"], ["sqT"])
                    B.stt(yT[r0:r1, ychunk, blk(qb)], on[r0:r1, :], gvec, sg[r0:r1, blk(qb)], ALU.mult, ALU.mult,
                          [f"ON{j}", "gvec", sgres], [f"yT{ychunk}"])

                for dly, fn in ((6, st1), (8, st2), (9, st3), (10, st4)):
                    ab["defer"].append((cur + dly, fn))

            def run_deferred(upto):
                q = ab["defer"]
                while q and q[0][0] <= upto:
                    q.pop(0)[1]()

            def emit_rest(i):
                qb, kt, first, last = steps[i]
                s = SB[(base + i) % 2]
                pi = (base + i) % 4
                pt = ab["PT"][pi]
                ob = OB[(ab["epi"]) % 2]
                B.act(pt[:], W[s][:, :], AF.Exp, [WN[s]], [f"PT{pi}"], scale=scale)
                B.mm(W[ob][0:vcols, :], V(kt), pt[:], first, last, [f"PT{pi}"] + reads, [WN[ob]])
                if den[0] == "sep":
                    B.mm(W[4][0:1, :], ones_bf[:, 0:1], pt[:], first, last, [f"PT{pi}", "ones_bf"], [WN[4]])
                if last:
                    if bulk is None:
                        epilogue(qb, ob, base + i)
                    elif den[0] == "sep":
                        B.cp("dve", bulk[0][:, blk(qb)], W[ob][:, :], [WN[ob]], [bulk[1]])
                        B.cp("act", bulk[2][0:1, blk(qb)], W[4][0:1, :], [WN[4]], [bulk[3]])
                        ab["epi"] += 1
                    else:
                        brows = slice(0, 128) if r0 == 64 else slice(0, 65)
                        B.tt("dve", bulk[0][brows, blk(qb)], W[ob][brows, :], bulk[0][brows, blk(qb)], ALU.add,
                             [WN[ob], bulk[1]], [bulk[1]])
                        ab["epi"] += 1

            n = len(steps)
            emit_S(0)
            for i in range(n):
                if i + 1 < n:
                    emit_S(i + 1)
                emit_rest(i)
                run_deferred(base + i)
            if bulk is None:
                run_deferred(1 << 60)
            ab["step"] = base + n

        def run_deferred_ab(ab, upto, maxn=1 << 30):
            q = ab["defer"]
            n = 0
            while q and q[0][0] <= upto and n < maxn:
                q.pop(0)[1]()
                n += 1

        def bulk_epilogue(ab, acc, accres, r0, r1, dr, sg, sgres, gvec, ychunk, sqT, start, dtile=None, dres=None):
            items = []
            if dtile is None:
                dtile, dres = acc, accres
            for qb in range(4):
                t0 = start + 6 * qb
                j = ab["bj"] % 2
                ab["bj"] += 1
                bcs, on = ab["BCS"][j], ab["ON"][j]

                def f_rec(qb=qb):
                    P.add("dve", lambda e: e.reciprocal(out=dtile[dr:dr + 1, blk(qb)], in_=dtile[dr:dr + 1, blk(qb)]),
                          [dres], [dres])

                def f_bc(qb=qb):
                    if r0 == 0 and r1 == 64:
                        B.mm(W[5][0:64, :], onesf[dr:dr + 1, 0:64], dtile[dr:dr + 1, blk(qb)], True, True, [dres, "onesf"], [WN[5]])
                    elif r0 == 64:
                        B.mm(W[5][0:128, :], sel_o[dr:dr + 1, 0:128], dtile[dr:dr + 1, blk(qb)], True, True, [dres, "sel_o"], [WN[5]])
                    else:
                        B.mm(W[5][0:128, :], onesf[dr:dr + 1, 0:128], dtile[dr:dr + 1, blk(qb)], True, True, [dres, "onesf"], [WN[5]])

                def f_cp(bcs=bcs, j=j):
                    B.cp("act", bcs[r0:r1, :], W[5][r0:r1, :], [WN[5]], [f"BCS{j}"])

                def f_mul(qb=qb, bcs=bcs, on=on, j=j):
                    B.tt("dve", on[r0:r1, :], acc[r0:r1, blk(qb)], bcs[r0:r1, :], ALU.mult, [accres, f"BCS{j}"], [f"ON{j}"])

                def f_fin(qb=qb, on=on, j=j):
                    B.act(sqT[r0:r1, blk(qb)], on[r0:r1, :], AF.Square, [f"ON{j}"], ["sqT"])
                    B.stt(yT[r0:r1, ychunk, blk(qb)], on[r0:r1, :], gvec, sg[r0:r1, blk(qb)], ALU.mult, ALU.mult,
                          [f"ON{j}", "gvec", sgres], [f"yT{ychunk}"])

                items += [(t0, f_rec), (t0 + 6, f_bc), (t0 + 8, f_cp), (t0 + 9, f_mul), (t0 + 10, f_fin)]
            ab["defer"] = sorted(ab["defer"] + items, key=lambda x: x[0])

        def attention_p3(ab, KTx, QTc_, V3, odd, mask3, O3s, reads):
            SB = (0, 1)
            OB = (2, 3)
            rows = slice(0, 128) if odd else slice(0, 65)
            vsl = slice(64, 192) if odd else slice(0, 128)
            base = ab["step"]

            def csl(r):
                return slice(r, r + 16 * 127 + 1, 16)

            def emit_S(r):
                s = SB[(base + r) % 2]
                B.mm(W[s][:, 0:128], KTx[:, csl(r)], QTc_[:, csl(r)], True, False, reads, [WN[s]])
                B.mm(W[s][:, 0:128], ident[:, :], mask3, False, True, ["ident", "maskT"], [WN[s]])

            def emit_rest(r):
                s = SB[(base + r) % 2]
                pi = (base + r) % 4
                pt = ab["PT"][pi]
                ob = OB[ab["epi"] % 2]
                B.act(pt[:, 0:128], W[s][:, 0:128], AF.Exp, [WN[s]], [f"PT{pi}"], scale=0.125)
                q = r % 4
                B.mm(W[ob][0:128, q * 128:(q + 1) * 128], V3[:, r, vsl], pt[:, 0:128], True, True, [f"PT{pi}", "V3"], [WN[ob]])
                if q == 3:
                    b3 = r // 4
                    dst = O3s[0][:].rearrange("p (j r) -> p r j", r=16)[rows, 4 * b3:4 * b3 + 4, :]
                    src = W[ob][rows, :].rearrange("p (q j) -> p q j", q=4)
                    B.cp("dve", dst, src, [WN[ob]], [O3s[1]])
                    ab["epi"] += 1

            emit_S(0)
            for r in range(16):
                if r + 1 < 16:
                    emit_S(r + 1)
                emit_rest(r)
                run_deferred_ab(ab, base + r)
            ab["step"] = base + 16

        def ssq_mms(sqT, chunk):
            for tt_ in range(16):
                col = tt_ * 16 + chunk
                B.mm(ssqP[:, col:col + 1], sqT[:, tt_ * 128:(tt_ + 1) * 128], ones_bf[:, 0:1], True, True,
                     ["sqT", "ones_bf"], ["ssqP"])

        def load_w_in(buf, col0, ncols, dst0=0):
            B.dma("pool", wst[buf][:, :, dst0:dst0 + ncols], w_in_v[:, :, col0:col0 + ncols], [], [f"wst{buf}"])

        for b in range(NB):
            m = B.mark()
            gbc = B.T("gbc", [128, 1024], F32)
            bbc = B.T("bbc", [128, 1024], F32)
            xt = [B.T("xt", [128, 1024], F32) for _ in range(4)]
            hb = [B.T("hb", [128, 1024], BF16) for _ in range(3)]
            ha = [B.T("ha", [128, 1024], F32) for _ in range(3)]
            L = ln_alloc(4, 2)
            B.dma("sp", gbc[:], g_emb.partition_broadcast(128), [], ["gbc"])
            B.dma("sp", bbc[:], b_emb.partition_broadcast(128), [], ["bbc"])

            def tsl_(t):
                return slice(t * 128, (t + 1) * 128)

            def p1_load(t):
                B.dma("sp", xt[t % 4][:], x[b, tsl_(t), :], [], [f"xt{t % 4}"])

            def p1_H(t):
                hfr = f"lt1_{t % 3}"
                hf_ = L["t1"][t % 3]
                B.act(hb[t % 3][:], hf_[:], AF.Copy, [hfr], [f"hb{t % 3}"])
                B.act(ha[t % 3][:], hf_[:], AF.Identity, [hfr], [f"ha{t % 3}"], scale=ALPHA)
                B.dma("sp", hscr[b, tsl_(t), :], ha[t % 3][:], [f"ha{t % 3}"], [f"hscr{t % 3}"])
                for k in range(8):
                    B.tr(psT[:, k * 128:(k + 1) * 128], hb[t % 3][:, k * 128:(k + 1) * 128], ident[:],
                         [f"hb{t % 3}", "ident"], ["psT"])

            def p1_J(t):
                B.cp("act", hT[:, :, tsl_(t)], psT[:].rearrange("p (k t) -> p k t", k=8), ["psT"], ["hT"])

            pipeline(16, [(0, p1_load)] + ln_stages(L, lambda t: (xt[t % 4][:], f"xt{t % 4}"), gbc, "gbc", bbc, "bbc", 1)
                     + [(4, p1_H), (5, p1_J)])
            P.barrier()
            B.release(m)

            m = B.mark()
            memb = B.T("memb", [128, 2, 1024], BF16)
            memT = B.T("memT", [128, 8, NMEM], BF16)
            mkT = B.T("mkT", [128, 4, NMEM], BF16)
            mv = B.T("mv", [128, 2, 512], BF16)
            qTm = [B.T("qTm", [128, S], BF16) for _ in range(2)]
            sgm = [B.T("sgm", [128, S], BF16) for _ in range(2)]
            accm = [B.T("accm", [128, S], F32) for _ in range(2)]
            denm = [B.T("denm", [128, S], F32) for _ in range(2)]
            sqT = B.T("sqT", [128, S], BF16)
            ab = attn_bufs()
            wmk = w_mem_kv.rearrange("(k p) c -> p k c", p=128)
            B.dma("pool", memb[:], mem[b].rearrange("(t p) d -> p t d", p=128), [], ["memb"])
            B.dma("pool", wst[0][:], wmk[:, :, 0:512], [], ["wst0"])
            B.dma("pool", wst[1][:], wmk[:, :, 512:1024], [], ["wst1"])
            for t in range(2):
                for k in range(8):
                    B.tr(psT[:, k * 128:(k + 1) * 128], memb[:, t, k * 128:(k + 1) * 128], ident[:], ["memb", "ident"], ["psT"])
                B.cp("act", memT[:, :, t * 128:(t + 1) * 128], psT[:].rearrange("p (k t) -> p k t", k=8), ["psT"], ["memT"])
            for h in range(4):
                bank, bn = next_w()
                for k in range(8):
                    B.mm(bank[:, 0:NMEM], wst[0][:, k, h * 128:(h + 1) * 128], memT[:, k, :], k == 0, k == 7,
                         ["wst0", "memT"], [bn])
                B.cp("dve", mkT[:, h, :], bank[:, 0:NMEM], [bn], ["mkT"])
            for t in range(2):
                bank, bn = next_w()
                for k in range(8):
                    B.mm(bank[:, :], memT[:, k, t * 128:(t + 1) * 128], wst[1][:, k, :], k == 0, k == 7, ["wst1", "memT"], [bn])
                B.cp("dve", mv[:, t, :], bank[:, :], [bn], ["mv"])
            load_w_in(0, 5024, 512)
            load_w_in(1, 5536, 512)
            for h in range(4):
                sl = h % 2
                for qb in range(4):
                    bank, bn = proj_fm(lambda k: wst[0][:, k, h * 128:(h + 1) * 128], 8, lambda k: hT[:, k, blk(qb)], 128,
                                       ["wst0", "hT"])
                    B.cp("dve", qTm[sl][:, blk(qb)], bank[:, :], [bn], [f"qTm{sl}"])
                    run_deferred_ab(ab, 1 << 60, maxn=3)
                    bank, bn = proj_fm(lambda k: wst[1][:, k, h * 128:(h + 1) * 128], 8, lambda k: hT[:, k, blk(qb)], 128,
                                       ["wst1", "hT"])
                    B.act(sgm[sl][:, blk(qb)], bank[:, :], AF.Silu, [bn], [f"sgm{sl}"])
                    run_deferred_ab(ab, 1 << 60, maxn=3)
                run_deferred_ab(ab, 1 << 60)
                if h > 0:
                    ssq_mms(sqT, 12 + h - 1)
                attention(ab, KT=lambda kt: mkT[:, h, kt * 128:(kt + 1) * 128], QT=lambda qb: qTm[sl][:, blk(qb)],
                          V=lambda kt: mv[:, kt, h * 128:(h + 1) * 128], vcols=128, nkt=2, r0=0, r1=128, den=("sep",),
                          scale=128.0 ** -0.5, maskfn=None, sg=sgm[sl], sgres=f"sgm{sl}", gvec=gom_t[:, h:h + 1],
                          ychunk=12 + h, sqT=sqT, reads=["mkT", "mv", f"qTm{sl}"],
                          bulk=(accm[sl], f"accm{sl}", denm[sl], f"denm{sl}"))
                bulk_epilogue(ab, accm[sl], f"accm{sl}", 0, 128, 0, sgm[sl], f"sgm{sl}", gom_t[:, h:h + 1], 12 + h, sqT,
                              ab["step"], dtile=denm[sl], dres=f"denm{sl}")
            run_deferred_ab(ab, 1 << 60)
            ssq_mms(sqT, 15)
            P.barrier()
            B.release(m)

            m = B.mark()
            cosB = B.T("cosB", [128, S], F32)
            sinB = B.T("sinB", [128, S], F32)
            rope_tables(b, 2, 3, cosB, sinB, "B")
            wuq = B.T("wuq", [128, 2, 768], BF16)
            wuq_sw = B.T("wuq_sw", [128, 2, 768], BF16)
            wukv = B.T("wukv", [128, 1024], BF16)
            wkr_sw = B.T("wkr_sw", [128, 8, 96], BF16)
            cqn = B.T("cqn", [128, 2, S], BF16)
            ckvn = B.T("ckvn", [128, S], BF16)
            kr = B.T("kr", [128, S], BF16)
            V2B = B.T("V2B", [128, 16, 192], BF16)
            sqt = [B.T("sqt", [128, 512], BF16) for _ in range(2)]
            R = [B.T("R", [128, 512], F32) for _ in range(2)]
            rt1 = B.T("rt1", [128, 512], F32)
            rt2 = B.T("rt2", [128, 512], F32)
            QTh = B.T("QTh", [128, S], BF16)
            KTh = B.T("KTh", [128, S], BF16)
            sgb = B.T("sgb", [128, S], BF16)
            sqT = B.T("sqT", [128, S], BF16)
            ab = attn_bufs()
            B.dma("pool", wuq[:], w_uq.rearrange("(k p) c -> p k c", p=128), [], ["wuq"])
            B.dma("pool", wukv[:], w_ukv, [], ["wukv"])
            load_w_in(0, 4096, 512)
            load_w_in(1, 4512, 512)
            B.memset("pool", wuq_sw[:], 0.0, ["wuq_sw"])
            wuq4 = wuq[:].rearrange("p k (h t) -> p k h t", t=96)
            wsw4 = wuq_sw[:].rearrange("p k (h t) -> p k h t", t=96)
            for kc in range(2):
                B.cp("pool", wsw4[:, kc, :, 64:80], wuq4[:, kc, :, 80:96], ["wuq", "wuq_sw"], ["wuq_sw"])
                B.cp("pool", wsw4[:, kc, :, 80:96], wuq4[:, kc, :, 64:80], ["wuq", "wuq_sw"], ["wuq_sw"])
            B.memset("pool", wkr_sw[:], 0.0, ["wkr_sw"])
            B.cp("pool", wkr_sw[:, :, 64:80], wst[0][:, :, 400:416], ["wst0", "wkr_sw"], ["wkr_sw"])
            B.cp("pool", wkr_sw[:, :, 80:96], wst[0][:, :, 384:400], ["wst0", "wkr_sw"], ["wkr_sw"])
            B.memset("pool", V2B[:, :, 64:65], 1.0, ["V2B"])
            B.memset("pool", V2B[:, :, 65:128], 0.0, ["V2B"])
            rc = 0
            for qb in range(4):
                hrhs = lambda k: hT[:, k, blk(qb)]
                cb = []
                for c in range(2):
                    cb.append(proj_fm(lambda k: wst[0][:, k, c * 128:(c + 1) * 128], 8, hrhs, 128, ["wst0", "hT"]))
                for c in range(2):
                    B.act(sqt[c][:], cb[c][0][:, :], AF.Square, [cb[c][1]], [f"sqt{c}"])
                sbank, sbn = next_w()
                for c in range(2):
                    B.mm(sbank[:, :], ones_bf[:, :], sqt[c][:], c == 0, c == 1, [f"sqt{c}", "ones_bf"], [sbn])
                j = rc % 2
                rc += 1
                B.ts("dve", R[j][:], sbank[:, :], 1.0 / 256, EPS, ALU.mult, ALU.add, [sbn], [f"R{j}"])
                B.act(R[j][:], R[j][:], AF.Sqrt, [f"R{j}"], [f"R{j}"])
                P.add("dve", lambda e, j=j: e.reciprocal(out=R[j][:], in_=R[j][:]), [f"R{j}"], [f"R{j}"])
                for c in range(2):
                    B.stt(cqn[:, c, blk(qb)], cb[c][0][:, :], gcq_t[:, c:c + 1], R[j][:], ALU.mult, ALU.mult,
                          [cb[c][1], "gvec", f"R{j}"], ["cqn"])
                kb, kbn = proj_fm(lambda k: wst[0][:, k, 256:384], 8, hrhs, 128, ["wst0", "hT"])
                B.act(sqt[0][:], kb[:, :], AF.Square, [kbn], ["sqt0"])
                sbank, sbn = next_w()
                B.mm(sbank[:, :], ones_bf[:, :], sqt[0][:], True, True, ["sqt0", "ones_bf"], [sbn])
                j = rc % 2
                rc += 1
                B.ts("dve", R[j][:], sbank[:, :], 1.0 / 128, EPS, ALU.mult, ALU.add, [sbn], [f"R{j}"])
                B.act(R[j][:], R[j][:], AF.Sqrt, [f"R{j}"], [f"R{j}"])
                P.add("dve", lambda e, j=j: e.reciprocal(out=R[j][:], in_=R[j][:]), [f"R{j}"], [f"R{j}"])
                B.stt(ckvn[:, blk(qb)], kb[:, :], gckv_t[:, 0:1], R[j][:], ALU.mult, ALU.mult, [kbn, "gvec", f"R{j}"], ["ckvn"])
                pa, pan = proj_fm(lambda k: wst[0][:, k, 320:416], 8, hrhs, 96, ["wst0", "hT"])
                pb, pbn = proj_fm(lambda k: wkr_sw[:, k, 0:96], 8, hrhs, 96, ["wkr_sw", "hT"])
                B.tt("dve", rt1[64:96, :], pa[64:96, :], cosB[64:96, blk(qb)], ALU.mult, [pan, "cosB"], ["rt1"])
                B.tt("dve", rt2[64:96, :], pb[64:96, :], sinB[64:96, blk(qb)], ALU.mult, [pbn, "sinB"], ["rt2"])
                B.tt("pool", kr[64:96, blk(qb)], rt1[64:96, :], rt2[64:96, :], ALU.add, ["rt1", "rt2"], ["kr"])
            wv3 = wukv[:].rearrange("p (h t) -> p h t", t=128)
            for h in range(8):
                pr = h // 2
                odd = h % 2
                if not odd:
                    for qb in range(4):
                        bank, bn = proj_fm(lambda k: wst[1][:, k, pr * 128:(pr + 1) * 128], 8, lambda k: hT[:, k, blk(qb)], 128,
                                           ["wst1", "hT"])
                        B.act(sgb[:, blk(qb)], bank[:, :], AF.Silu, [bn], ["sgb"])
                    for t4 in range(4):
                        bank, bn = next_w()
                        for tq in range(4):
                            tt_ = t4 * 4 + tq
                            B.mm(bank[:, tq * 128:(tq + 1) * 128], ckvn[:, tt_ * 128:(tt_ + 1) * 128],
                                 wv3[:, 2 * pr:2 * pr + 2, 64:128], True, True, ["ckvn", "wukv"], [bn])
                        bv = bank[:, :].rearrange("p (q e t) -> p q e t", q=4, e=2)
                        B.cp("act", V2B[:, t4 * 4:(t4 + 1) * 4, 0:64], bv[:, :, 0, :], [bn, "V2B"], ["V2B"])
                        B.cp("dve", V2B[:, t4 * 4:(t4 + 1) * 4, 128:192], bv[:, :, 1, :], [bn, "V2B"], ["V2B"])
                for qb in range(4):
                    pa, pan = proj_fm(lambda k: wuq[:, k, h * 96:(h + 1) * 96], 2, lambda k: cqn[:, k, blk(qb)], 96, ["wuq", "cqn"])
                    pb, pbn = proj_fm(lambda k: wuq_sw[:, k, h * 96:(h + 1) * 96], 2, lambda k: cqn[:, k, blk(qb)], 96,
                                      ["wuq_sw", "cqn"])
                    B.cp("act", QTh[0:64, blk(qb)], pa[0:64, :], [pan], ["QTh"])
                    B.tt("dve", rt1[64:96, :], pa[64:96, :], cosB[64:96, blk(qb)], ALU.mult, [pan, "cosB"], ["rt1"])
                    B.tt("dve", rt2[64:96, :], pb[64:96, :], sinB[64:96, blk(qb)], ALU.mult, [pbn, "sinB"], ["rt2"])
                    B.tt("pool", QTh[64:96, blk(qb)], rt1[64:96, :], rt2[64:96, :], ALU.add, ["rt1", "rt2", "QTh"], ["QTh"])
                    pc, pcn = proj_fm(lambda k: wukv[:, h * 128:h * 128 + 64], 1, lambda k: ckvn[:, blk(qb)], 64, ["wukv", "ckvn"])
                    B.cp("act", KTh[0:64, blk(qb)], pc[0:64, :], [pcn], ["KTh"])
                B.cp("pool", KTh[64:96, :], kr[64:96, :], ["kr", "KTh"], ["KTh"])
                if not odd:
                    V = lambda kt: V2B[:, kt, 0:128]
                    vcols, r0, r1, den = 128, 0, 64, ("row", 64)
                else:
                    V = lambda kt: V2B[:, kt, 64:192]
                    vcols, r0, r1, den = 128, 64, 128, ("row", 0)
                attention(ab, KT=lambda kt: KTh[0:96, kt * 128:(kt + 1) * 128], QT=lambda qb: QTh[0:96, blk(qb)],
                          V=V, vcols=vcols, nkt=16, r0=r0, r1=r1, den=den, scale=96.0 ** -0.5, maskfn=None,
                          sg=sgb, sgres="sgb", gvec=gob_t[r0:r1, pr:pr + 1], ychunk=8 + pr, sqT=sqT,
                          reads=["V2B", "QTh", "KTh"])
                if odd:
                    ssq_mms(sqT, 8 + pr)
            P.barrier()
            B.release(m)

            m = B.mark()
            cosA = B.T("cosA", [128, S], F32)
            sinA = B.T("sinA", [128, S], F32)
            rope_tables(b, 0, 1, cosA, sinA, "A")
            maskT = B.T("maskT", [128, MASK_W + 128], BF16)
            V3 = B.T("V3", [128, 16, 192], BF16)
            O3 = [B.T("O3s", [128, S], F32) for _ in range(2)]
            wswq = B.T("wswq", [128, 8, 128], BF16)
            wswk = B.T("wswk", [128, 8, 128], BF16)
            QTc = B.T("QTc", [128, S], BF16)
            KTc = B.T("KTc", [128, S], BF16)
            KTo = B.T("KTo", [128, S], BF16)
            V2A = B.T("V2A", [128, 16, 192], BF16)
            sga = [B.T("sga", [128, S], BF16) for _ in range(2)]
            rt1 = B.T("rt1", [128, 512], F32)
            rt2 = B.T("rt2", [128, 512], F32)
            sqT = B.T("sqT", [128, S], BF16)
            ab = dict(PT=[B.T("PT", [128, 512], BF16) for _ in range(4)],
                      BCS=[B.T("BCS", [128, 512], F32) for _ in range(2)],
                      ON=[B.T("ON", [128, 512], F32) for _ in range(2)],
                      step=0, epi=0, defer=[], bj=0)
            B.dma("pool", maskT[:], maskd, [], ["maskT"])
            B.memset("pool", KTc[64:128, :], 0.0, ["KTc"])
            B.memset("pool", KTo[0:64, :], 0.0, ["KTc"])
            B.memset("pool", wswq[:], 0.0, ["wswq"])
            B.memset("pool", wswk[:], 0.0, ["wswk"])
            B.memset("pool", V2A[:, :, 64:65], 1.0, ["V2A"])
            B.memset("pool", V2A[:, :, 65:128], 0.0, ["V2A"])
            B.memset("pool", V3[:, :, 64:65], 1.0, ["V3"])
            B.memset("pool", V3[:, :, 65:128], 0.0, ["V3"])

            def maskfn(kt, qb):
                dmin = 128 * kt - 512 * qb - 511
                dmax = 128 * kt + 127 - 512 * qb
                if dmin > 256 or dmax < -256:
                    return None
                off = MASK_X0 - 128 * kt + 512 * qb
                assert 0 <= off and off + 512 <= MASK_W
                return maskT[:, off:off + 512]

            for c in range(8):
                sl = c % 2
                wb = wst[sl]
                sgc = sga[sl]
                sgn = f"sga{sl}"
                for i in range(4):
                    load_w_in(sl, 1024 * i + c * 128, 128, dst0=128 * i)
                wq4 = wb[:, :, 0:128].rearrange("p k (e t) -> p k e t", e=2)
                wk4 = wb[:, :, 128:256].rearrange("p k (e t) -> p k e t", e=2)
                sq4 = wswq[:].rearrange("p k (e t) -> p k e t", e=2)
                sk4 = wswk[:].rearrange("p k (e t) -> p k e t", e=2)
                B.cp("pool", sq4[:, :, :, 0:8], wq4[:, :, :, 8:16], [f"wst{sl}", "wswq"], ["wswq"])
                B.cp("pool", sq4[:, :, :, 8:16], wq4[:, :, :, 0:8], [f"wst{sl}", "wswq"], ["wswq"])
                B.cp("pool", sk4[:, :, :, 0:8], wk4[:, :, :, 8:16], [f"wst{sl}", "wswk"], ["wswk"])
                B.cp("pool", sk4[:, :, :, 8:16], wk4[:, :, :, 0:8], [f"wst{sl}", "wswk"], ["wswk"])
                for qb in range(4):
                    hrhs = lambda k: hT[:, k, blk(qb)]
                    for (dst, dname, c0, wsw, wswn) in ((QTc, "QTc", 0, wswq, "wswq"), (KTc, "KTc", 128, wswk, "wswk")):
                        pa, pan = proj_fm(lambda k: wb[:, k, c0:c0 + 128], 8, hrhs, 128, [f"wst{sl}", "hT"])
                        pb, pbn = proj_fm(lambda k: wsw[:, k, :], 8, hrhs, 128, [wswn, "hT"])
                        B.tt("dve", rt1[:], pa[:, :], cosA[:, blk(qb)], ALU.mult, [pan, "cosA"], ["rt1"])
                        B.tt("dve", rt2[:], pb[:, :], sinA[:, blk(qb)], ALU.mult, [pbn, "sinA"], ["rt2"])
                        if dname == "QTc":
                            B.tt("pool", dst[:, blk(qb)], rt1[:], rt2[:], ALU.add, ["rt1", "rt2"], [dname])
                        else:
                            B.tt("pool", KTc[0:64, blk(qb)], rt1[0:64, :], rt2[0:64, :], ALU.add, ["rt1", "rt2"], [dname])
                            B.tt("pool", KTo[64:128, blk(qb)], rt1[64:128, :], rt2[64:128, :], ALU.add, ["rt1", "rt2"], [dname])
                        run_deferred_ab(ab, 1 << 60, maxn=3)
                    bank, bn = proj_fm(lambda k: wb[:, k, 384:512], 8, hrhs, 128, [f"wst{sl}", "hT"])
                    B.act(sgc[:, blk(qb)], bank[:, :], AF.Silu, [bn], [sgn])
                for t4 in range(4):
                    bank, bn = next_w()
                    for tq in range(4):
                        tt_ = t4 * 4 + tq
                        for k in range(8):
                            B.mm(bank[:, tq * 128:(tq + 1) * 128], hT[:, k, tt_ * 128:(tt_ + 1) * 128], wb[:, k, 256:384],
                                 k == 0, k == 7, [f"wst{sl}", "hT"], [bn])
                    bv = bank[:, :].rearrange("p (q e t) -> p q e t", q=4, e=2)
                    B.cp("act", V2A[:, t4 * 4:(t4 + 1) * 4, 0:64], bv[:, :, 0, :], [bn, "V2A"], ["V2A"])
                    B.cp("dve", V2A[:, t4 * 4:(t4 + 1) * 4, 128:192], bv[:, :, 1, :], [bn, "V2A"], ["V2A"])
                for r4 in range(4):
                    bank, bn = next_w()
                    for tq in range(4):
                        r = r4 * 4 + tq
                        for k in range(8):
                            B.mm(bank[:, tq * 128:(tq + 1) * 128], hT[:, k, r:r + 16 * 127 + 1:16], wb[:, k, 256:384],
                                 k == 0, k == 7, [f"wst{sl}", "hT"], [bn])
                    bv = bank[:, :].rearrange("p (q e t) -> p q e t", q=4, e=2)
                    B.cp("act", V3[:, r4 * 4:(r4 + 1) * 4, 0:64], bv[:, :, 0, :], [bn, "V3"], ["V3"])
                    B.cp("dve", V3[:, r4 * 4:(r4 + 1) * 4, 128:192], bv[:, :, 1, :], [bn, "V3"], ["V3"])
                run_deferred_ab(ab, 1 << 60)
                if c > 0:
                    ssq_mms(sqT, c - 1)
                for odd in range(2):
                    if not odd:
                        V = lambda kt: V2A[:, kt, 0:128]
                        vcols, r0, r1, den = 128, 0, 64, ("row", 64)
                    else:
                        V = lambda kt: V2A[:, kt, 64:192]
                        vcols, r0, r1, den = 128, 64, 128, ("row", 0)
                    KTx = KTo if odd else KTc
                    acc, accres = O3[odd], f"O3s{odd}"
                    attention_p3(ab, KTx, QTc, V3, odd, maskT[:, MASK_W:MASK_W + 128], (acc, accres), ["QTc", "KTc"])
                    attention(ab, KT=lambda kt: KTx[:, kt * 128:(kt + 1) * 128], QT=lambda qb: QTc[:, blk(qb)],
                              V=V, vcols=vcols, nkt=16, r0=r0, r1=r1, den=den, scale=0.125, maskfn=maskfn,
                              sg=sgc, sgres=sgn, gvec=goa_t[r0:r1, c:c + 1], ychunk=c, sqT=sqT,
                              reads=["V2A", "QTc", "KTc"], bulk=(acc, accres))
                    bulk_epilogue(ab, acc, accres, r0, r1, den[1], sgc, sgn, goa_t[r0:r1, c:c + 1], c, sqT,
                                  ab["step"] + (17 if not odd else 0))
            run_deferred_ab(ab, 1 << 60)
            ssq_mms(sqT, 7)
            P.barrier()
            B.release(m)

            m = B.mark()
            wo = B.T("wo", [128, 16, D], BF16)
            gpc = B.T("gpc", [128, 1024], F32)
            bpc = B.T("bpc", [128, 1024], F32)
            hf = [B.T("hf", [128, 1024], F32) for _ in range(3)]
            acc = [B.T("acc", [128, 1024], F32) for _ in range(3)]
            rg = [B.T("rg", [128, 4], F32) for _ in range(3)]
            L = ln_alloc(4, 2)
            wo_v = w_out.rearrange("(k p) c -> p k c", p=128)
            for q4 in range(4):
                B.dma("pool", wo[:, q4 * 4:(q4 + 1) * 4, :], wo_v[:, q4 * 4:(q4 + 1) * 4, :], [], ["wo"])
            B.dma("sp", gpc[:], g_post.partition_broadcast(128), [], ["gpc"])
            B.dma("sp", bpc[:], b_post.partition_broadcast(128), [], ["bpc"])
            groups = ((0, 8, 1024.0), (8, 12, 512.0), (12, 16, 512.0))

            def tsl_(t):
                return slice(t * 128, (t + 1) * 128)

            def f_load(t):
                B.dma("sp", hf[t % 3][:], hscr[b, tsl_(t), :], [], [f"hf{t % 3}"])

            def f_mm(t):
                for n in range(2):
                    for gi, (c0, c1, width) in enumerate(groups):
                        bi = n * 3 + gi
                        for c in range(c0, c1):
                            B.mm(W[bi][:, :], yT[:, c, tsl_(t)], wo[:, c, blk(n)], c == c0, c == c1 - 1, ["yT", "wo"],
                                 [f"{WN[bi]}"])

            def f_rg(t):
                k = t % 3
                for gi, (c0, c1, width) in enumerate(groups):
                    P.add("dve", lambda e, gi=gi, c0=c0, c1=c1: e.reduce_sum(
                        out=rg[k][:, gi:gi + 1], in_=ssqP[:, t * 16 + c0:t * 16 + c1], axis=mybir.AxisListType.X),
                        ["ssqP"], [f"rg{k}"])
                    B.ts("dve", rg[k][:, gi:gi + 1], rg[k][:, gi:gi + 1], 1.0 / width, EPS, ALU.mult, ALU.add,
                         [f"rg{k}"], [f"rg{k}"])
                B.tt("pool", rg[k][:, 0:3], rg[k][:, 0:3], mhalf[:, 0:3], ALU.pow, [f"rg{k}", "mhalf"], [f"rg{k}"])

            def f_stt(t):
                k = t % 3
                for n in range(2):
                    for gi in range(3):
                        bi = n * 3 + gi
                        src = hf[k][:, blk(n)] if gi == 0 else acc[k][:, blk(n)]
                        srcn = f"hf{k}" if gi == 0 else f"acc{k}"
                        B.stt(acc[k][:, blk(n)], W[bi][:, :], rg[k][:, gi:gi + 1], src, ALU.mult, ALU.add,
                              [WN[bi], f"rg{k}", srcn], [f"acc{k}"])

            def f_store(t):
                B.dma("sp", out[b, tsl_(t), :], L["t1"][t % 3][:], [f"lt1_{t % 3}"], [f"out{t % 3}"])

            pipeline(16, [(0, f_load), (0, f_mm), (0, f_rg), (1, f_stt)]
                     + ln_stages(L, lambda t: (acc[t % 3][:], f"acc{t % 3}"), gpc, "gpc", bpc, "bpc", 2)
                     + [(5, f_store)])
            P.barrier()
            B.release(m)

        stats = P.emit(esems, dsems)
    return nc, stats


def _host_consts():
    cst = np.zeros((128, 8), np.float32)
    for p in range(128):
        j = p % 64
        if j < 16:
            cst[p, 0] = THETA ** (-(2.0 * (j % 8)) / 16.0)
            cst[p, 1] = -1.0 if j < 8 else 1.0
        else:
            cst[p, 0] = 0.0
            cst[p, 1] = 1.0
        if 64 <= p < 96:
            jj = p - 64
            cst[p, 2] = THETA ** (-(2.0 * (jj % 16)) / 32.0)
            cst[p, 3] = -1.0 if jj < 16 else 1.0
        else:
            cst[p, 2] = 0.0
            cst[p, 3] = 1.0
    kl = np.arange(128)[:, None]
    xx = np.arange(MASK_W)[None, :]
    d = kl - xx + MASK_X0
    f = ((np.abs(d) <= 64).astype(np.float32)
         + ((d % 4 == 0) & (np.abs(d) <= 256)).astype(np.float32))
    mb = np.where(f > 0, np.log(np.maximum(f, 1.0)) / 0.125, -320.0)
    d3 = np.arange(128)[:, None] - np.arange(128)[None, :]
    m3 = np.where(np.abs(d3) <= 64, 0.0, -320.0)
    return cst, np.concatenate([mb, m3], axis=1).astype(np.float32)


_CACHE = {}


def kernel(x, mem, positions, g_emb, b_emb, w_in, g_cq, g_ckv, w_uq, w_ukv, w_mem_kv,
           g_out_a, g_out_b, g_out_m, w_out, g_post, b_post):
    f32 = lambda a: np.ascontiguousarray(np.asarray(a), dtype=np.float32)
    if "nc" not in _CACHE:
        _CACHE["nc"] = build_program()[0]
    nc = _CACHE["nc"]
    cst, maskd = _host_consts()
    x = f32(x)
    mem = f32(mem)
    positions = np.ascontiguousarray(np.asarray(positions), dtype=np.int32)
    shared = dict(
        g_emb=f32(g_emb), b_emb=f32(b_emb), w_in=f32(w_in).reshape(D, D_IN), g_cq=f32(g_cq).reshape(256),
        g_ckv=f32(g_ckv).reshape(128), w_uq=f32(w_uq).reshape(256, 768), w_ukv=f32(w_ukv).reshape(128, 1024),
        w_mem_kv=f32(w_mem_kv).reshape(D, 1024), g_out_a=f32(g_out_a).reshape(1024), g_out_b=f32(g_out_b).reshape(512),
        g_out_m=f32(g_out_m).reshape(512), w_out=f32(w_out).reshape(2048, D), g_post=f32(g_post).reshape(D),
        b_post=f32(b_post).reshape(D), cst=cst, maskd=maskd)
    in_maps = []
    for c in range(N_CORES):
        d = dict(shared)
        d["x"] = x[c * NB:(c + 1) * NB]
        d["mem"] = mem[c * NB:(c + 1) * NB]
        d["positions"] = positions[c * NB:(c + 1) * NB]
        in_maps.append(d)
    res = run_bass_kernel_spmd(nc, in_maps, core_ids=list(range(N_CORES)))
    return np.concatenate([r["out"] for r in res.results], axis=0)
```

```python
import numpy as np
from contextlib import ExitStack

import concourse.bass as bass
import concourse.mybir as mybir
from concourse.bass_utils import run_bass_kernel_spmd

F32 = mybir.dt.float32
BF16 = mybir.dt.bfloat16
I32 = mybir.dt.int32
ALU = mybir.AluOpType
AF = mybir.ActivationFunctionType

N_CORES = 8
NB = 2
S = 2048
D = 1024
D_IN = 6048
NMEM = 256
EPS = 1e-5
ALPHA = 2.0 ** 0.25
THETA = 500000.0
TWO_PI = float(2 * np.pi)
C1 = 6.28125
C2 = float(2 * np.pi - 6.28125)
MASK_W = 1408
MASK_X0 = 640

ENGS = ("pe", "act", "dve", "pool", "sp")


class Op:
    __slots__ = ("eng", "fn", "deps", "signal", "sigval", "is_dma", "dsem", "dval", "dprev",
                 "epoch", "esem")

    def __init__(self, eng, fn, is_dma):
        self.eng = eng
        self.fn = fn
        self.deps = {}
        self.signal = False
        self.sigval = 0
        self.is_dma = is_dma
        self.dsem = None
        self.dval = 0
        self.dprev = 0
        self.epoch = 0
        self.esem = None


class Prog:
    N_EPOCH_SEMS = 6
    N_DMA_SEMS = 28
    N_SW_SEMS = 10

    def __init__(self, nc):
        self.nc = nc
        self.items = {e: [] for e in ENGS}
        self.last_writer = {}
        self.readers = {}
        self.epoch = 0
        self.barriers = []
        self.dma_count = 0
        self.sw_count = 0
        self.dma_sem_counts = [0] * self.N_DMA_SEMS
        self.epoch_ops = {e: [] for e in ENGS}

    def add(self, eng, fn, reads=(), writes=(), dma=False):
        op = Op(eng, fn, dma)
        op.epoch = self.epoch
        for r in reads:
            w = self.last_writer.get(r)
            if w is not None:
                op.deps[w] = True
            self.readers.setdefault(r, []).append(op)
        for r in writes:
            w = self.last_writer.get(r)
            if w is not None and w is not op:
                op.deps.setdefault(w, False)
            for rd in self.readers.get(r, ()):
                if rd is not op:
                    op.deps.setdefault(rd, False)
            self.last_writer[r] = op
            self.readers[r] = []
        if dma:
            if eng == "pool":
                k = self.N_DMA_SEMS - self.N_SW_SEMS + (self.sw_count % self.N_SW_SEMS)
                self.sw_count += 1
            else:
                k = self.dma_count % (self.N_DMA_SEMS - self.N_SW_SEMS)
                self.dma_count += 1
            op.dsem = k
            op.dprev = 16 * self.dma_sem_counts[k]
            self.dma_sem_counts[k] += 1
            op.dval = 16 * self.dma_sem_counts[k]
        self.items[eng].append(op)
        self.epoch_ops[eng].append(op)
        return op

    def barrier(self):
        lasts = {}
        for e in ENGS:
            ops = [o for o in self.epoch_ops[e] if not o.is_dma]
            lasts[e] = ops[-1] if ops else None
            if lasts[e] is not None:
                lasts[e].signal = True
        self.barriers.append((lasts, list(self.dma_sem_counts)))
        for e in ENGS:
            self.items[e].append(("barrier", len(self.barriers) - 1))
            self.epoch_ops[e] = []
        self.last_writer = {}
        self.readers = {}
        self.epoch += 1

    def emit(self, esems, dsems):
        nc = self.nc
        for e in ENGS:
            for it in self.items[e]:
                if isinstance(it, tuple):
                    continue
                for d, raw in it.deps.items():
                    if d.is_dma or d.epoch != it.epoch:
                        continue
                    if d.eng == it.eng and not it.is_dma:
                        if it.eng == "pe" or not raw:
                            continue
                    d.signal = True
        cum = {e: [0] * self.N_EPOCH_SEMS for e in ENGS}
        for e in ENGS:
            for it in self.items[e]:
                if isinstance(it, tuple) or it.is_dma:
                    continue
                k = it.epoch % self.N_EPOCH_SEMS
                it.esem = esems[e][k]
                if it.signal:
                    cum[e][k] += 1
                    it.sigval = cum[e][k]
        stats = {e: [0, 0] for e in ENGS}

        def run_engine(e, eng):
            waited = {}

            def wait(sem, val):
                if val <= 0:
                    return
                key = id(sem)
                if waited.get(key, 0) >= val:
                    return
                eng.wait_ge(sem, val)
                waited[key] = val
                stats[e][1] += 1

            for it in self.items[e]:
                if isinstance(it, tuple):
                    lasts, dcounts = self.barriers[it[1]]
                    for e2 in ENGS:
                        lo = lasts[e2]
                        if lo is not None and e2 != e:
                            wait(lo.esem, lo.sigval)
                    for k, c in enumerate(dcounts):
                        wait(dsems[k], 16 * c)
                    continue
                op = it
                for d, raw in op.deps.items():
                    if d.epoch != op.epoch:
                        continue
                    if d.is_dma:
                        wait(dsems[d.dsem], d.dval)
                        continue
                    if d.eng == op.eng and not op.is_dma:
                        if op.eng == "pe" or not raw:
                            continue
                    wait(d.esem, d.sigval)
                if op.is_dma:
                    wait(dsems[op.dsem], op.dprev)
                ins = op.fn(eng)
                stats[e][0] += 1
                if op.is_dma:
                    ins.then_inc(dsems[op.dsem], 16)
                elif op.signal:
                    ins.then_inc(op.esem, 1)
            if e == "sp":
                for k, c in enumerate(self.dma_sem_counts):
                    wait(dsems[k], 16 * c)

        with nc.Block() as block:
            @block.tensor
            def _(eng):
                run_engine("pe", eng)

            @block.scalar
            def _(eng):
                run_engine("act", eng)

            @block.vector
            def _(eng):
                run_engine("dve", eng)

            @block.gpsimd
            def _(eng):
                run_engine("pool", eng)

            @block.sync
            def _(eng):
                run_engine("sp", eng)
        return stats


class Builder:
    SB_BASE = 16512
    SB_LIMIT = 229344

    def __init__(self, nc):
        self.nc = nc
        self.P = Prog(nc)
        self.cur = self.SB_BASE
        self.uid = 0
        self.cache = {}

    def T(self, name, shape, dt):
        n = 1
        for s in shape[1:]:
            n *= s
        nbytes = n * mybir.dt.size(dt)
        nbytes = (nbytes + 31) // 32 * 32
        assert self.cur + nbytes <= self.SB_LIMIT, (name, self.cur, nbytes)
        key = (self.cur, tuple(shape), str(dt))
        t = self.cache.get(key)
        if t is None:
            self.uid += 1
            t = self.nc.alloc_sbuf_tensor_at(f"{name}_{self.uid}", shape, dt, offset=self.cur)
            self.cache[key] = t
        self.cur += nbytes
        return t

    def mark(self):
        return self.cur

    def release(self, m):
        self.cur = m

    def mm(self, out, lhsT, rhs, start, stop, reads, writes):
        self.P.add("pe", lambda e: e.matmul(out, lhsT=lhsT, rhs=rhs, start=start, stop=stop), reads, writes)

    def tr(self, out, in_, ident, reads, writes):
        self.P.add("pe", lambda e: e.transpose(out=out, in_=in_, identity=ident), reads, writes)

    def act(self, out, in_, func, reads, writes, scale=None, bias=None):
        kw = {}
        if scale is not None:
            kw["scale"] = scale
        if bias is not None:
            kw["bias"] = bias
        self.P.add("act", lambda e: e.activation(out=out, in_=in_, func=func, **kw), reads, writes)

    def tt(self, eng, out, in0, in1, op, reads, writes):
        self.P.add(eng, lambda e: e.tensor_tensor(out=out, in0=in0, in1=in1, op=op), reads, writes)

    def ts(self, eng, out, in0, s1, s2, op0, op1, reads, writes):
        if s2 is None:
            self.P.add(eng, lambda e: e.tensor_scalar(out=out, in0=in0, scalar1=s1, scalar2=None, op0=op0), reads, writes)
        else:
            self.P.add(eng, lambda e: e.tensor_scalar(out=out, in0=in0, scalar1=s1, scalar2=s2, op0=op0, op1=op1), reads, writes)

    def stt(self, out, in0, scalar, in1, op0, op1, reads, writes):
        self.P.add("dve", lambda e: e.scalar_tensor_tensor(out=out, in0=in0, scalar=scalar, in1=in1, op0=op0, op1=op1),
                   reads, writes)

    def cp(self, eng, out, in_, reads, writes):
        if eng == "act":
            self.P.add("act", lambda e: e.activation(out=out, in_=in_, func=AF.Copy), reads, writes)
        else:
            self.P.add(eng, lambda e: e.tensor_copy(out=out, in_=in_), reads, writes)

    def memset(self, eng, ap, val, writes):
        self.P.add(eng, lambda e: e.memset(ap, val), (), writes)

    def dma(self, q, out, in_, reads, writes, slow=False):
        if slow:
            self.P.add(q, lambda e: e.dma_start(out=out, in_=in_, allow_slow_non_contiguous=True), reads, writes, dma=True)
        else:
            self.P.add(q, lambda e: e.dma_start(out=out, in_=in_), reads, writes, dma=True)


def blk(i, n=512):
    return slice(i * n, (i + 1) * n)


def build_program():
    nc = bass.Bass("TRN2", target_bir_lowering=False)

    def din(name, shape, dt=F32):
        return nc.dram_tensor(name, shape, dt, kind="ExternalInput").ap()

    x = din("x", [NB, S, D])
    mem = din("mem", [NB, NMEM, D])
    pos = din("positions", [NB, S], I32)
    g_emb = din("g_emb", [D])
    b_emb = din("b_emb", [D])
    w_in = din("w_in", [D, D_IN])
    g_cq = din("g_cq", [256])
    g_ckv = din("g_ckv", [128])
    w_uq = din("w_uq", [256, 768])
    w_ukv = din("w_ukv", [128, 1024])
    w_mem_kv = din("w_mem_kv", [D, 1024])
    g_out_a = din("g_out_a", [1024])
    g_out_b = din("g_out_b", [512])
    g_out_m = din("g_out_m", [512])
    w_out = din("w_out", [2048, D])
    g_post = din("g_post", [D])
    b_post = din("b_post", [D])
    cst = din("cst", [128, 8])
    maskd = din("maskd", [128, MASK_W + 128])
    out = nc.dram_tensor("out", [NB, S, D], F32, kind="ExternalOutput").ap()
    hscr = nc.dram_tensor("hscr", [NB, S, D], F32, kind="Internal").ap()

    w_in_v = w_in.rearrange("(k p) c -> p k c", p=128)

    with ExitStack() as st:
        esems = {e: [st.enter_context(nc.semaphore(f"s_{e}_{i}")) for i in range(Prog.N_EPOCH_SEMS)]
                 for e in ENGS}
        dsems = [st.enter_context(nc.semaphore(f"d_{i}")) for i in range(Prog.N_DMA_SEMS)]
        psT = st.enter_context(nc.psum_tensor("psT", [128, 1024], BF16))
        ssqP = st.enter_context(nc.psum_tensor("ssqP", [128, 512], F32))
        W = [st.enter_context(nc.psum_tensor(f"W{i}", [128, 512], F32)) for i in range(6)]
        WN = [f"W{i}" for i in range(6)]

        B = Builder(nc)
        P = B.P

        hT = B.T("hT", [128, 8, S], BF16)
        yT = B.T("yT", [128, 16, S], BF16)
        ident = B.T("ident", [128, 128], BF16)
        ones_bf = B.T("ones_bf", [128, 128], BF16)
        onesf = B.T("onesf", [128, 128], F32)
        sel_o = B.T("sel_o", [128, 128], F32)
        mhalf = B.T("mhalf", [128, 512], F32)
        cst_t = B.T("cst", [128, 8], F32)
        gcq_t = B.T("gcq", [128, 2], F32)
        gckv_t = B.T("gckv", [128, 1], F32)
        goa_t = B.T("goa", [128, 8], F32)
        gob_t = B.T("gob", [128, 4], F32)
        gom_t = B.T("gom", [128, 4], F32)
        wst = [B.T(f"wst{i}", [128, 8, 512], BF16) for i in range(2)]

        B.memset("pool", ident[:], 0.0, ["ident"])
        P.add("pool", lambda e: e.affine_select(out=ident[:], in_=ident[:], compare_op=ALU.not_equal, fill=1.0,
                                                base=0, pattern=[[-1, 128]], channel_multiplier=1),
              ["ident"], ["ident"])
        B.memset("pool", ones_bf[:], 1.0, ["ones_bf"])
        B.memset("pool", onesf[:], 1.0, ["onesf"])
        B.memset("pool", sel_o[:, 0:64], 0.0, ["sel_o"])
        B.memset("pool", sel_o[:, 64:128], 1.0, ["sel_o"])
        B.memset("pool", mhalf[:], -0.5, ["mhalf"])
        B.dma("sp", cst_t[:], cst, [], ["cst"])
        B.dma("sp", gcq_t[:], g_cq.rearrange("(c p) -> p c", p=128), [], ["gvec"], slow=True)
        B.dma("sp", gckv_t[:], g_ckv.rearrange("(c p) -> p c", p=128), [], ["gvec"], slow=True)
        B.dma("sp", goa_t[:], g_out_a.rearrange("(c p) -> p c", p=128), [], ["gvec"], slow=True)
        B.dma("sp", gob_t[:], g_out_b.rearrange("(c p) -> p c", p=128), [], ["gvec"], slow=True)
        B.dma("sp", gom_t[:], g_out_m.rearrange("(c p) -> p c", p=128), [], ["gvec"], slow=True)
        P.barrier()

        base_mark = B.mark()

        def pipeline(T, stages):
            maxlag = max(l for l, _ in stages)
            stages = sorted(stages, key=lambda lf: -lf[0])
            for tau in range(T + maxlag):
                for lag, fn in stages:
                    t = tau - lag
                    if 0 <= t < T:
                        fn(t)

        def ln_alloc(nsm, nbig):
            return dict(st=[B.T("lst", [128, 2, 6], F32) for _ in range(nsm)],
                        mv=[B.T("lmv", [128, 2], F32) for _ in range(nsm)],
                        rs=[B.T("lrs", [128, 1], F32) for _ in range(nsm)],
                        nmr=[B.T("lnm", [128, 1], F32) for _ in range(nsm)],
                        xn=[B.T("lxn", [128, 1024], F32) for _ in range(nbig)],
                        t1=[B.T("lt1", [128, 1024], F32) for _ in range(nbig + 1)])

        def ln_stages(L, src_fn, gbc, gres, bbc, bres, lag0):
            nsm, nxn, nt1 = len(L["st"]), len(L["xn"]), len(L["t1"])

            def sB(t):
                k = t % nsm
                src, sres = src_fn(t)
                for hh in range(2):
                    P.add("dve", lambda e, hh=hh: e.bn_stats(out=L["st"][k][:, hh, :], in_=src[:, hh * 512:(hh + 1) * 512]),
                          [sres], [f"lst{k}"])
                P.add("dve", lambda e: e.bn_aggr(out=L["mv"][k][:], in_=L["st"][k][:].rearrange("p a b -> p (a b)")),
                      [f"lst{k}"], [f"lmv{k}"])
                B.ts("dve", L["rs"][k][:], L["mv"][k][:, 1:2], EPS, None, ALU.add, None, [f"lmv{k}"], [f"lrs{k}"])

            def sC(t):
                k = t % nsm
                B.tt("pool", L["rs"][k][:], L["rs"][k][:], mhalf[:, 0:1], ALU.pow, [f"lrs{k}", "mhalf"], [f"lrs{k}"])

            def sD(t):
                k = t % nsm
                B.stt(L["nmr"][k][:], L["mv"][k][:, 0:1], -1.0, L["rs"][k][:], ALU.mult, ALU.mult,
                      [f"lmv{k}", f"lrs{k}"], [f"lnm{k}"])

            def sE(t):
                k = t % nsm
                src, sres = src_fn(t)
                B.act(L["xn"][t % nxn][:], src, AF.Identity, [sres, f"lnm{k}", f"lrs{k}"], [f"lxn{t % nxn}"],
                      scale=L["rs"][k][:, 0:1], bias=L["nmr"][k][:, 0:1])

            def sF(t):
                B.tt("dve", L["t1"][t % nt1][:], L["xn"][t % nxn][:], gbc[:], ALU.mult, [f"lxn{t % nxn}", gres], [f"lt1_{t % nt1}"])

            def sG(t):
                B.tt("pool", L["t1"][t % nt1][:], L["t1"][t % nt1][:], bbc[:], ALU.add, [f"lt1_{t % nt1}", bres], [f"lt1_{t % nt1}"])

            return [(lag0, sB), (lag0, sC), (lag0 + 1, sD), (lag0 + 1, sE), (lag0 + 2, sF), (lag0 + 2, sG)]

        wcount = [0]

        def next_w():
            i = wcount[0] % 5
            wcount[0] += 1
            return W[i], WN[i]

        def proj_fm(lhs_fn, K, rhs_fn, nrows, reads):
            bank, bn = next_w()
            for k in range(K):
                B.mm(bank[0:nrows, :], lhs_fn(k), rhs_fn(k), k == 0, k == K - 1, reads, [bn])
            return bank, bn

        def rope_tables(b, fr_col, sgn_col, cosT, sinT, tag):
            m = B.mark()
            posb = B.T("posb", [128, S], I32)
            ang = B.T("ang", [128, S], F32)
            ki = B.T("ki", [128, S], I32)
            kf = B.T("kf", [128, S], F32)
            a = B.T("a", [128, S], F32)
            B.dma("sp", posb[:], pos[b].partition_broadcast(128), [], ["posb"])
            B.cp("dve", ang[:], posb[:], ["posb"], ["ang"])
            B.ts("dve", ang[:], ang[:], cst_t[:, fr_col:fr_col + 1], None, ALU.mult, None, ["ang", "cst"], ["ang"])
            for which in range(2):
                if which == 1:
                    B.ts("dve", ang[:], ang[:], float(np.pi / 2), None, ALU.add, None, ["ang"], ["ang"])
                B.ts("dve", ki[:], ang[:], float(1.0 / TWO_PI), None, ALU.mult, None, ["ang"], ["ki"])
                B.cp("dve", kf[:], ki[:], ["ki"], ["kf"])
                B.stt(a[:], kf[:], -C1, ang[:], ALU.mult, ALU.add, ["kf", "ang"], ["a"])
                B.stt(a[:], kf[:], -C2, a[:], ALU.mult, ALU.add, ["kf", "a"], ["a"])
                B.ts("dve", a[:], a[:], float(-np.pi), float(np.pi), ALU.max, ALU.min, ["a"], ["a"])
                if which == 0:
                    B.act(sinT[:], a[:], AF.Sin, ["a", "cst"], [f"sin{tag}"], scale=cst_t[:, sgn_col:sgn_col + 1])
                else:
                    B.act(cosT[:], a[:], AF.Sin, ["a"], [f"cos{tag}"])
            P.barrier()
            B.release(m)

        def attn_bufs():
            d = {}
            d["PT"] = [B.T("PT", [128, 512], BF16) for _ in range(4)]
            d["RD"] = [B.T("RD", [128, 512], F32) for _ in range(1)]
            d["BCS"] = [B.T("BCS", [128, 512], F32) for _ in range(2)]
            d["ON"] = [B.T("ON", [128, 512], F32) for _ in range(2)]
            for t in d["RD"]:
                B.memset("pool", t[:], 1.0, ["RD0"])
            d["step"] = 0
            d["defer"] = []
            d["bj"] = 0
            d["epi"] = 0
            return d

        def attention(ab, KT, QT, V, vcols, nkt, r0, r1, den, scale, maskfn, sg, sgres, gvec, ychunk, sqT, reads, o3=None, bulk=None):
            SB = (0, 1)
            OB = (2, 3)
            steps = []
            for qb in range(4):
                kts = [kt for kt in range(nkt) if maskfn is None or maskfn(kt, qb) is not None]
                rng = {}
                for kt in kts:
                    if maskfn is None:
                        rng[kt] = (0, 512)
                    else:
                        rng[kt] = (max(0, 128 * kt - 256 - 512 * qb), min(512, 128 * kt + 384 - 512 * qb))
                full = [kt for kt in kts if rng[kt] == (0, 512)]
                if maskfn is not None:
                    assert len(full) >= 2
                    kts = [full[0]] + [kt for kt in kts if kt not in (full[0], full[-1])] + [full[-1]]
                for j, kt in enumerate(kts):
                    steps.append((qb, kt, j == 0, j == len(kts) - 1, rng[kt][0], rng[kt][1]))
            base = ab["step"]

            def emit_S(i):
                qb, kt, _, _, c0, c1 = steps[i]
                s = SB[(base + i) % 2]
                if maskfn is None:
                    B.mm(W[s][:, :], KT(kt), QT(qb), True, True, reads, [WN[s]])
                else:
                    B.mm(W[s][:, c0:c1], KT(kt), QT(qb)[:, c0:c1], True, False, reads, [WN[s]])
                    B.mm(W[s][:, c0:c1], ident[:, :], maskfn(kt, qb)[:, c0:c1], False, True, ["ident", "maskT"], [WN[s]])

            def epilogue(qb, ob, cur):
                run_deferred(1 << 60)
                j = ab["epi"] % 2
                ab["epi"] += 1
                rd, bcs, on = ab["RD"][0], ab["BCS"][j], ab["ON"][j]
                if den[0] == "row":
                    dr = den[1]
                    den_ap = W[ob][dr:dr + 1, :]
                    den_res = WN[ob]
                else:
                    dr = 0
                    den_ap = W[4][0:1, :]
                    den_res = WN[4]
                if o3 is None:
                    P.add("dve", lambda e: e.reciprocal(out=rd[dr:dr + 1, :], in_=den_ap), [den_res], ["RD0"])
                else:
                    B.tt("dve", rd[dr:dr + 1, :], den_ap, o3[0][dr:dr + 1, blk(qb)], ALU.add, [den_res, o3[1]], ["RD0"])
                    P.add("dve", lambda e: e.reciprocal(out=rd[dr:dr + 1, :], in_=rd[dr:dr + 1, :]), ["RD0"], ["RD0"])

                def st1():
                    if r0 == 0 and r1 == 64:
                        B.mm(W[5][0:64, :], onesf[dr:dr + 1, 0:64], rd[dr:dr + 1, :], True, True, ["RD0", "onesf"], [WN[5]])
                    elif r0 == 64:
                        B.mm(W[5][0:128, :], sel_o[dr:dr + 1, 0:128], rd[dr:dr + 1, :], True, True, ["RD0", "sel_o"], [WN[5]])
                    else:
                        B.mm(W[5][0:128, :], onesf[dr:dr + 1, 0:128], rd[dr:dr + 1, :], True, True, ["RD0", "onesf"], [WN[5]])

                def st2():
                    B.cp("act", bcs[r0:r1, :], W[5][r0:r1, :], [WN[5]], [f"BCS{j}"])

                def st3():
                    if o3 is None:
                        B.tt("dve", on[r0:r1, :], W[ob][r0:r1, :], bcs[r0:r1, :], ALU.mult, [WN[ob], f"BCS{j}"], [f"ON{j}"])
                    else:
                        B.tt("dve", on[r0:r1, :], W[ob][r0:r1, :], o3[0][r0:r1, blk(qb)], ALU.add, [WN[ob], o3[1]], [f"ON{j}"])
                        B.tt("dve", on[r0:r1, :], on[r0:r1, :], bcs[r0:r1, :], ALU.mult, [f"ON{j}", f"BCS{j}"], [f"ON{j}"])

                def st4():
                    B.act(sqT[r0:r1, blk(qb)], on[r0:r1, :], AF.Square, [f"ON{j}"], ["sqT"])
                    B.stt(yT[r0:r1, ychunk, blk(qb)], on[r0:r1, :], gvec, sg[r0:r1, blk(qb)], ALU.mult, ALU.mult,
                          [f"ON{j}", "gvec", sgres], [f"yT{ychunk}"])

                for dly, fn in ((6, st1), (8, st2), (9, st3), (10, st4)):
                    ab["defer"].append((cur + dly, fn))

            def run_deferred(upto):
                q = ab["defer"]
                while q and q[0][0] <= upto:
                    q.pop(0)[1]()

            def emit_rest(i):
                qb, kt, first, last, c0, c1 = steps[i]
                s = SB[(base + i) % 2]
                pi = (base + i) % 4
                pt = ab["PT"][pi]
                ob = OB[(ab["epi"]) % 2]
                B.act(pt[:, c0:c1], W[s][:, c0:c1], AF.Exp, [WN[s]], [f"PT{pi}"], scale=scale)
                B.mm(W[ob][0:vcols, c0:c1], V(kt), pt[:, c0:c1], first, last, [f"PT{pi}"] + reads, [WN[ob]])
                if den[0] == "sep":
                    B.mm(W[4][0:1, :], ones_bf[:, 0:1], pt[:], first, last, [f"PT{pi}", "ones_bf"], [WN[4]])
                if last:
                    if bulk is None:
                        epilogue(qb, ob, base + i)
                    elif den[0] == "sep":
                        B.cp("dve", bulk[0][:, blk(qb)], W[ob][:, :], [WN[ob]], [bulk[1]])
                        B.cp("act", bulk[2][0:1, blk(qb)], W[4][0:1, :], [WN[4]], [bulk[3]])
                        ab["epi"] += 1
                    else:
                        brows = slice(0, 128) if r0 == 64 else slice(0, 65)
                        B.tt("dve", bulk[0][brows, blk(qb)], W[ob][brows, :], bulk[0][brows, blk(qb)], ALU.add,
                             [WN[ob], bulk[1]], [bulk[1]])
                        ab["epi"] += 1

            n = len(steps)
            emit_S(0)
            for i in range(n):
                if i + 1 < n:
                    emit_S(i + 1)
                emit_rest(i)
                run_deferred(base + i)
            if bulk is None:
                run_deferred(1 << 60)
            ab["step"] = base + n

        def run_deferred_ab(ab, upto, maxn=1 << 30):
            q = ab["defer"]
            n = 0
            while q and q[0][0] <= upto and n < maxn:
                q.pop(0)[1]()
                n += 1

        def bulk_epilogue(ab, acc, accres, r0, r1, dr, sg, sgres, gvec, ychunk, sqT, start, dtile=None, dres=None):
            items = []
            if dtile is None:
                dtile, dres = acc, accres
            for qb in range(4):
                t0 = start + 6 * qb
                j = ab["bj"] % 2
                ab["bj"] += 1
                bcs, on = ab["BCS"][j], ab["ON"][j]

                def f_rec(qb=qb):
                    P.add("dve", lambda e: e.reciprocal(out=dtile[dr:dr + 1, blk(qb)], in_=dtile[dr:dr + 1, blk(qb)]),
                          [dres], [dres])

                def f_bc(qb=qb):
                    if r0 == 0 and r1 == 64:
                        B.mm(W[5][0:64, :], onesf[dr:dr + 1, 0:64], dtile[dr:dr + 1, blk(qb)], True, True, [dres, "onesf"], [WN[5]])
                    elif r0 == 64:
                        B.mm(W[5][0:128, :], sel_o[dr:dr + 1, 0:128], dtile[dr:dr + 1, blk(qb)], True, True, [dres, "sel_o"], [WN[5]])
                    else:
                        B.mm(W[5][0:128, :], onesf[dr:dr + 1, 0:128], dtile[dr:dr + 1, blk(qb)], True, True, [dres, "onesf"], [WN[5]])

                def f_cp(bcs=bcs, j=j):
                    B.cp("act", bcs[r0:r1, :], W[5][r0:r1, :], [WN[5]], [f"BCS{j}"])

                def f_mul(qb=qb, bcs=bcs, on=on, j=j):
                    B.tt("dve", on[r0:r1, :], acc[r0:r1, blk(qb)], bcs[r0:r1, :], ALU.mult, [accres, f"BCS{j}"], [f"ON{j}"])

                def f_fin(qb=qb, on=on, j=j):
                    B.act(sqT[r0:r1, blk(qb)], on[r0:r1, :], AF.Square, [f"ON{j}"], ["sqT"])
                    B.stt(yT[r0:r1, ychunk, blk(qb)], on[r0:r1, :], gvec, sg[r0:r1, blk(qb)], ALU.mult, ALU.mult,
                          [f"ON{j}", "gvec", sgres], [f"yT{ychunk}"])

                items += [(t0, f_rec), (t0 + 6, f_bc), (t0 + 8, f_cp), (t0 + 9, f_mul), (t0 + 10, f_fin)]
            ab["defer"] = sorted(ab["defer"] + items, key=lambda x: x[0])

        def attention_p3(ab, KTx, QTc_, V3, odd, mask3, O3s, reads):
            SB = (0, 1)
            OB = (2, 3)
            rows = slice(0, 128) if odd else slice(0, 65)
            vsl = slice(64, 192) if odd else slice(0, 128)
            base = ab["step"]

            def csl(r):
                return slice(r, r + 16 * 127 + 1, 16)

            def emit_S(r):
                s = SB[(base + r) % 2]
                B.mm(W[s][:, 0:128], KTx[:, csl(r)], QTc_[:, csl(r)], True, False, reads, [WN[s]])
                B.mm(W[s][:, 0:128], ident[:, :], mask3, False, True, ["ident", "maskT"], [WN[s]])

            def emit_rest(r):
                s = SB[(base + r) % 2]
                pi = (base + r) % 4
                pt = ab["PT"][pi]
                ob = OB[ab["epi"] % 2]
                B.act(pt[:, 0:128], W[s][:, 0:128], AF.Exp, [WN[s]], [f"PT{pi}"], scale=0.125)
                q = r % 4
                B.mm(W[ob][0:128, q * 128:(q + 1) * 128], V3[:, r, vsl], pt[:, 0:128], True, True, [f"PT{pi}", "V3"], [WN[ob]])
                if q == 3:
                    b3 = r // 4
                    dst = O3s[0][:].rearrange("p (j r) -> p r j", r=16)[rows, 4 * b3:4 * b3 + 4, :]
                    src = W[ob][rows, :].rearrange("p (q j) -> p q j", q=4)
                    B.cp("dve", dst, src, [WN[ob]], [O3s[1]])
                    ab["epi"] += 1

            emit_S(0)
            for r in range(16):
                if r + 1 < 16:
                    emit_S(r + 1)
                emit_rest(r)
                run_deferred_ab(ab, base + r)
            ab["step"] = base + 16

        def ssq_mms(sqT, chunk):
            for tt_ in range(16):
                col = tt_ * 16 + chunk
                B.mm(ssqP[:, col:col + 1], sqT[:, tt_ * 128:(tt_ + 1) * 128], ones_bf[:, 0:1], True, True,
                     ["sqT", "ones_bf"], ["ssqP"])

        def load_w_in(buf, col0, ncols, dst0=0):
            B.dma("pool", wst[buf][:, :, dst0:dst0 + ncols], w_in_v[:, :, col0:col0 + ncols], [], [f"wst{buf}"])

        for b in range(NB):
            m = B.mark()
            gbc = B.T("gbc", [128, 1024], F32)
            bbc = B.T("bbc", [128, 1024], F32)
            xt = [B.T("xt", [128, 1024], F32) for _ in range(4)]
            hb = [B.T("hb", [128, 1024], BF16) for _ in range(3)]
            ha = [B.T("ha", [128, 1024], F32) for _ in range(3)]
            L = ln_alloc(4, 2)
            B.dma("sp", gbc[:], g_emb.partition_broadcast(128), [], ["gbc"])
            B.dma("sp", bbc[:], b_emb.partition_broadcast(128), [], ["bbc"])

            def tsl_(t):
                return slice(t * 128, (t + 1) * 128)

            def p1_load(t):
                B.dma("sp", xt[t % 4][:], x[b, tsl_(t), :], [], [f"xt{t % 4}"])

            def p1_H(t):
                hfr = f"lt1_{t % 3}"
                hf_ = L["t1"][t % 3]
                B.act(hb[t % 3][:], hf_[:], AF.Copy, [hfr], [f"hb{t % 3}"])
                B.act(ha[t % 3][:], hf_[:], AF.Identity, [hfr], [f"ha{t % 3}"], scale=ALPHA)
                B.dma("sp", hscr[b, tsl_(t), :], ha[t % 3][:], [f"ha{t % 3}"], [f"hscr{t % 3}"])
                for k in range(8):
                    B.tr(psT[:, k * 128:(k + 1) * 128], hb[t % 3][:, k * 128:(k + 1) * 128], ident[:],
                         [f"hb{t % 3}", "ident"], ["psT"])

            def p1_J(t):
                B.cp("act", hT[:, :, tsl_(t)], psT[:].rearrange("p (k t) -> p k t", k=8), ["psT"], ["hT"])

            pipeline(16, [(0, p1_load)] + ln_stages(L, lambda t: (xt[t % 4][:], f"xt{t % 4}"), gbc, "gbc", bbc, "bbc", 1)
                     + [(4, p1_H), (5, p1_J)])
            P.barrier()
            B.release(m)

            m = B.mark()
            memb = B.T("memb", [128, 2, 1024], BF16)
            memT = B.T("memT", [128, 8, NMEM], BF16)
            mkT = B.T("mkT", [128, 4, NMEM], BF16)
            mv = B.T("mv", [128, 2, 512], BF16)
            qTm = [B.T("qTm", [128, S], BF16) for _ in range(2)]
            sgm = [B.T("sgm", [128, S], BF16) for _ in range(2)]
            accm = [B.T("accm", [128, S], F32) for _ in range(2)]
            denm = [B.T("denm", [128, S], F32) for _ in range(2)]
            sqT = B.T("sqT", [128, S], BF16)
            ab = attn_bufs()
            wmk = w_mem_kv.rearrange("(k p) c -> p k c", p=128)
            B.dma("pool", memb[:], mem[b].rearrange("(t p) d -> p t d", p=128), [], ["memb"])
            B.dma("pool", wst[0][:], wmk[:, :, 0:512], [], ["wst0"])
            B.dma("pool", wst[1][:], wmk[:, :, 512:1024], [], ["wst1"])
            for t in range(2):
                for k in range(8):
                    B.tr(psT[:, k * 128:(k + 1) * 128], memb[:, t, k * 128:(k + 1) * 128], ident[:], ["memb", "ident"], ["psT"])
                B.cp("act", memT[:, :, t * 128:(t + 1) * 128], psT[:].rearrange("p (k t) -> p k t", k=8), ["psT"], ["memT"])
            for h in range(4):
                bank, bn = next_w()
                for k in range(8):
                    B.mm(bank[:, 0:NMEM], wst[0][:, k, h * 128:(h + 1) * 128], memT[:, k, :], k == 0, k == 7,
                         ["wst0", "memT"], [bn])
                B.cp("dve", mkT[:, h, :], bank[:, 0:NMEM], [bn], ["mkT"])
            for t in range(2):
                bank, bn = next_w()
                for k in range(8):
                    B.mm(bank[:, :], memT[:, k, t * 128:(t + 1) * 128], wst[1][:, k, :], k == 0, k == 7, ["wst1", "memT"], [bn])
                B.cp("dve", mv[:, t, :], bank[:, :], [bn], ["mv"])
            load_w_in(0, 5024, 512)
            load_w_in(1, 5536, 512)
            for h in range(4):
                sl = h % 2
                for qb in range(4):
                    bank, bn = proj_fm(lambda k: wst[0][:, k, h * 128:(h + 1) * 128], 8, lambda k: hT[:, k, blk(qb)], 128,
                                       ["wst0", "hT"])
                    B.cp("dve", qTm[sl][:, blk(qb)], bank[:, :], [bn], [f"qTm{sl}"])
                    run_deferred_ab(ab, 1 << 60, maxn=3)
                    bank, bn = proj_fm(lambda k: wst[1][:, k, h * 128:(h + 1) * 128], 8, lambda k: hT[:, k, blk(qb)], 128,
                                       ["wst1", "hT"])
                    B.act(sgm[sl][:, blk(qb)], bank[:, :], AF.Silu, [bn], [f"sgm{sl}"])
                    run_deferred_ab(ab, 1 << 60, maxn=3)
                run_deferred_ab(ab, 1 << 60)
                if h > 0:
                    ssq_mms(sqT, 12 + h - 1)
                attention(ab, KT=lambda kt: mkT[:, h, kt * 128:(kt + 1) * 128], QT=lambda qb: qTm[sl][:, blk(qb)],
                          V=lambda kt: mv[:, kt, h * 128:(h + 1) * 128], vcols=128, nkt=2, r0=0, r1=128, den=("sep",),
                          scale=128.0 ** -0.5, maskfn=None, sg=sgm[sl], sgres=f"sgm{sl}", gvec=gom_t[:, h:h + 1],
                          ychunk=12 + h, sqT=sqT, reads=["mkT", "mv", f"qTm{sl}"],
                          bulk=(accm[sl], f"accm{sl}", denm[sl], f"denm{sl}"))
                bulk_epilogue(ab, accm[sl], f"accm{sl}", 0, 128, 0, sgm[sl], f"sgm{sl}", gom_t[:, h:h + 1], 12 + h, sqT,
                              ab["step"], dtile=denm[sl], dres=f"denm{sl}")
            run_deferred_ab(ab, 1 << 60)
            ssq_mms(sqT, 15)
            P.barrier()
            B.release(m)

            m = B.mark()
            cosB = B.T("cosB", [128, S], F32)
            sinB = B.T("sinB", [128, S], F32)
            rope_tables(b, 2, 3, cosB, sinB, "B")
            wuq = B.T("wuq", [128, 2, 768], BF16)
            wuq_sw = B.T("wuq_sw", [128, 2, 768], BF16)
            wukv = B.T("wukv", [128, 1024], BF16)
            wkr_sw = B.T("wkr_sw", [128, 8, 96], BF16)
            cqn = B.T("cqn", [128, 2, S], BF16)
            ckvn = B.T("ckvn", [128, S], BF16)
            kr = B.T("kr", [128, S], BF16)
            V2B = B.T("V2B", [128, 16, 192], BF16)
            sqt = [B.T("sqt", [128, 512], BF16) for _ in range(2)]
            R = [B.T("R", [128, 512], F32) for _ in range(2)]
            rt1 = B.T("rt1", [128, 512], F32)
            rt2 = B.T("rt2", [128, 512], F32)
            QTh = B.T("QTh", [128, S], BF16)
            KTh = B.T("KTh", [128, S], BF16)
            sgb = B.T("sgb", [128, S], BF16)
            sqT = B.T("sqT", [128, S], BF16)
            ab = attn_bufs()
            B.dma("pool", wuq[:], w_uq.rearrange("(k p) c -> p k c", p=128), [], ["wuq"])
            B.dma("pool", wukv[:], w_ukv, [], ["wukv"])
            load_w_in(0, 4096, 512)
            load_w_in(1, 4512, 512)
            B.memset("pool", wuq_sw[:], 0.0, ["wuq_sw"])
            wuq4 = wuq[:].rearrange("p k (h t) -> p k h t", t=96)
            wsw4 = wuq_sw[:].rearrange("p k (h t) -> p k h t", t=96)
            for kc in range(2):
                B.cp("pool", wsw4[:, kc, :, 64:80], wuq4[:, kc, :, 80:96], ["wuq", "wuq_sw"], ["wuq_sw"])
                B.cp("pool", wsw4[:, kc, :, 80:96], wuq4[:, kc, :, 64:80], ["wuq", "wuq_sw"], ["wuq_sw"])
            B.memset("pool", wkr_sw[:], 0.0, ["wkr_sw"])
            B.cp("pool", wkr_sw[:, :, 64:80], wst[0][:, :, 400:416], ["wst0", "wkr_sw"], ["wkr_sw"])
            B.cp("pool", wkr_sw[:, :, 80:96], wst[0][:, :, 384:400], ["wst0", "wkr_sw"], ["wkr_sw"])
            B.memset("pool", V2B[:, :, 64:65], 1.0, ["V2B"])
            B.memset("pool", V2B[:, :, 65:128], 0.0, ["V2B"])
            rc = 0
            for qb in range(4):
                hrhs = lambda k: hT[:, k, blk(qb)]
                cb = []
                for c in range(2):
                    cb.append(proj_fm(lambda k: wst[0][:, k, c * 128:(c + 1) * 128], 8, hrhs, 128, ["wst0", "hT"]))
                for c in range(2):
                    B.act(sqt[c][:], cb[c][0][:, :], AF.Square, [cb[c][1]], [f"sqt{c}"])
                sbank, sbn = next_w()
                for c in range(2):
                    B.mm(sbank[:, :], ones_bf[:, :], sqt[c][:], c == 0, c == 1, [f"sqt{c}", "ones_bf"], [sbn])
                j = rc % 2
                rc += 1
                B.ts("dve", R[j][:], sbank[:, :], 1.0 / 256, EPS, ALU.mult, ALU.add, [sbn], [f"R{j}"])
                B.act(R[j][:], R[j][:], AF.Sqrt, [f"R{j}"], [f"R{j}"])
                P.add("dve", lambda e, j=j: e.reciprocal(out=R[j][:], in_=R[j][:]), [f"R{j}"], [f"R{j}"])
                for c in range(2):
                    B.stt(cqn[:, c, blk(qb)], cb[c][0][:, :], gcq_t[:, c:c + 1], R[j][:], ALU.mult, ALU.mult,
                          [cb[c][1], "gvec", f"R{j}"], ["cqn"])
                kb, kbn = proj_fm(lambda k: wst[0][:, k, 256:384], 8, hrhs, 128, ["wst0", "hT"])
                B.act(sqt[0][:], kb[:, :], AF.Square, [kbn], ["sqt0"])
                sbank, sbn = next_w()
                B.mm(sbank[:, :], ones_bf[:, :], sqt[0][:], True, True, ["sqt0", "ones_bf"], [sbn])
                j = rc % 2
                rc += 1
                B.ts("dve", R[j][:], sbank[:, :], 1.0 / 128, EPS, ALU.mult, ALU.add, [sbn], [f"R{j}"])
                B.act(R[j][:], R[j][:], AF.Sqrt, [f"R{j}"], [f"R{j}"])
                P.add("dve", lambda e, j=j: e.reciprocal(out=R[j][:], in_=R[j][:]), [f"R{j}"], [f"R{j}"])
                B.stt(ckvn[:, blk(qb)], kb[:, :], gckv_t[:, 0:1], R[j][:], ALU.mult, ALU.mult, [kbn, "gvec", f"R{j}"], ["ckvn"])
                pa, pan = proj_fm(lambda k: wst[0][:, k, 320:416], 8, hrhs, 96, ["wst0", "hT"])
                pb, pbn = proj_fm(lambda k: wkr_sw[:, k, 0:96], 8, hrhs, 96, ["wkr_sw", "hT"])
                B.tt("dve", rt1[64:96, :], pa[64:96, :], cosB[64:96, blk(qb)], ALU.mult, [pan, "cosB"], ["rt1"])
                B.tt("dve", rt2[64:96, :], pb[64:96, :], sinB[64:96, blk(qb)], ALU.mult, [pbn, "sinB"], ["rt2"])
                B.tt("pool", kr[64:96, blk(qb)], rt1[64:96, :], rt2[64:96, :], ALU.add, ["rt1", "rt2"], ["kr"])
            wv3 = wukv[:].rearrange("p (h t) -> p h t", t=128)
            for h in range(8):
                pr = h // 2
                odd = h % 2
                if not odd:
                    for qb in range(4):
                        bank, bn = proj_fm(lambda k: wst[1][:, k, pr * 128:(pr + 1) * 128], 8, lambda k: hT[:, k, blk(qb)], 128,
                                           ["wst1", "hT"])
                        B.act(sgb[:, blk(qb)], bank[:, :], AF.Silu, [bn], ["sgb"])
                    for t4 in range(4):
                        bank, bn = next_w()
                        for tq in range(4):
                            tt_ = t4 * 4 + tq
                            B.mm(bank[:, tq * 128:(tq + 1) * 128], ckvn[:, tt_ * 128:(tt_ + 1) * 128],
                                 wv3[:, 2 * pr:2 * pr + 2, 64:128], True, True, ["ckvn", "wukv"], [bn])
                        bv = bank[:, :].rearrange("p (q e t) -> p q e t", q=4, e=2)
                        B.cp("act", V2B[:, t4 * 4:(t4 + 1) * 4, 0:64], bv[:, :, 0, :], [bn, "V2B"], ["V2B"])
                        B.cp("dve", V2B[:, t4 * 4:(t4 + 1) * 4, 128:192], bv[:, :, 1, :], [bn, "V2B"], ["V2B"])
                for qb in range(4):
                    pa, pan = proj_fm(lambda k: wuq[:, k, h * 96:(h + 1) * 96], 2, lambda k: cqn[:, k, blk(qb)], 96, ["wuq", "cqn"])
                    pb, pbn = proj_fm(lambda k: wuq_sw[:, k, h * 96:(h + 1) * 96], 2, lambda k: cqn[:, k, blk(qb)], 96,
                                      ["wuq_sw", "cqn"])
                    B.cp("act", QTh[0:64, blk(qb)], pa[0:64, :], [pan], ["QTh"])
                    B.tt("dve", rt1[64:96, :], pa[64:96, :], cosB[64:96, blk(qb)], ALU.mult, [pan, "cosB"], ["rt1"])
                    B.tt("dve", rt2[64:96, :], pb[64:96, :], sinB[64:96, blk(qb)], ALU.mult, [pbn, "sinB"], ["rt2"])
                    B.tt("pool", QTh[64:96, blk(qb)], rt1[64:96, :], rt2[64:96, :], ALU.add, ["rt1", "rt2", "QTh"], ["QTh"])
                    pc, pcn = proj_fm(lambda k: wukv[:, h * 128:h * 128 + 64], 1, lambda k: ckvn[:, blk(qb)], 64, ["wukv", "ckvn"])
                    B.cp("act", KTh[0:64, blk(qb)], pc[0:64, :], [pcn], ["KTh"])
                B.cp("pool", KTh[64:96, :], kr[64:96, :], ["kr", "KTh"], ["KTh"])
                if not odd:
                    V = lambda kt: V2B[:, kt, 0:128]
                    vcols, r0, r1, den = 128, 0, 64, ("row", 64)
                else:
                    V = lambda kt: V2B[:, kt, 64:192]
                    vcols, r0, r1, den = 128, 64, 128, ("row", 0)
                attention(ab, KT=lambda kt: KTh[0:96, kt * 128:(kt + 1) * 128], QT=lambda qb: QTh[0:96, blk(qb)],
                          V=V, vcols=vcols, nkt=16, r0=r0, r1=r1, den=den, scale=96.0 ** -0.5, maskfn=None,
                          sg=sgb, sgres="sgb", gvec=gob_t[r0:r1, pr:pr + 1], ychunk=8 + pr, sqT=sqT,
                          reads=["V2B", "QTh", "KTh"])
                if odd:
                    ssq_mms(sqT, 8 + pr)
            P.barrier()
            B.release(m)

            m = B.mark()
            cosA = B.T("cosA", [128, S], F32)
            sinA = B.T("sinA", [128, S], F32)
            rope_tables(b, 0, 1, cosA, sinA, "A")
            maskT = B.T("maskT", [128, MASK_W + 128], BF16)
            V3 = B.T("V3", [128, 16, 192], BF16)
            O3 = [B.T("O3s", [128, S], F32) for _ in range(2)]
            wswq = B.T("wswq", [128, 8, 128], BF16)
            wswk = B.T("wswk", [128, 8, 128], BF16)
            QTc = B.T("QTc", [128, S], BF16)
            KTc = B.T("KTc", [128, S], BF16)
            KTo = B.T("KTo", [128, S], BF16)
            V2A = B.T("V2A", [128, 16, 192], BF16)
            sga = [B.T("sga", [128, S], BF16) for _ in range(2)]
            rt1 = B.T("rt1", [128, 512], F32)
            rt2 = B.T("rt2", [128, 512], F32)
            sqT = B.T("sqT", [128, S], BF16)
            ab = dict(PT=[B.T("PT", [128, 512], BF16) for _ in range(4)],
                      BCS=[B.T("BCS", [128, 512], F32) for _ in range(2)],
                      ON=[B.T("ON", [128, 512], F32) for _ in range(2)],
                      step=0, epi=0, defer=[], bj=0)
            B.dma("pool", maskT[:], maskd, [], ["maskT"])
            B.memset("pool", KTc[64:128, :], 0.0, ["KTc"])
            B.memset("pool", KTo[0:64, :], 0.0, ["KTc"])
            B.memset("pool", wswq[:], 0.0, ["wswq"])
            B.memset("pool", wswk[:], 0.0, ["wswk"])
            B.memset("pool", V2A[:, :, 64:65], 1.0, ["V2A"])
            B.memset("pool", V2A[:, :, 65:128], 0.0, ["V2A"])
            B.memset("pool", V3[:, :, 64:65], 1.0, ["V3"])
            B.memset("pool", V3[:, :, 65:128], 0.0, ["V3"])

            def maskfn(kt, qb):
                dmin = 128 * kt - 512 * qb - 511
                dmax = 128 * kt + 127 - 512 * qb
                if dmin > 256 or dmax < -256:
                    return None
                off = MASK_X0 - 128 * kt + 512 * qb
                assert 0 <= off and off + 512 <= MASK_W
                return maskT[:, off:off + 512]

            for c in range(8):
                sl = c % 2
                wb = wst[sl]
                sgc = sga[sl]
                sgn = f"sga{sl}"
                for i in range(4):
                    load_w_in(sl, 1024 * i + c * 128, 128, dst0=128 * i)
                wq4 = wb[:, :, 0:128].rearrange("p k (e t) -> p k e t", e=2)
                wk4 = wb[:, :, 128:256].rearrange("p k (e t) -> p k e t", e=2)
                sq4 = wswq[:].rearrange("p k (e t) -> p k e t", e=2)
                sk4 = wswk[:].rearrange("p k (e t) -> p k e t", e=2)
                B.cp("pool", sq4[:, :, :, 0:8], wq4[:, :, :, 8:16], [f"wst{sl}", "wswq"], ["wswq"])
                B.cp("pool", sq4[:, :, :, 8:16], wq4[:, :, :, 0:8], [f"wst{sl}", "wswq"], ["wswq"])
                B.cp("pool", sk4[:, :, :, 0:8], wk4[:, :, :, 8:16], [f"wst{sl}", "wswk"], ["wswk"])
                B.cp("pool", sk4[:, :, :, 8:16], wk4[:, :, :, 0:8], [f"wst{sl}", "wswk"], ["wswk"])
                for qb in range(4):
                    hrhs = lambda k: hT[:, k, blk(qb)]
                    for (dst, dname, c0, wsw, wswn) in ((QTc, "QTc", 0, wswq, "wswq"), (KTc, "KTc", 128, wswk, "wswk")):
                        pa, pan = proj_fm(lambda k: wb[:, k, c0:c0 + 128], 8, hrhs, 128, [f"wst{sl}", "hT"])
                        pb, pbn = proj_fm(lambda k: wsw[:, k, :], 8, hrhs, 128, [wswn, "hT"])
                        B.tt("dve", rt1[:], pa[:, :], cosA[:, blk(qb)], ALU.mult, [pan, "cosA"], ["rt1"])
                        B.tt("dve", rt2[:], pb[:, :], sinA[:, blk(qb)], ALU.mult, [pbn, "sinA"], ["rt2"])
                        if dname == "QTc":
                            B.tt("pool", dst[:, blk(qb)], rt1[:], rt2[:], ALU.add, ["rt1", "rt2"], [dname])
                        else:
                            B.tt("pool", KTc[0:64, blk(qb)], rt1[0:64, :], rt2[0:64, :], ALU.add, ["rt1", "rt2"], [dname])
                            B.tt("pool", KTo[64:128, blk(qb)], rt1[64:128, :], rt2[64:128, :], ALU.add, ["rt1", "rt2"], [dname])
                        run_deferred_ab(ab, 1 << 60, maxn=3)
                    bank, bn = proj_fm(lambda k: wb[:, k, 384:512], 8, hrhs, 128, [f"wst{sl}", "hT"])
                    B.act(sgc[:, blk(qb)], bank[:, :], AF.Silu, [bn], [sgn])
                for t4 in range(4):
                    bank, bn = next_w()
                    for tq in range(4):
                        tt_ = t4 * 4 + tq
                        for k in range(8):
                            B.mm(bank[:, tq * 128:(tq + 1) * 128], hT[:, k, tt_ * 128:(tt_ + 1) * 128], wb[:, k, 256:384],
                                 k == 0, k == 7, [f"wst{sl}", "hT"], [bn])
                    bv = bank[:, :].rearrange("p (q e t) -> p q e t", q=4, e=2)
                    B.cp("act", V2A[:, t4 * 4:(t4 + 1) * 4, 0:64], bv[:, :, 0, :], [bn, "V2A"], ["V2A"])
                    B.cp("dve", V2A[:, t4 * 4:(t4 + 1) * 4, 128:192], bv[:, :, 1, :], [bn, "V2A"], ["V2A"])
                for r4 in range(4):
                    bank, bn = next_w()
                    for tq in range(4):
                        r = r4 * 4 + tq
                        for k in range(8):
                            B.mm(bank[:, tq * 128:(tq + 1) * 128], hT[:, k, r:r + 16 * 127 + 1:16], wb[:, k, 256:384],
                                 k == 0, k == 7, [f"wst{sl}", "hT"], [bn])
                    bv = bank[:, :].rearrange("p (q e t) -> p q e t", q=4, e=2)
                    B.cp("act", V3[:, r4 * 4:(r4 + 1) * 4, 0:64], bv[:, :, 0, :], [bn, "V3"], ["V3"])
                    B.cp("dve", V3[:, r4 * 4:(r4 + 1) * 4, 128:192], bv[:, :, 1, :], [bn, "V3"], ["V3"])
                run_deferred_ab(ab, 1 << 60)
                if c > 0:
                    ssq_mms(sqT, c - 1)
                for odd in range(2):
                    if not odd:
                        V = lambda kt: V2A[:, kt, 0:128]
                        vcols, r0, r1, den = 128, 0, 64, ("row", 64)
                    else:
                        V = lambda kt: V2A[:, kt, 64:192]
                        vcols, r0, r1, den = 128, 64, 128, ("row", 0)
                    KTx = KTo if odd else KTc
                    acc, accres = O3[odd], f"O3s{odd}"
                    attention_p3(ab, KTx, QTc, V3, odd, maskT[:, MASK_W:MASK_W + 128], (acc, accres), ["QTc", "KTc"])
                    attention(ab, KT=lambda kt: KTx[:, kt * 128:(kt + 1) * 128], QT=lambda qb: QTc[:, blk(qb)],
                              V=V, vcols=vcols, nkt=16, r0=r0, r1=r1, den=den, scale=0.125, maskfn=maskfn,
                              sg=sgc, sgres=sgn, gvec=goa_t[r0:r1, c:c + 1], ychunk=c, sqT=sqT,
                              reads=["V2A", "QTc", "KTc"], bulk=(acc, accres))
                    bulk_epilogue(ab, acc, accres, r0, r1, den[1], sgc, sgn, goa_t[r0:r1, c:c + 1], c, sqT,
                                  ab["step"] + (17 if not odd else 0))
            run_deferred_ab(ab, 1 << 60)
            ssq_mms(sqT, 7)
            P.barrier()
            B.release(m)

            m = B.mark()
            wo = B.T("wo", [128, 16, D], BF16)
            gpc = B.T("gpc", [128, 1024], F32)
            bpc = B.T("bpc", [128, 1024], F32)
            hf = [B.T("hf", [128, 1024], F32) for _ in range(3)]
            acc = [B.T("acc", [128, 1024], F32) for _ in range(3)]
            rg = [B.T("rg", [128, 4], F32) for _ in range(3)]
            L = ln_alloc(4, 2)
            wo_v = w_out.rearrange("(k p) c -> p k c", p=128)
            for q4 in range(4):
                B.dma("pool", wo[:, q4 * 4:(q4 + 1) * 4, :], wo_v[:, q4 * 4:(q4 + 1) * 4, :], [], ["wo"])
            B.dma("sp", gpc[:], g_post.partition_broadcast(128), [], ["gpc"])
            B.dma("sp", bpc[:], b_post.partition_broadcast(128), [], ["bpc"])
            groups = ((0, 8, 1024.0), (8, 12, 512.0), (12, 16, 512.0))

            def tsl_(t):
                return slice(t * 128, (t + 1) * 128)

            def f_load(t):
                B.dma("sp", hf[t % 3][:], hscr[b, tsl_(t), :], [], [f"hf{t % 3}"])

            def f_mm(t):
                for n in range(2):
                    for gi, (c0, c1, width) in enumerate(groups):
                        bi = n * 3 + gi
                        for c in range(c0, c1):
                            B.mm(W[bi][:, :], yT[:, c, tsl_(t)], wo[:, c, blk(n)], c == c0, c == c1 - 1, ["yT", "wo"],
                                 [f"{WN[bi]}"])

            def f_rg(t):
                k = t % 3
                for gi, (c0, c1, width) in enumerate(groups):
                    P.add("dve", lambda e, gi=gi, c0=c0, c1=c1: e.reduce_sum(
                        out=rg[k][:, gi:gi + 1], in_=ssqP[:, t * 16 + c0:t * 16 + c1], axis=mybir.AxisListType.X),
                        ["ssqP"], [f"rg{k}"])
                    B.ts("dve", rg[k][:, gi:gi + 1], rg[k][:, gi:gi + 1], 1.0 / width, EPS, ALU.mult, ALU.add,
                         [f"rg{k}"], [f"rg{k}"])
                B.tt("pool", rg[k][:, 0:3], rg[k][:, 0:3], mhalf[:, 0:3], ALU.pow, [f"rg{k}", "mhalf"], [f"rg{k}"])

            def f_stt(t):
                k = t % 3
                for n in range(2):
                    for gi in range(3):
                        bi = n * 3 + gi
                        src = hf[k][:, blk(n)] if gi == 0 else acc[k][:, blk(n)]
                        srcn = f"hf{k}" if gi == 0 else f"acc{k}"
                        B.stt(acc[k][:, blk(n)], W[bi][:, :], rg[k][:, gi:gi + 1], src, ALU.mult, ALU.add,
                              [WN[bi], f"rg{k}", srcn], [f"acc{k}"])

            def f_store(t):
                B.dma("sp", out[b, tsl_(t), :], L["t1"][t % 3][:], [f"lt1_{t % 3}"], [f"out{t % 3}"])

            pipeline(16, [(0, f_load), (0, f_mm), (0, f_rg), (1, f_stt)]
                     + ln_stages(L, lambda t: (acc[t % 3][:], f"acc{t % 3}"), gpc, "gpc", bpc, "bpc", 2)
                     + [(5, f_store)])
            P.barrier()
            B.release(m)

        stats = P.emit(esems, dsems)
    return nc, stats


def _host_consts():
    cst = np.zeros((128, 8), np.float32)
    for p in range(128):
        j = p % 64
        if j < 16:
            cst[p, 0] = THETA ** (-(2.0 * (j % 8)) / 16.0)
            cst[p, 1] = -1.0 if j < 8 else 1.0
        else:
            cst[p, 0] = 0.0
            cst[p, 1] = 1.0
        if 64 <= p < 96:
            jj = p - 64
            cst[p, 2] = THETA ** (-(2.0 * (jj % 16)) / 32.0)
            cst[p, 3] = -1.0 if jj < 16 else 1.0
        else:
            cst[p, 2] = 0.0
            cst[p, 3] = 1.0
    kl = np.arange(128)[:, None]
    xx = np.arange(MASK_W)[None, :]
    d = kl - xx + MASK_X0
    f = ((np.abs(d) <= 64).astype(np.float32)
         + ((d % 4 == 0) & (np.abs(d) <= 256)).astype(np.float32))
    mb = np.where(f > 0, np.log(np.maximum(f, 1.0)) / 0.125, -320.0)
    d3 = np.arange(128)[:, None] - np.arange(128)[None, :]
    m3 = np.where(np.abs(d3) <= 64, 0.0, -320.0)
    return cst, np.concatenate([mb, m3], axis=1).astype(np.float32)


_CACHE = {}


def kernel(x, mem, positions, g_emb, b_emb, w_in, g_cq, g_ckv, w_uq, w_ukv, w_mem_kv,
           g_out_a, g_out_b, g_out_m, w_out, g_post, b_post):
    f32 = lambda a: np.ascontiguousarray(np.asarray(a), dtype=np.float32)
    if "nc" not in _CACHE:
        _CACHE["nc"] = build_program()[0]
    nc = _CACHE["nc"]
    cst, maskd = _host_consts()
    x = f32(x)
    mem = f32(mem)
    positions = np.ascontiguousarray(np.asarray(positions), dtype=np.int32)
    shared = dict(
        g_emb=f32(g_emb), b_emb=f32(b_emb), w_in=f32(w_in).reshape(D, D_IN), g_cq=f32(g_cq).reshape(256),
        g_ckv=f32(g_ckv).reshape(128), w_uq=f32(w_uq).reshape(256, 768), w_ukv=f32(w_ukv).reshape(128, 1024),
        w_mem_kv=f32(w_mem_kv).reshape(D, 1024), g_out_a=f32(g_out_a).reshape(1024), g_out_b=f32(g_out_b).reshape(512),
        g_out_m=f32(g_out_m).reshape(512), w_out=f32(w_out).reshape(2048, D), g_post=f32(g_post).reshape(D),
        b_post=f32(b_post).reshape(D), cst=cst, maskd=maskd)
    in_maps = []
    for c in range(N_CORES):
        d = dict(shared)
        d["x"] = x[c * NB:(c + 1) * NB]
        d["mem"] = mem[c * NB:(c + 1) * NB]
        d["positions"] = positions[c * NB:(c + 1) * NB]
        in_maps.append(d)
    res = run_bass_kernel_spmd(nc, in_maps, core_ids=list(range(N_CORES)))
    return np.concatenate([r["out"] for r in res.results], axis=0)
```

```python
import numpy as np
from contextlib import ExitStack

import concourse.bass as bass
import concourse.mybir as mybir
from concourse.bass_utils import run_bass_kernel_spmd

F32 = mybir.dt.float32
BF16 = mybir.dt.bfloat16
I32 = mybir.dt.int32
ALU = mybir.AluOpType
AF = mybir.ActivationFunctionType

N_CORES = 8
NB = 2
S = 2048
D = 1024
D_IN = 6048
NMEM = 256
EPS = 1e-5
ALPHA = 2.0 ** 0.25
THETA = 500000.0
TWO_PI = float(2 * np.pi)
C1 = 6.28125
C2 = float(2 * np.pi - 6.28125)
MASK_W = 1408
MASK_X0 = 640

ENGS = ("pe", "act", "dve", "pool", "sp")


class Op:
    __slots__ = ("eng", "fn", "deps", "signal", "sigval", "is_dma", "dsem", "dval", "dprev",
                 "epoch", "esem")

    def __init__(self, eng, fn, is_dma):
        self.eng = eng
        self.fn = fn
        self.deps = {}
        self.signal = False
        self.sigval = 0
        self.is_dma = is_dma
        self.dsem = None
        self.dval = 0
        self.dprev = 0
        self.epoch = 0
        self.esem = None


class Prog:
    N_EPOCH_SEMS = 6
    N_DMA_SEMS = 28
    N_SW_SEMS = 10

    def __init__(self, nc):
        self.nc = nc
        self.items = {e: [] for e in ENGS}
        self.last_writer = {}
        self.readers = {}
        self.epoch = 0
        self.barriers = []
        self.dma_count = 0
        self.sw_count = 0
        self.dma_sem_counts = [0] * self.N_DMA_SEMS
        self.epoch_ops = {e: [] for e in ENGS}

    def add(self, eng, fn, reads=(), writes=(), dma=False):
        op = Op(eng, fn, dma)
        op.epoch = self.epoch
        for r in reads:
            w = self.last_writer.get(r)
            if w is not None:
                op.deps[w] = True
            self.readers.setdefault(r, []).append(op)
        for r in writes:
            w = self.last_writer.get(r)
            if w is not None and w is not op:
                op.deps.setdefault(w, False)
            for rd in self.readers.get(r, ()):
                if rd is not op:
                    op.deps.setdefault(rd, False)
            self.last_writer[r] = op
            self.readers[r] = []
        if dma:
            if eng == "pool":
                k = self.N_DMA_SEMS - self.N_SW_SEMS + (self.sw_count % self.N_SW_SEMS)
                self.sw_count += 1
            else:
                k = self.dma_count % (self.N_DMA_SEMS - self.N_SW_SEMS)
                self.dma_count += 1
            op.dsem = k
            op.dprev = 16 * self.dma_sem_counts[k]
            self.dma_sem_counts[k] += 1
            op.dval = 16 * self.dma_sem_counts[k]
        self.items[eng].append(op)
        self.epoch_ops[eng].append(op)
        return op

    def barrier(self):
        lasts = {}
        for e in ENGS:
            ops = [o for o in self.epoch_ops[e] if not o.is_dma]
            lasts[e] = ops[-1] if ops else None
            if lasts[e] is not None:
                lasts[e].signal = True
        self.barriers.append((lasts, list(self.dma_sem_counts)))
        for e in ENGS:
            self.items[e].append(("barrier", len(self.barriers) - 1))
            self.epoch_ops[e] = []
        self.last_writer = {}
        self.readers = {}
        self.epoch += 1

    def emit(self, esems, dsems):
        nc = self.nc
        for e in ENGS:
            for it in self.items[e]:
                if isinstance(it, tuple):
                    continue
                for d, raw in it.deps.items():
                    if d.is_dma or d.epoch != it.epoch:
                        continue
                    if d.eng == it.eng and not it.is_dma:
                        if it.eng == "pe" or not raw:
                            continue
                    d.signal = True
        cum = {e: [0] * self.N_EPOCH_SEMS for e in ENGS}
        for e in ENGS:
            for it in self.items[e]:
                if isinstance(it, tuple) or it.is_dma:
                    continue
                k = it.epoch % self.N_EPOCH_SEMS
                it.esem = esems[e][k]
                if it.signal:
                    cum[e][k] += 1
                    it.sigval = cum[e][k]
        stats = {e: [0, 0] for e in ENGS}

        def run_engine(e, eng):
            waited = {}

            def wait(sem, val):
                if val <= 0:
                    return
                key = id(sem)
                if waited.get(key, 0) >= val:
                    return
                eng.wait_ge(sem, val)
                waited[key] = val
                stats[e][1] += 1

            for it in self.items[e]:
                if isinstance(it, tuple):
                    lasts, dcounts = self.barriers[it[1]]
                    for e2 in ENGS:
                        lo = lasts[e2]
                        if lo is not None and e2 != e:
                            wait(lo.esem, lo.sigval)
                    for k, c in enumerate(dcounts):
                        wait(dsems[k], 16 * c)
                    continue
                op = it
                for d, raw in op.deps.items():
                    if d.epoch != op.epoch:
                        continue
                    if d.is_dma:
                        wait(dsems[d.dsem], d.dval)
                        continue
                    if d.eng == op.eng and not op.is_dma:
                        if op.eng == "pe" or not raw:
                            continue
                    wait(d.esem, d.sigval)
                if op.is_dma:
                    wait(dsems[op.dsem], op.dprev)
                ins = op.fn(eng)
                stats[e][0] += 1
                if op.is_dma:
                    ins.then_inc(dsems[op.dsem], 16)
                elif op.signal:
                    ins.then_inc(op.esem, 1)
            if e == "sp":
                for k, c in enumerate(self.dma_sem_counts):
                    wait(dsems[k], 16 * c)

        with nc.Block() as block:
            @block.tensor
            def _(eng):
                run_engine("pe", eng)

            @block.scalar
            def _(eng):
                run_engine("act", eng)

            @block.vector
            def _(eng):
                run_engine("dve", eng)

            @block.gpsimd
            def _(eng):
                run_engine("pool", eng)

            @block.sync
            def _(eng):
                run_engine("sp", eng)
        return stats


class Builder:
    SB_BASE = 16512
    SB_LIMIT = 229344

    def __init__(self, nc):
        self.nc = nc
        self.P = Prog(nc)
        self.cur = self.SB_BASE
        self.uid = 0
        self.cache = {}

    def T(self, name, shape, dt):
        n = 1
        for s in shape[1:]:
            n *= s
        nbytes = n * mybir.dt.size(dt)
        nbytes = (nbytes + 31) // 32 * 32
        assert self.cur + nbytes <= self.SB_LIMIT, (name, self.cur, nbytes)
        key = (self.cur, tuple(shape), str(dt))
        t = self.cache.get(key)
        if t is None:
            self.uid += 1
            t = self.nc.alloc_sbuf_tensor_at(f"{name}_{self.uid}", shape, dt, offset=self.cur)
            self.cache[key] = t
        self.cur += nbytes
        return t

    def mark(self):
        return self.cur

    def release(self, m):
        self.cur = m

    def mm(self, out, lhsT, rhs, start, stop, reads, writes):
        self.P.add("pe", lambda e: e.matmul(out, lhsT=lhsT, rhs=rhs, start=start, stop=stop), reads, writes)

    def tr(self, out, in_, ident, reads, writes):
        self.P.add("pe", lambda e: e.transpose(out=out, in_=in_, identity=ident), reads, writes)

    def act(self, out, in_, func, reads, writes, scale=None, bias=None):
        kw = {}
        if scale is not None:
            kw["scale"] = scale
        if bias is not None:
            kw["bias"] = bias
        self.P.add("act", lambda e: e.activation(out=out, in_=in_, func=func, **kw), reads, writes)

    def tt(self, eng, out, in0, in1, op, reads, writes):
        self.P.add(eng, lambda e: e.tensor_tensor(out=out, in0=in0, in1=in1, op=op), reads, writes)

    def ts(self, eng, out, in0, s1, s2, op0, op1, reads, writes):
        if s2 is None:
            self.P.add(eng, lambda e: e.tensor_scalar(out=out, in0=in0, scalar1=s1, scalar2=None, op0=op0), reads, writes)
        else:
            self.P.add(eng, lambda e: e.tensor_scalar(out=out, in0=in0, scalar1=s1, scalar2=s2, op0=op0, op1=op1), reads, writes)

    def stt(self, out, in0, scalar, in1, op0, op1, reads, writes):
        self.P.add("dve", lambda e: e.scalar_tensor_tensor(out=out, in0=in0, scalar=scalar, in1=in1, op0=op0, op1=op1),
                   reads, writes)

    def cp(self, eng, out, in_, reads, writes):
        if eng == "act":
            self.P.add("act", lambda e: e.activation(out=out, in_=in_, func=AF.Copy), reads, writes)
        else:
            self.P.add(eng, lambda e: e.tensor_copy(out=out, in_=in_), reads, writes)

    def memset(self, eng, ap, val, writes):
        self.P.add(eng, lambda e: e.memset(ap, val), (), writes)

    def dma(self, q, out, in_, reads, writes, slow=False):
        if slow:
            self.P.add(q, lambda e: e.dma_start(out=out, in_=in_, allow_slow_non_contiguous=True), reads, writes, dma=True)
        else:
            self.P.add(q, lambda e: e.dma_start(out=out, in_=in_), reads, writes, dma=True)


def blk(i, n=512):
    return slice(i * n, (i + 1) * n)


def build_program():
    nc = bass.Bass("TRN2", target_bir_lowering=False)

    def din(name, shape, dt=F32):
        return nc.dram_tensor(name, shape, dt, kind="ExternalInput").ap()

    x = din("x", [NB, S, D])
    mem = din("mem", [NB, NMEM, D])
    pos = din("positions", [NB, S], I32)
    g_emb = din("g_emb", [D])
    b_emb = din("b_emb", [D])
    w_in = din("w_in", [D, D_IN])
    g_cq = din("g_cq", [256])
    g_ckv = din("g_ckv", [128])
    w_uq = din("w_uq", [256, 768])
    w_ukv = din("w_ukv", [128, 1024])
    w_mem_kv = din("w_mem_kv", [D, 1024])
    g_out_a = din("g_out_a", [1024])
    g_out_b = din("g_out_b", [512])
    g_out_m = din("g_out_m", [512])
    w_out = din("w_out", [2048, D])
    g_post = din("g_post", [D])
    b_post = din("b_post", [D])
    cst = din("cst", [128, 8])
    maskd = din("maskd", [128, MASK_W + 128])
    out = nc.dram_tensor("out", [NB, S, D], F32, kind="ExternalOutput").ap()
    hscr = nc.dram_tensor("hscr", [NB, S, D], F32, kind="Internal").ap()

    w_in_v = w_in.rearrange("(k p) c -> p k c", p=128)

    with ExitStack() as st:
        esems = {e: [st.enter_context(nc.semaphore(f"s_{e}_{i}")) for i in range(Prog.N_EPOCH_SEMS)]
                 for e in ENGS}
        dsems = [st.enter_context(nc.semaphore(f"d_{i}")) for i in range(Prog.N_DMA_SEMS)]
        psT = st.enter_context(nc.psum_tensor("psT", [128, 1024], BF16))
        ssqP = st.enter_context(nc.psum_tensor("ssqP", [128, 512], F32))
        W = [st.enter_context(nc.psum_tensor(f"W{i}", [128, 512], F32)) for i in range(6)]
        WN = [f"W{i}" for i in range(6)]

        B = Builder(nc)
        P = B.P

        hT = B.T("hT", [128, 8, S], BF16)
        yT = B.T("yT", [128, 16, S], BF16)
        ident = B.T("ident", [128, 128], BF16)
        ones_bf = B.T("ones_bf", [128, 128], BF16)
        onesf = B.T("onesf", [128, 128], F32)
        sel_o = B.T("sel_o", [128, 128], F32)
        mhalf = B.T("mhalf", [128, 512], F32)
        cst_t = B.T("cst", [128, 8], F32)
        gcq_t = B.T("gcq", [128, 2], F32)
        gckv_t = B.T("gckv", [128, 1], F32)
        goa_t = B.T("goa", [128, 8], F32)
        gob_t = B.T("gob", [128, 4], F32)
        gom_t = B.T("gom", [128, 4], F32)
        wst = [B.T(f"wst{i}", [128, 8, 512], BF16) for i in range(2)]

        B.memset("pool", ident[:], 0.0, ["ident"])
        P.add("pool", lambda e: e.affine_select(out=ident[:], in_=ident[:], compare_op=ALU.not_equal, fill=1.0,
                                                base=0, pattern=[[-1, 128]], channel_multiplier=1),
              ["ident"], ["ident"])
        B.memset("pool", ones_bf[:], 1.0, ["ones_bf"])
        B.memset("pool", onesf[:], 1.0, ["onesf"])
        B.memset("pool", sel_o[:, 0:64], 0.0, ["sel_o"])
        B.memset("pool", sel_o[:, 64:128], 1.0, ["sel_o"])
        B.memset("pool", mhalf[:], -0.5, ["mhalf"])
        B.dma("sp", cst_t[:], cst, [], ["cst"])
        B.dma("sp", gcq_t[:], g_cq.rearrange("(c p) -> p c", p=128), [], ["gvec"], slow=True)
        B.dma("sp", gckv_t[:], g_ckv.rearrange("(c p) -> p c", p=128), [], ["gvec"], slow=True)
        B.dma("sp", goa_t[:], g_out_a.rearrange("(c p) -> p c", p=128), [], ["gvec"], slow=True)
        B.dma("sp", gob_t[:], g_out_b.rearrange("(c p) -> p c", p=128), [], ["gvec"], slow=True)
        B.dma("sp", gom_t[:], g_out_m.rearrange("(c p) -> p c", p=128), [], ["gvec"], slow=True)
        P.barrier()

        base_mark = B.mark()

        def pipeline(T, stages):
            maxlag = max(l for l, _ in stages)
            stages = sorted(stages, key=lambda lf: -lf[0])
            for tau in range(T + maxlag):
                for lag, fn in stages:
                    t = tau - lag
                    if 0 <= t < T:
                        fn(t)

        def ln_alloc(nsm, nbig):
            return dict(st=[B.T("lst", [128, 2, 6], F32) for _ in range(nsm)],
                        mv=[B.T("lmv", [128, 2], F32) for _ in range(nsm)],
                        rs=[B.T("lrs", [128, 1], F32) for _ in range(nsm)],
                        nmr=[B.T("lnm", [128, 1], F32) for _ in range(nsm)],
                        xn=[B.T("lxn", [128, 1024], F32) for _ in range(nbig)],
                        t1=[B.T("lt1", [128, 1024], F32) for _ in range(nbig + 1)])

        def ln_stages(L, src_fn, gbc, gres, bbc, bres, lag0):
            nsm, nxn, nt1 = len(L["st"]), len(L["xn"]), len(L["t1"])

            def sB(t):
                k = t % nsm
                src, sres = src_fn(t)
                for hh in range(2):
                    P.add("dve", lambda e, hh=hh: e.bn_stats(out=L["st"][k][:, hh, :], in_=src[:, hh * 512:(hh + 1) * 512]),
                          [sres], [f"lst{k}"])
                P.add("dve", lambda e: e.bn_aggr(out=L["mv"][k][:], in_=L["st"][k][:].rearrange("p a b -> p (a b)")),
                      [f"lst{k}"], [f"lmv{k}"])
                B.ts("dve", L["rs"][k][:], L["mv"][k][:, 1:2], EPS, None, ALU.add, None, [f"lmv{k}"], [f"lrs{k}"])

            def sC(t):
                k = t % nsm
                B.tt("pool", L["rs"][k][:], L["rs"][k][:], mhalf[:, 0:1], ALU.pow, [f"lrs{k}", "mhalf"], [f"lrs{k}"])

            def sD(t):
                k = t % nsm
                B.stt(L["nmr"][k][:], L["mv"][k][:, 0:1], -1.0, L["rs"][k][:], ALU.mult, ALU.mult,
                      [f"lmv{k}", f"lrs{k}"], [f"lnm{k}"])

            def sE(t):
                k = t % nsm
                src, sres = src_fn(t)
                B.act(L["xn"][t % nxn][:], src, AF.Identity, [sres, f"lnm{k}", f"lrs{k}"], [f"lxn{t % nxn}"],
                      scale=L["rs"][k][:, 0:1], bias=L["nmr"][k][:, 0:1])

            def sF(t):
                B.tt("dve", L["t1"][t % nt1][:], L["xn"][t % nxn][:], gbc[:], ALU.mult, [f"lxn{t % nxn}", gres], [f"lt1_{t % nt1}"])

            def sG(t):
                B.tt("pool", L["t1"][t % nt1][:], L["t1"][t % nt1][:], bbc[:], ALU.add, [f"lt1_{t % nt1}", bres], [f"lt1_{t % nt1}"])

            return [(lag0, sB), (lag0, sC), (lag0 + 1, sD), (lag0 + 1, sE), (lag0 + 2, sF), (lag0 + 2, sG)]

        wcount = [0]
        wsel = [(0, 1, 2, 3, 4)]

        def next_w():
            sel = wsel[0]
            i = sel[wcount[0] % len(sel)]
            wcount[0] += 1
            return W[i], WN[i]

        def proj_fm(lhs_fn, K, rhs_fn, nrows, reads):
            bank, bn = next_w()
            for k in range(K):
                B.mm(bank[0:nrows, :], lhs_fn(k), rhs_fn(k), k == 0, k == K - 1, reads, [bn])
            return bank, bn

        def rope_tables(b, fr_col, sgn_col, cosT, sinT, tag):
            m = B.mark()
            posb = B.T("posb", [128, S], I32)
            ang = B.T("ang", [128, S], F32)
            ki = B.T("ki", [128, S], I32)
            kf = B.T("kf", [128, S], F32)
            a = B.T("a", [128, S], F32)
            B.dma("sp", posb[:], pos[b].partition_broadcast(128), [], ["posb"])
            B.cp("dve", ang[:], posb[:], ["posb"], ["ang"])
            B.ts("dve", ang[:], ang[:], cst_t[:, fr_col:fr_col + 1], None, ALU.mult, None, ["ang", "cst"], ["ang"])
            for which in range(2):
                if which == 1:
                    B.ts("dve", ang[:], ang[:], float(np.pi / 2), None, ALU.add, None, ["ang"], ["ang"])
                B.ts("dve", ki[:], ang[:], float(1.0 / TWO_PI), None, ALU.mult, None, ["ang"], ["ki"])
                B.cp("dve", kf[:], ki[:], ["ki"], ["kf"])
                B.stt(a[:], kf[:], -C1, ang[:], ALU.mult, ALU.add, ["kf", "ang"], ["a"])
                B.stt(a[:], kf[:], -C2, a[:], ALU.mult, ALU.add, ["kf", "a"], ["a"])
                B.ts("dve", a[:], a[:], float(-np.pi), float(np.pi), ALU.max, ALU.min, ["a"], ["a"])
                if which == 0:
                    B.act(sinT[:], a[:], AF.Sin, ["a", "cst"], [f"sin{tag}"], scale=cst_t[:, sgn_col:sgn_col + 1])
                else:
                    B.act(cosT[:], a[:], AF.Sin, ["a"], [f"cos{tag}"])
            P.barrier()
            B.release(m)

        def attn_bufs():
            d = {}
            d["PT"] = [B.T("PT", [128, 512], BF16) for _ in range(4)]
            d["RD"] = [B.T("RD", [128, 512], F32) for _ in range(1)]
            d["BCS"] = [B.T("BCS", [128, 512], F32) for _ in range(2)]
            d["ON"] = [B.T("ON", [128, 512], F32) for _ in range(2)]
            for t in d["RD"]:
                B.memset("pool", t[:], 1.0, ["RD0"])
            d["step"] = 0
            d["defer"] = []
            d["bj"] = 0
            d["epi"] = 0
            return d

        def attention(ab, KT, QT, V, vcols, nkt, r0, r1, den, scale, maskfn, sg, sgres, gvec, ychunk, sqT, reads, o3=None, bulk=None, flush=True):
            SB = (0, 1)
            OB = (2, 3)
            steps = []
            for qb in range(4):
                kts = [kt for kt in range(nkt) if maskfn is None or maskfn(kt, qb) is not None]
                rng = {}
                for kt in kts:
                    if maskfn is None:
                        rng[kt] = (0, 512)
                    else:
                        rng[kt] = (max(0, 128 * kt - 256 - 512 * qb), min(512, 128 * kt + 384 - 512 * qb))
                full = [kt for kt in kts if rng[kt] == (0, 512)]
                if maskfn is not None:
                    assert len(full) >= 2
                    kts = [full[0]] + [kt for kt in kts if kt not in (full[0], full[-1])] + [full[-1]]
                for j, kt in enumerate(kts):
                    steps.append((qb, kt, j == 0, j == len(kts) - 1, rng[kt][0], rng[kt][1]))
            base = ab["step"]

            def emit_S(i):
                qb, kt, _, _, c0, c1 = steps[i]
                s = SB[(base + i) % 2]
                if maskfn is None:
                    B.mm(W[s][:, :], KT(kt), QT(qb), True, True, reads, [WN[s]])
                else:
                    B.mm(W[s][:, c0:c1], KT(kt), QT(qb)[:, c0:c1], True, False, reads, [WN[s]])
                    B.mm(W[s][:, c0:c1], ident[:, :], maskfn(kt, qb)[:, c0:c1], False, True, ["ident", "maskT"], [WN[s]])

            def epilogue(qb, ob, cur):
                run_deferred(1 << 60)
                j = ab["epi"] % 2
                ab["epi"] += 1
                rd, bcs, on = ab["RD"][0], ab["BCS"][j], ab["ON"][j]
                if den[0] == "row":
                    dr = den[1]
                    den_ap = W[ob][dr:dr + 1, :]
                    den_res = WN[ob]
                else:
                    dr = 0
                    den_ap = W[4][0:1, :]
                    den_res = WN[4]
                if o3 is None:
                    P.add("dve", lambda e: e.reciprocal(out=rd[dr:dr + 1, :], in_=den_ap), [den_res], ["RD0"])
                else:
                    B.tt("dve", rd[dr:dr + 1, :], den_ap, o3[0][dr:dr + 1, blk(qb)], ALU.add, [den_res, o3[1]], ["RD0"])
                    P.add("dve", lambda e: e.reciprocal(out=rd[dr:dr + 1, :], in_=rd[dr:dr + 1, :]), ["RD0"], ["RD0"])

                def st1():
                    if r0 == 0 and r1 == 64:
                        B.mm(W[5][0:64, :], onesf[dr:dr + 1, 0:64], rd[dr:dr + 1, :], True, True, ["RD0", "onesf"], [WN[5]])
                    elif r0 == 64:
                        B.mm(W[5][0:128, :], sel_o[dr:dr + 1, 0:128], rd[dr:dr + 1, :], True, True, ["RD0", "sel_o"], [WN[5]])
                    else:
                        B.mm(W[5][0:128, :], onesf[dr:dr + 1, 0:128], rd[dr:dr + 1, :], True, True, ["RD0", "onesf"], [WN[5]])

                def st2():
                    B.cp("act", bcs[r0:r1, :], W[5][r0:r1, :], [WN[5]], [f"BCS{j}"])

                def st3():
                    if o3 is None:
                        B.tt("dve", on[r0:r1, :], W[ob][r0:r1, :], bcs[r0:r1, :], ALU.mult, [WN[ob], f"BCS{j}"], [f"ON{j}"])
                    else:
                        B.tt("dve", on[r0:r1, :], W[ob][r0:r1, :], o3[0][r0:r1, blk(qb)], ALU.add, [WN[ob], o3[1]], [f"ON{j}"])
                        B.tt("dve", on[r0:r1, :], on[r0:r1, :], bcs[r0:r1, :], ALU.mult, [f"ON{j}", f"BCS{j}"], [f"ON{j}"])

                def st4():
                    B.act(sqT[r0:r1, blk(qb)], on[r0:r1, :], AF.Square, [f"ON{j}"], ["sqT"])
                    B.stt(yT[r0:r1, ychunk, blk(qb)], on[r0:r1, :], gvec, sg[r0:r1, blk(qb)], ALU.mult, ALU.mult,
                          [f"ON{j}", "gvec", sgres], [f"yT{ychunk}"])

                for dly, fn in ((6, st1), (8, st2), (9, st3), (10, st4)):
                    ab["defer"].append((cur + dly, fn))

            def run_deferred(upto):
                q = ab["defer"]
                while q and q[0][0] <= upto:
                    q.pop(0)[1]()

            def emit_rest(i):
                qb, kt, first, last, c0, c1 = steps[i]
                s = SB[(base + i) % 2]
                pi = (base + i) % 4
                pt = ab["PT"][pi]
                ob = OB[(ab["epi"]) % 2]
                B.act(pt[:, c0:c1], W[s][:, c0:c1], AF.Exp, [WN[s]], [f"PT{pi}"], scale=scale)
                B.mm(W[ob][0:vcols, c0:c1], V(kt), pt[:, c0:c1], first, last, [f"PT{pi}"] + reads, [WN[ob]])
                if den[0] == "sep":
                    B.mm(W[4][0:1, :], ones_bf[:, 0:1], pt[:], first, last, [f"PT{pi}", "ones_bf"], [WN[4]])
                if last:
                    if bulk is None:
                        epilogue(qb, ob, base + i)
                    elif den[0] == "sep":
                        B.cp("dve", bulk[0][:, blk(qb)], W[ob][:, :], [WN[ob]], [bulk[1]])
                        B.cp("act", bulk[2][0:1, blk(qb)], W[4][0:1, :], [WN[4]], [bulk[3]])
                        ab["epi"] += 1
                    else:
                        brows = slice(0, 128) if r0 == 64 else slice(0, 65)
                        B.tt("dve", bulk[0][brows, blk(qb)], W[ob][brows, :], bulk[0][brows, blk(qb)], ALU.add,
                             [WN[ob], bulk[1]], [bulk[1]])
                        ab["epi"] += 1

            n = len(steps)
            emit_S(0)
            for i in range(n):
                if i + 1 < n:
                    emit_S(i + 1)
                emit_rest(i)
                run_deferred(base + i)
            if bulk is None and flush:
                run_deferred(1 << 60)
            ab["step"] = base + n

        def run_deferred_ab(ab, upto, maxn=1 << 30):
            q = ab["defer"]
            n = 0
            while q and q[0][0] <= upto and n < maxn:
                q.pop(0)[1]()
                n += 1

        def bulk_epilogue(ab, acc, accres, r0, r1, dr, sg, sgres, gvec, ychunk, sqT, start, dtile=None, dres=None):
            items = []
            if dtile is None:
                dtile, dres = acc, accres
            for qb in range(4):
                t0 = start + 6 * qb
                j = ab["bj"] % 2
                ab["bj"] += 1
                bcs, on = ab["BCS"][j], ab["ON"][j]

                def f_rec(qb=qb):
                    P.add("dve", lambda e: e.reciprocal(out=dtile[dr:dr + 1, blk(qb)], in_=dtile[dr:dr + 1, blk(qb)]),
                          [dres], [dres])

                def f_bc(qb=qb):
                    if r0 == 0 and r1 == 64:
                        B.mm(W[5][0:64, :], onesf[dr:dr + 1, 0:64], dtile[dr:dr + 1, blk(qb)], True, True, [dres, "onesf"], [WN[5]])
                    elif r0 == 64:
                        B.mm(W[5][0:128, :], sel_o[dr:dr + 1, 0:128], dtile[dr:dr + 1, blk(qb)], True, True, [dres, "sel_o"], [WN[5]])
                    else:
                        B.mm(W[5][0:128, :], onesf[dr:dr + 1, 0:128], dtile[dr:dr + 1, blk(qb)], True, True, [dres, "onesf"], [WN[5]])

                def f_cp(bcs=bcs, j=j):
                    B.cp("act", bcs[r0:r1, :], W[5][r0:r1, :], [WN[5]], [f"BCS{j}"])

                def f_mul(qb=qb, bcs=bcs, on=on, j=j):
                    B.tt("dve", on[r0:r1, :], acc[r0:r1, blk(qb)], bcs[r0:r1, :], ALU.mult, [accres, f"BCS{j}"], [f"ON{j}"])

                def f_fin(qb=qb, on=on, j=j):
                    B.act(sqT[r0:r1, blk(qb)], on[r0:r1, :], AF.Square, [f"ON{j}"], ["sqT"])
                    B.stt(yT[r0:r1, ychunk, blk(qb)], on[r0:r1, :], gvec, sg[r0:r1, blk(qb)], ALU.mult, ALU.mult,
                          [f"ON{j}", "gvec", sgres], [f"yT{ychunk}"])

                items += [(t0, f_rec), (t0 + 6, f_bc), (t0 + 8, f_cp), (t0 + 9, f_mul), (t0 + 10, f_fin)]
            ab["defer"] = sorted(ab["defer"] + items, key=lambda x: x[0])

        def attention_p3(ab, KTx, QTc_, V3, odd, mask3, O3s, reads):
            SB = (0, 1)
            OB = (2, 3)
            rows = slice(0, 128) if odd else slice(0, 65)
            vsl = slice(64, 192) if odd else slice(0, 128)
            base = ab["step"]

            def csl(r):
                return slice(r, r + 16 * 127 + 1, 16)

            def emit_S(r):
                s = SB[(base + r) % 2]
                B.mm(W[s][:, 0:128], KTx[:, csl(r)], QTc_[:, csl(r)], True, False, reads, [WN[s]])
                B.mm(W[s][:, 0:128], ident[:, :], mask3, False, True, ["ident", "maskT"], [WN[s]])

            def emit_rest(r):
                s = SB[(base + r) % 2]
                pi = (base + r) % 4
                pt = ab["PT"][pi]
                ob = OB[ab["epi"] % 2]
                B.act(pt[:, 0:128], W[s][:, 0:128], AF.Exp, [WN[s]], [f"PT{pi}"], scale=0.125)
                q = r % 4
                B.mm(W[ob][0:128, q * 128:(q + 1) * 128], V3[:, r, vsl], pt[:, 0:128], True, True, [f"PT{pi}", "V3"], [WN[ob]])
                if q == 3:
                    b3 = r // 4
                    dst = O3s[0][:].rearrange("p (j r) -> p r j", r=16)[rows, 4 * b3:4 * b3 + 4, :]
                    src = W[ob][rows, :].rearrange("p (q j) -> p q j", q=4)
                    B.cp("dve", dst, src, [WN[ob]], [O3s[1]])
                    ab["epi"] += 1

            emit_S(0)
            for r in range(16):
                if r + 1 < 16:
                    emit_S(r + 1)
                emit_rest(r)
                run_deferred_ab(ab, base + r)
            ab["step"] = base + 16

        def ssq_mms(sqT, chunk):
            for tt_ in range(16):
                col = tt_ * 16 + chunk
                B.mm(ssqP[:, col:col + 1], sqT[:, tt_ * 128:(tt_ + 1) * 128], ones_bf[:, 0:1], True, True,
                     ["sqT", "ones_bf"], ["ssqP"])

        def load_w_in(buf, col0, ncols, dst0=0):
            B.dma("pool", wst[buf][:, :, dst0:dst0 + ncols], w_in_v[:, :, col0:col0 + ncols], [], [f"wst{buf}"])

        for b in range(NB):
            m = B.mark()
            gbc = B.T("gbc", [128, 1024], F32)
            bbc = B.T("bbc", [128, 1024], F32)
            xt = [B.T("xt", [128, 1024], F32) for _ in range(4)]
            hb = [B.T("hb", [128, 1024], BF16) for _ in range(3)]
            ha = [B.T("ha", [128, 1024], F32) for _ in range(3)]
            L = ln_alloc(4, 2)
            B.dma("sp", gbc[:], g_emb.partition_broadcast(128), [], ["gbc"])
            B.dma("sp", bbc[:], b_emb.partition_broadcast(128), [], ["bbc"])

            def tsl_(t):
                return slice(t * 128, (t + 1) * 128)

            def p1_load(t):
                B.dma("sp", xt[t % 4][:], x[b, tsl_(t), :], [], [f"xt{t % 4}"])

            def p1_H(t):
                hfr = f"lt1_{t % 3}"
                hf_ = L["t1"][t % 3]
                B.act(hb[t % 3][:], hf_[:], AF.Copy, [hfr], [f"hb{t % 3}"])
                B.act(ha[t % 3][:], hf_[:], AF.Identity, [hfr], [f"ha{t % 3}"], scale=ALPHA)
                B.dma("sp", hscr[b, tsl_(t), :], ha[t % 3][:], [f"ha{t % 3}"], [f"hscr{t % 3}"])
                for k in range(8):
                    B.tr(psT[:, k * 128:(k + 1) * 128], hb[t % 3][:, k * 128:(k + 1) * 128], ident[:],
                         [f"hb{t % 3}", "ident"], ["psT"])

            def p1_J(t):
                B.cp("act", hT[:, :, tsl_(t)], psT[:].rearrange("p (k t) -> p k t", k=8), ["psT"], ["hT"])

            pipeline(16, [(0, p1_load)] + ln_stages(L, lambda t: (xt[t % 4][:], f"xt{t % 4}"), gbc, "gbc", bbc, "bbc", 1)
                     + [(4, p1_H), (5, p1_J)])
            P.barrier()
            B.release(m)

            m = B.mark()
            memb = B.T("memb", [128, 2, 1024], BF16)
            memT = B.T("memT", [128, 8, NMEM], BF16)
            mkT = B.T("mkT", [128, 4, NMEM], BF16)
            mv = B.T("mv", [128, 2, 512], BF16)
            qTm = [B.T("qTm", [128, S], BF16) for _ in range(2)]
            sgm = [B.T("sgm", [128, S], BF16) for _ in range(2)]
            accm = [B.T("accm", [128, S], F32) for _ in range(2)]
            denm = [B.T("denm", [128, S], F32) for _ in range(2)]
            sqT = B.T("sqT", [128, S], BF16)
            ab = attn_bufs()
            wmk = w_mem_kv.rearrange("(k p) c -> p k c", p=128)
            B.dma("pool", memb[:], mem[b].rearrange("(t p) d -> p t d", p=128), [], ["memb"])
            B.dma("pool", wst[0][:], wmk[:, :, 0:512], [], ["wst0"])
            B.dma("pool", wst[1][:], wmk[:, :, 512:1024], [], ["wst1"])
            for t in range(2):
                for k in range(8):
                    B.tr(psT[:, k * 128:(k + 1) * 128], memb[:, t, k * 128:(k + 1) * 128], ident[:], ["memb", "ident"], ["psT"])
                B.cp("act", memT[:, :, t * 128:(t + 1) * 128], psT[:].rearrange("p (k t) -> p k t", k=8), ["psT"], ["memT"])
            for h in range(4):
                bank, bn = next_w()
                for k in range(8):
                    B.mm(bank[:, 0:NMEM], wst[0][:, k, h * 128:(h + 1) * 128], memT[:, k, :], k == 0, k == 7,
                         ["wst0", "memT"], [bn])
                B.cp("dve", mkT[:, h, :], bank[:, 0:NMEM], [bn], ["mkT"])
            for t in range(2):
                bank, bn = next_w()
                for k in range(8):
                    B.mm(bank[:, :], memT[:, k, t * 128:(t + 1) * 128], wst[1][:, k, :], k == 0, k == 7, ["wst1", "memT"], [bn])
                B.cp("dve", mv[:, t, :], bank[:, :], [bn], ["mv"])
            load_w_in(0, 5024, 512)
            load_w_in(1, 5536, 512)
            for h in range(4):
                sl = h % 2
                for qb in range(4):
                    bank, bn = proj_fm(lambda k: wst[0][:, k, h * 128:(h + 1) * 128], 8, lambda k: hT[:, k, blk(qb)], 128,
                                       ["wst0", "hT"])
                    B.cp("dve", qTm[sl][:, blk(qb)], bank[:, :], [bn], [f"qTm{sl}"])
                    run_deferred_ab(ab, 1 << 60, maxn=3)
                    bank, bn = proj_fm(lambda k: wst[1][:, k, h * 128:(h + 1) * 128], 8, lambda k: hT[:, k, blk(qb)], 128,
                                       ["wst1", "hT"])
                    B.act(sgm[sl][:, blk(qb)], bank[:, :], AF.Silu, [bn], [f"sgm{sl}"])
                    run_deferred_ab(ab, 1 << 60, maxn=3)
                run_deferred_ab(ab, 1 << 60)
                if h > 0:
                    ssq_mms(sqT, 12 + h - 1)
                attention(ab, KT=lambda kt: mkT[:, h, kt * 128:(kt + 1) * 128], QT=lambda qb: qTm[sl][:, blk(qb)],
                          V=lambda kt: mv[:, kt, h * 128:(h + 1) * 128], vcols=128, nkt=2, r0=0, r1=128, den=("sep",),
                          scale=128.0 ** -0.5, maskfn=None, sg=sgm[sl], sgres=f"sgm{sl}", gvec=gom_t[:, h:h + 1],
                          ychunk=12 + h, sqT=sqT, reads=["mkT", "mv", f"qTm{sl}"],
                          bulk=(accm[sl], f"accm{sl}", denm[sl], f"denm{sl}"))
                bulk_epilogue(ab, accm[sl], f"accm{sl}", 0, 128, 0, sgm[sl], f"sgm{sl}", gom_t[:, h:h + 1], 12 + h, sqT,
                              ab["step"], dtile=denm[sl], dres=f"denm{sl}")
            run_deferred_ab(ab, 1 << 60)
            ssq_mms(sqT, 15)
            P.barrier()
            B.release(m)

            m = B.mark()
            cosB = B.T("cosB", [128, S], F32)
            sinB = B.T("sinB", [128, S], F32)
            rope_tables(b, 2, 3, cosB, sinB, "B")
            wuq = B.T("wuq", [128, 2, 768], BF16)
            wuq_sw = B.T("wuq_sw", [128, 2, 768], BF16)
            wukv = B.T("wukv", [128, 1024], BF16)
            wkr_sw = B.T("wkr_sw", [128, 8, 96], BF16)
            cqn = B.T("cqn", [128, 2, S], BF16)
            ckvn = B.T("ckvn", [128, S], BF16)
            kr = B.T("kr", [128, S], BF16)
            V2B = B.T("V2B", [128, 16, 192], BF16)
            sqt = [B.T("sqt", [128, 512], BF16) for _ in range(2)]
            R = [B.T("R", [128, 512], F32) for _ in range(2)]
            rt1 = B.T("rt1", [128, 512], F32)
            rt2 = B.T("rt2", [128, 512], F32)
            QTh = B.T("QTh", [128, S], BF16)
            KTh = B.T("KTh", [128, S], BF16)
            sgb = B.T("sgb", [128, S], BF16)
            sqT = B.T("sqT", [128, S], BF16)
            ab = attn_bufs()
            B.dma("pool", wuq[:], w_uq.rearrange("(k p) c -> p k c", p=128), [], ["wuq"])
            B.dma("pool", wukv[:], w_ukv, [], ["wukv"])
            load_w_in(0, 4096, 512)
            load_w_in(1, 4512, 512)
            B.memset("pool", wuq_sw[:], 0.0, ["wuq_sw"])
            wuq4 = wuq[:].rearrange("p k (h t) -> p k h t", t=96)
            wsw4 = wuq_sw[:].rearrange("p k (h t) -> p k h t", t=96)
            for kc in range(2):
                B.cp("pool", wsw4[:, kc, :, 64:80], wuq4[:, kc, :, 80:96], ["wuq", "wuq_sw"], ["wuq_sw"])
                B.cp("pool", wsw4[:, kc, :, 80:96], wuq4[:, kc, :, 64:80], ["wuq", "wuq_sw"], ["wuq_sw"])
            B.memset("pool", wkr_sw[:], 0.0, ["wkr_sw"])
            B.cp("pool", wkr_sw[:, :, 64:80], wst[0][:, :, 400:416], ["wst0", "wkr_sw"], ["wkr_sw"])
            B.cp("pool", wkr_sw[:, :, 80:96], wst[0][:, :, 384:400], ["wst0", "wkr_sw"], ["wkr_sw"])
            B.memset("pool", V2B[:, :, 64:65], 1.0, ["V2B"])
            B.memset("pool", V2B[:, :, 65:128], 0.0, ["V2B"])
            rc = 0
            for qb in range(4):
                hrhs = lambda k: hT[:, k, blk(qb)]
                cb = []
                for c in range(2):
                    cb.append(proj_fm(lambda k: wst[0][:, k, c * 128:(c + 1) * 128], 8, hrhs, 128, ["wst0", "hT"]))
                for c in range(2):
                    B.act(sqt[c][:], cb[c][0][:, :], AF.Square, [cb[c][1]], [f"sqt{c}"])
                sbank, sbn = next_w()
                for c in range(2):
                    B.mm(sbank[:, :], ones_bf[:, :], sqt[c][:], c == 0, c == 1, [f"sqt{c}", "ones_bf"], [sbn])
                j = rc % 2
                rc += 1
                B.ts("dve", R[j][:], sbank[:, :], 1.0 / 256, EPS, ALU.mult, ALU.add, [sbn], [f"R{j}"])
                B.act(R[j][:], R[j][:], AF.Sqrt, [f"R{j}"], [f"R{j}"])
                P.add("dve", lambda e, j=j: e.reciprocal(out=R[j][:], in_=R[j][:]), [f"R{j}"], [f"R{j}"])
                for c in range(2):
                    B.stt(cqn[:, c, blk(qb)], cb[c][0][:, :], gcq_t[:, c:c + 1], R[j][:], ALU.mult, ALU.mult,
                          [cb[c][1], "gvec", f"R{j}"], ["cqn"])
                kb, kbn = proj_fm(lambda k: wst[0][:, k, 256:384], 8, hrhs, 128, ["wst0", "hT"])
                B.act(sqt[0][:], kb[:, :], AF.Square, [kbn], ["sqt0"])
                sbank, sbn = next_w()
                B.mm(sbank[:, :], ones_bf[:, :], sqt[0][:], True, True, ["sqt0", "ones_bf"], [sbn])
                j = rc % 2
                rc += 1
                B.ts("dve", R[j][:], sbank[:, :], 1.0 / 128, EPS, ALU.mult, ALU.add, [sbn], [f"R{j}"])
                B.act(R[j][:], R[j][:], AF.Sqrt, [f"R{j}"], [f"R{j}"])
                P.add("dve", lambda e, j=j: e.reciprocal(out=R[j][:], in_=R[j][:]), [f"R{j}"], [f"R{j}"])
                B.stt(ckvn[:, blk(qb)], kb[:, :], gckv_t[:, 0:1], R[j][:], ALU.mult, ALU.mult, [kbn, "gvec", f"R{j}"], ["ckvn"])
                pa, pan = proj_fm(lambda k: wst[0][:, k, 320:416], 8, hrhs, 96, ["wst0", "hT"])
                pb, pbn = proj_fm(lambda k: wkr_sw[:, k, 0:96], 8, hrhs, 96, ["wkr_sw", "hT"])
                B.tt("dve", rt1[64:96, :], pa[64:96, :], cosB[64:96, blk(qb)], ALU.mult, [pan, "cosB"], ["rt1"])
                B.tt("dve", rt2[64:96, :], pb[64:96, :], sinB[64:96, blk(qb)], ALU.mult, [pbn, "sinB"], ["rt2"])
                B.tt("pool", kr[64:96, blk(qb)], rt1[64:96, :], rt2[64:96, :], ALU.add, ["rt1", "rt2"], ["kr"])
            wv3 = wukv[:].rearrange("p (h t) -> p h t", t=128)
            for h in range(8):
                pr = h // 2
                odd = h % 2
                wsel[0] = (0, 1, 4)
                for qb in range(4):
                    pa, pan = proj_fm(lambda k: wuq[:, k, h * 96:(h + 1) * 96], 2, lambda k: cqn[:, k, blk(qb)], 96, ["wuq", "cqn"])
                    pb, pbn = proj_fm(lambda k: wuq_sw[:, k, h * 96:(h + 1) * 96], 2, lambda k: cqn[:, k, blk(qb)], 96,
                                      ["wuq_sw", "cqn"])
                    B.cp("act", QTh[0:64, blk(qb)], pa[0:64, :], [pan], ["QTh"])
                    B.tt("dve", rt1[64:96, :], pa[64:96, :], cosB[64:96, blk(qb)], ALU.mult, [pan, "cosB"], ["rt1"])
                    B.tt("dve", rt2[64:96, :], pb[64:96, :], sinB[64:96, blk(qb)], ALU.mult, [pbn, "sinB"], ["rt2"])
                    B.tt("pool", QTh[64:96, blk(qb)], rt1[64:96, :], rt2[64:96, :], ALU.add, ["rt1", "rt2", "QTh"], ["QTh"])
                    pc, pcn = proj_fm(lambda k: wukv[:, h * 128:h * 128 + 64], 1, lambda k: ckvn[:, blk(qb)], 64, ["wukv", "ckvn"])
                    B.cp("act", KTh[0:64, blk(qb)], pc[0:64, :], [pcn], ["KTh"])
                B.cp("pool", KTh[64:96, :], kr[64:96, :], ["kr", "KTh"], ["KTh"])
                wsel[0] = (0, 1, 2, 3, 4)
                run_deferred_ab(ab, 1 << 60)
                if not odd:
                    if h > 0:
                        ssq_mms(sqT, 8 + pr - 1)
                    for qb in range(4):
                        bank, bn = proj_fm(lambda k: wst[1][:, k, pr * 128:(pr + 1) * 128], 8, lambda k: hT[:, k, blk(qb)], 128,
                                           ["wst1", "hT"])
                        B.act(sgb[:, blk(qb)], bank[:, :], AF.Silu, [bn], ["sgb"])
                    for t4 in range(4):
                        bank, bn = next_w()
                        for tq in range(4):
                            tt_ = t4 * 4 + tq
                            B.mm(bank[:, tq * 128:(tq + 1) * 128], ckvn[:, tt_ * 128:(tt_ + 1) * 128],
                                 wv3[:, 2 * pr:2 * pr + 2, 64:128], True, True, ["ckvn", "wukv"], [bn])
                        bv = bank[:, :].rearrange("p (q e t) -> p q e t", q=4, e=2)
                        B.cp("act", V2B[:, t4 * 4:(t4 + 1) * 4, 0:64], bv[:, :, 0, :], [bn, "V2B"], ["V2B"])
                        B.cp("dve", V2B[:, t4 * 4:(t4 + 1) * 4, 128:192], bv[:, :, 1, :], [bn, "V2B"], ["V2B"])
                if not odd:
                    V = lambda kt: V2B[:, kt, 0:128]
                    vcols, r0, r1, den = 128, 0, 64, ("row", 64)
                else:
                    V = lambda kt: V2B[:, kt, 64:192]
                    vcols, r0, r1, den = 128, 64, 128, ("row", 0)
                attention(ab, KT=lambda kt: KTh[0:96, kt * 128:(kt + 1) * 128], QT=lambda qb: QTh[0:96, blk(qb)],
                          V=V, vcols=vcols, nkt=16, r0=r0, r1=r1, den=den, scale=96.0 ** -0.5, maskfn=None,
                          sg=sgb, sgres="sgb", gvec=gob_t[r0:r1, pr:pr + 1], ychunk=8 + pr, sqT=sqT,
                          reads=["V2B", "QTh", "KTh"], flush=False)
            run_deferred_ab(ab, 1 << 60)
            ssq_mms(sqT, 11)
            P.barrier()
            B.release(m)

            m = B.mark()
            cosA = B.T("cosA", [128, S], F32)
            sinA = B.T("sinA", [128, S], F32)
            rope_tables(b, 0, 1, cosA, sinA, "A")
            maskT = B.T("maskT", [128, MASK_W + 128], BF16)
            V3 = B.T("V3", [128, 16, 192], BF16)
            O3 = [B.T("O3s", [128, S], F32) for _ in range(2)]
            wswq = B.T("wswq", [128, 8, 128], BF16)
            wswk = B.T("wswk", [128, 8, 128], BF16)
            QTc = B.T("QTc", [128, S], BF16)
            KTc = B.T("KTc", [128, S], BF16)
            KTo = B.T("KTo", [128, S], BF16)
            V2A = B.T("V2A", [128, 16, 192], BF16)
            sga = [B.T("sga", [128, S], BF16) for _ in range(2)]
            rt1 = B.T("rt1", [128, 512], F32)
            rt2 = B.T("rt2", [128, 512], F32)
            sqT = B.T("sqT", [128, S], BF16)
            ab = dict(PT=[B.T("PT", [128, 512], BF16) for _ in range(4)],
                      BCS=[B.T("BCS", [128, 512], F32) for _ in range(2)],
                      ON=[B.T("ON", [128, 512], F32) for _ in range(2)],
                      step=0, epi=0, defer=[], bj=0)
            B.dma("pool", maskT[:], maskd, [], ["maskT"])
            B.memset("pool", KTc[64:128, :], 0.0, ["KTc"])
            B.memset("pool", KTo[0:64, :], 0.0, ["KTc"])
            B.memset("pool", wswq[:], 0.0, ["wswq"])
            B.memset("pool", wswk[:], 0.0, ["wswk"])
            B.memset("pool", V2A[:, :, 64:65], 1.0, ["V2A"])
            B.memset("pool", V2A[:, :, 65:128], 0.0, ["V2A"])
            B.memset("pool", V3[:, :, 64:65], 1.0, ["V3"])
            B.memset("pool", V3[:, :, 65:128], 0.0, ["V3"])

            def maskfn(kt, qb):
                dmin = 128 * kt - 512 * qb - 511
                dmax = 128 * kt + 127 - 512 * qb
                if dmin > 256 or dmax < -256:
                    return None
                off = MASK_X0 - 128 * kt + 512 * qb
                assert 0 <= off and off + 512 <= MASK_W
                return maskT[:, off:off + 512]

            for c in range(8):
                sl = c % 2
                wb = wst[sl]
                sgc = sga[sl]
                sgn = f"sga{sl}"
                for i in range(4):
                    load_w_in(sl, 1024 * i + c * 128, 128, dst0=128 * i)
                wq4 = wb[:, :, 0:128].rearrange("p k (e t) -> p k e t", e=2)
                wk4 = wb[:, :, 128:256].rearrange("p k (e t) -> p k e t", e=2)
                sq4 = wswq[:].rearrange("p k (e t) -> p k e t", e=2)
                sk4 = wswk[:].rearrange("p k (e t) -> p k e t", e=2)
                B.cp("pool", sq4[:, :, :, 0:8], wq4[:, :, :, 8:16], [f"wst{sl}", "wswq"], ["wswq"])
                B.cp("pool", sq4[:, :, :, 8:16], wq4[:, :, :, 0:8], [f"wst{sl}", "wswq"], ["wswq"])
                B.cp("pool", sk4[:, :, :, 0:8], wk4[:, :, :, 8:16], [f"wst{sl}", "wswk"], ["wswk"])
                B.cp("pool", sk4[:, :, :, 8:16], wk4[:, :, :, 0:8], [f"wst{sl}", "wswk"], ["wswk"])
                for qb in range(4):
                    hrhs = lambda k: hT[:, k, blk(qb)]
                    for (dst, dname, c0, wsw, wswn) in ((QTc, "QTc", 0, wswq, "wswq"), (KTc, "KTc", 128, wswk, "wswk")):
                        pa, pan = proj_fm(lambda k: wb[:, k, c0:c0 + 128], 8, hrhs, 128, [f"wst{sl}", "hT"])
                        pb, pbn = proj_fm(lambda k: wsw[:, k, :], 8, hrhs, 128, [wswn, "hT"])
                        B.tt("dve", rt1[:], pa[:, :], cosA[:, blk(qb)], ALU.mult, [pan, "cosA"], ["rt1"])
                        B.tt("dve", rt2[:], pb[:, :], sinA[:, blk(qb)], ALU.mult, [pbn, "sinA"], ["rt2"])
                        if dname == "QTc":
                            B.tt("pool", dst[:, blk(qb)], rt1[:], rt2[:], ALU.add, ["rt1", "rt2"], [dname])
                        else:
                            B.tt("pool", KTc[0:64, blk(qb)], rt1[0:64, :], rt2[0:64, :], ALU.add, ["rt1", "rt2"], [dname])
                            B.tt("pool", KTo[64:128, blk(qb)], rt1[64:128, :], rt2[64:128, :], ALU.add, ["rt1", "rt2"], [dname])
                        run_deferred_ab(ab, 1 << 60, maxn=3)
                    bank, bn = proj_fm(lambda k: wb[:, k, 384:512], 8, hrhs, 128, [f"wst{sl}", "hT"])
                    B.act(sgc[:, blk(qb)], bank[:, :], AF.Silu, [bn], [sgn])
                for t4 in range(4):
                    bank, bn = next_w()
                    for tq in range(4):
                        tt_ = t4 * 4 + tq
                        for k in range(8):
                            B.mm(bank[:, tq * 128:(tq + 1) * 128], hT[:, k, tt_ * 128:(tt_ + 1) * 128], wb[:, k, 256:384],
                                 k == 0, k == 7, [f"wst{sl}", "hT"], [bn])
                    bv = bank[:, :].rearrange("p (q e t) -> p q e t", q=4, e=2)
                    B.cp("act", V2A[:, t4 * 4:(t4 + 1) * 4, 0:64], bv[:, :, 0, :], [bn, "V2A"], ["V2A"])
                    B.cp("dve", V2A[:, t4 * 4:(t4 + 1) * 4, 128:192], bv[:, :, 1, :], [bn, "V2A"], ["V2A"])
                for r4 in range(4):
                    bank, bn = next_w()
                    for tq in range(4):
                        r = r4 * 4 + tq
                        for k in range(8):
                            B.mm(bank[:, tq * 128:(tq + 1) * 128], hT[:, k, r:r + 16 * 127 + 1:16], wb[:, k, 256:384],
                                 k == 0, k == 7, [f"wst{sl}", "hT"], [bn])
                    bv = bank[:, :].rearrange("p (q e t) -> p q e t", q=4, e=2)
                    B.cp("act", V3[:, r4 * 4:(r4 + 1) * 4, 0:64], bv[:, :, 0, :], [bn, "V3"], ["V3"])
                    B.cp("dve", V3[:, r4 * 4:(r4 + 1) * 4, 128:192], bv[:, :, 1, :], [bn, "V3"], ["V3"])
                run_deferred_ab(ab, 1 << 60)
                if c > 0:
                    ssq_mms(sqT, c - 1)
                for odd in range(2):
                    if not odd:
                        V = lambda kt: V2A[:, kt, 0:128]
                        vcols, r0, r1, den = 128, 0, 64, ("row", 64)
                    else:
                        V = lambda kt: V2A[:, kt, 64:192]
                        vcols, r0, r1, den = 128, 64, 128, ("row", 0)
                    KTx = KTo if odd else KTc
                    acc, accres = O3[odd], f"O3s{odd}"
                    attention_p3(ab, KTx, QTc, V3, odd, maskT[:, MASK_W:MASK_W + 128], (acc, accres), ["QTc", "KTc"])
                    attention(ab, KT=lambda kt: KTx[:, kt * 128:(kt + 1) * 128], QT=lambda qb: QTc[:, blk(qb)],
                              V=V, vcols=vcols, nkt=16, r0=r0, r1=r1, den=den, scale=0.125, maskfn=maskfn,
                              sg=sgc, sgres=sgn, gvec=goa_t[r0:r1, c:c + 1], ychunk=c, sqT=sqT,
                              reads=["V2A", "QTc", "KTc"], bulk=(acc, accres))
                    bulk_epilogue(ab, acc, accres, r0, r1, den[1], sgc, sgn, goa_t[r0:r1, c:c + 1], c, sqT,
                                  ab["step"] + (17 if not odd else 0))
            run_deferred_ab(ab, 1 << 60)
            ssq_mms(sqT, 7)
            P.barrier()
            B.release(m)

            m = B.mark()
            wo = B.T("wo", [128, 16, D], BF16)
            gpc = B.T("gpc", [128, 1024], F32)
            bpc = B.T("bpc", [128, 1024], F32)
            hf = [B.T("hf", [128, 1024], F32) for _ in range(3)]
            acc = [B.T("acc", [128, 1024], F32) for _ in range(3)]
            rg = [B.T("rg", [128, 4], F32) for _ in range(3)]
            L = ln_alloc(4, 2)
            wo_v = w_out.rearrange("(k p) c -> p k c", p=128)
            for q4 in range(4):
                B.dma("pool", wo[:, q4 * 4:(q4 + 1) * 4, :], wo_v[:, q4 * 4:(q4 + 1) * 4, :], [], ["wo"])
            B.dma("sp", gpc[:], g_post.partition_broadcast(128), [], ["gpc"])
            B.dma("sp", bpc[:], b_post.partition_broadcast(128), [], ["bpc"])
            groups = ((0, 8, 1024.0), (8, 12, 512.0), (12, 16, 512.0))

            def tsl_(t):
                return slice(t * 128, (t + 1) * 128)

            def f_load(t):
                B.dma("sp", hf[t % 3][:], hscr[b, tsl_(t), :], [], [f"hf{t % 3}"])

            def f_mm(t):
                for n in range(2):
                    for gi, (c0, c1, width) in enumerate(groups):
                        bi = n * 3 + gi
                        for c in range(c0, c1):
                            B.mm(W[bi][:, :], yT[:, c, tsl_(t)], wo[:, c, blk(n)], c == c0, c == c1 - 1, ["yT", "wo"],
                                 [f"{WN[bi]}"])

            def f_rg(t):
                k = t % 3
                for gi, (c0, c1, width) in enumerate(groups):
                    P.add("dve", lambda e, gi=gi, c0=c0, c1=c1: e.reduce_sum(
                        out=rg[k][:, gi:gi + 1], in_=ssqP[:, t * 16 + c0:t * 16 + c1], axis=mybir.AxisListType.X),
                        ["ssqP"], [f"rg{k}"])
                    B.ts("dve", rg[k][:, gi:gi + 1], rg[k][:, gi:gi + 1], 1.0 / width, EPS, ALU.mult, ALU.add,
                         [f"rg{k}"], [f"rg{k}"])
                B.tt("pool", rg[k][:, 0:3], rg[k][:, 0:3], mhalf[:, 0:3], ALU.pow, [f"rg{k}", "mhalf"], [f"rg{k}"])

            def f_stt(t):
                k = t % 3
                for n in range(2):
                    for gi in range(3):
                        bi = n * 3 + gi
                        src = hf[k][:, blk(n)] if gi == 0 else acc[k][:, blk(n)]
                        srcn = f"hf{k}" if gi == 0 else f"acc{k}"
                        B.stt(acc[k][:, blk(n)], W[bi][:, :], rg[k][:, gi:gi + 1], src, ALU.mult, ALU.add,
                              [WN[bi], f"rg{k}", srcn], [f"acc{k}"])

            def f_store(t):
                B.dma("sp", out[b, tsl_(t), :], L["t1"][t % 3][:], [f"lt1_{t % 3}"], [f"out{t % 3}"])

            pipeline(16, [(0, f_load), (0, f_mm), (0, f_rg), (1, f_stt)]
                     + ln_stages(L, lambda t: (acc[t % 3][:], f"acc{t % 3}"), gpc, "gpc", bpc, "bpc", 2)
                     + [(5, f_store)])
            P.barrier()
            B.release(m)

        stats = P.emit(esems, dsems)
    return nc, stats


def _host_consts():
    cst = np.zeros((128, 8), np.float32)
    for p in range(128):
        j = p % 64
        if j < 16:
            cst[p, 0] = THETA ** (-(2.0 * (j % 8)) / 16.0)
            cst[p, 1] = -1.0 if j < 8 else 1.0
        else:
            cst[p, 0] = 0.0
            cst[p, 1] = 1.0
        if 64 <= p < 96:
            jj = p - 64
            cst[p, 2] = THETA ** (-(2.0 * (jj % 16)) / 32.0)
            cst[p, 3] = -1.0 if jj < 16 else 1.0
        else:
            cst[p, 2] = 0.0
            cst[p, 3] = 1.0
    kl = np.arange(128)[:, None]
    xx = np.arange(MASK_W)[None, :]
    d = kl - xx + MASK_X0
    f = ((np.abs(d) <= 64).astype(np.float32)
         + ((d % 4 == 0) & (np.abs(d) <= 256)).astype(np.float32))
    mb = np.where(f > 0, np.log(np.maximum(f, 1.0)) / 0.125, -320.0)
    d3 = np.arange(128)[:, None] - np.arange(128)[None, :]
    m3 = np.where(np.abs(d3) <= 64, 0.0, -320.0)
    return cst, np.concatenate([mb, m3], axis=1).astype(np.float32)


_CACHE = {}


def kernel(x, mem, positions, g_emb, b_emb, w_in, g_cq, g_ckv, w_uq, w_ukv, w_mem_kv,
           g_out_a, g_out_b, g_out_m, w_out, g_post, b_post):
    f32 = lambda a: np.ascontiguousarray(np.asarray(a), dtype=np.float32)
    if "nc" not in _CACHE:
        _CACHE["nc"] = build_program()[0]
    nc = _CACHE["nc"]
    cst, maskd = _host_consts()
    x = f32(x)
    mem = f32(mem)
    positions = np.ascontiguousarray(np.asarray(positions), dtype=np.int32)
    shared = dict(
        g_emb=f32(g_emb), b_emb=f32(b_emb), w_in=f32(w_in).reshape(D, D_IN), g_cq=f32(g_cq).reshape(256),
        g_ckv=f32(g_ckv).reshape(128), w_uq=f32(w_uq).reshape(256, 768), w_ukv=f32(w_ukv).reshape(128, 1024),
        w_mem_kv=f32(w_mem_kv).reshape(D, 1024), g_out_a=f32(g_out_a).reshape(1024), g_out_b=f32(g_out_b).reshape(512),
        g_out_m=f32(g_out_m).reshape(512), w_out=f32(w_out).reshape(2048, D), g_post=f32(g_post).reshape(D),
        b_post=f32(b_post).reshape(D), cst=cst, maskd=maskd)
    in_maps = []
    for c in range(N_CORES):
        d = dict(shared)
        d["x"] = x[c * NB:(c + 1) * NB]
        d["mem"] = mem[c * NB:(c + 1) * NB]
        d["positions"] = positions[c * NB:(c + 1) * NB]
        in_maps.append(d)
    res = run_bass_kernel_spmd(nc, in_maps, core_ids=list(range(N_CORES)))
    return np.concatenate([r["out"] for r in res.results], axis=0)
```

```python
import numpy as np
from contextlib import ExitStack

import concourse.bass as bass
import concourse.mybir as mybir
from concourse.bass_utils import run_bass_kernel_spmd

F32 = mybir.dt.float32
BF16 = mybir.dt.bfloat16
I32 = mybir.dt.int32
ALU = mybir.AluOpType
AF = mybir.ActivationFunctionType

N_CORES = 8
NB = 2
S = 2048
D = 1024
D_IN = 6048
NMEM = 256
EPS = 1e-5
ALPHA = 2.0 ** 0.25
THETA = 500000.0
TWO_PI = float(2 * np.pi)
C1 = 6.28125
C2 = float(2 * np.pi - 6.28125)
MASK_W = 1408
MASK_X0 = 640

ENGS = ("pe", "act", "dve", "pool", "sp")


class Op:
    __slots__ = ("eng", "fn", "deps", "signal", "sigval", "is_dma", "dsem", "dval", "dprev",
                 "epoch", "esem")

    def __init__(self, eng, fn, is_dma):
        self.eng = eng
        self.fn = fn
        self.deps = {}
        self.signal = False
        self.sigval = 0
        self.is_dma = is_dma
        self.dsem = None
        self.dval = 0
        self.dprev = 0
        self.epoch = 0
        self.esem = None


class Prog:
    N_EPOCH_SEMS = 6
    N_DMA_SEMS = 28
    N_SW_SEMS = 10

    def __init__(self, nc):
        self.nc = nc
        self.items = {e: [] for e in ENGS}
        self.last_writer = {}
        self.readers = {}
        self.epoch = 0
        self.barriers = []
        self.dma_count = 0
        self.sw_count = 0
        self.dma_sem_counts = [0] * self.N_DMA_SEMS
        self.epoch_ops = {e: [] for e in ENGS}

    def add(self, eng, fn, reads=(), writes=(), dma=False):
        op = Op(eng, fn, dma)
        op.epoch = self.epoch
        for r in reads:
            w = self.last_writer.get(r)
            if w is not None:
                op.deps[w] = True
            self.readers.setdefault(r, []).append(op)
        for r in writes:
            w = self.last_writer.get(r)
            if w is not None and w is not op:
                op.deps.setdefault(w, False)
            for rd in self.readers.get(r, ()):
                if rd is not op:
                    op.deps.setdefault(rd, False)
            self.last_writer[r] = op
            self.readers[r] = []
        if dma:
            if eng == "pool":
                k = self.N_DMA_SEMS - self.N_SW_SEMS + (self.sw_count % self.N_SW_SEMS)
                self.sw_count += 1
            else:
                k = self.dma_count % (self.N_DMA_SEMS - self.N_SW_SEMS)
                self.dma_count += 1
            op.dsem = k
            op.dprev = 16 * self.dma_sem_counts[k]
            self.dma_sem_counts[k] += 1
            op.dval = 16 * self.dma_sem_counts[k]
        self.items[eng].append(op)
        self.epoch_ops[eng].append(op)
        return op

    def barrier(self):
        lasts = {}
        for e in ENGS:
            ops = [o for o in self.epoch_ops[e] if not o.is_dma]
            lasts[e] = ops[-1] if ops else None
            if lasts[e] is not None:
                lasts[e].signal = True
        self.barriers.append((lasts, list(self.dma_sem_counts)))
        for e in ENGS:
            self.items[e].append(("barrier", len(self.barriers) - 1))
            self.epoch_ops[e] = []
        self.last_writer = {}
        self.readers = {}
        self.epoch += 1

    def emit(self, esems, dsems):
        nc = self.nc
        for e in ENGS:
            for it in self.items[e]:
                if isinstance(it, tuple):
                    continue
                for d, raw in it.deps.items():
                    if d.is_dma or d.epoch != it.epoch:
                        continue
                    if d.eng == it.eng and not it.is_dma:
                        if it.eng == "pe" or not raw:
                            continue
                    d.signal = True
        cum = {e: [0] * self.N_EPOCH_SEMS for e in ENGS}
        for e in ENGS:
            for it in self.items[e]:
                if isinstance(it, tuple) or it.is_dma:
                    continue
                k = it.epoch % self.N_EPOCH_SEMS
                it.esem = esems[e][k]
                if it.signal:
                    cum[e][k] += 1
                    it.sigval = cum[e][k]
        stats = {e: [0, 0] for e in ENGS}

        def run_engine(e, eng):
            waited = {}

            def wait(sem, val):
                if val <= 0:
                    return
                key = id(sem)
                if waited.get(key, 0) >= val:
                    return
                eng.wait_ge(sem, val)
                waited[key] = val
                stats[e][1] += 1

            for it in self.items[e]:
                if isinstance(it, tuple):
                    lasts, dcounts = self.barriers[it[1]]
                    for e2 in ENGS:
                        lo = lasts[e2]
                        if lo is not None and e2 != e:
                            wait(lo.esem, lo.sigval)
                    for k, c in enumerate(dcounts):
                        wait(dsems[k], 16 * c)
                    continue
                op = it
                for d, raw in op.deps.items():
                    if d.epoch != op.epoch:
                        continue
                    if d.is_dma:
                        wait(dsems[d.dsem], d.dval)
                        continue
                    if d.eng == op.eng and not op.is_dma:
                        if op.eng == "pe" or not raw:
                            continue
                    wait(d.esem, d.sigval)
                if op.is_dma:
                    wait(dsems[op.dsem], op.dprev)
                ins = op.fn(eng)
                stats[e][0] += 1
                if op.is_dma:
                    ins.then_inc(dsems[op.dsem], 16)
                elif op.signal:
                    ins.then_inc(op.esem, 1)
            if e == "sp":
                for k, c in enumerate(self.dma_sem_counts):
                    wait(dsems[k], 16 * c)

        with nc.Block() as block:
            @block.tensor
            def _(eng):
                run_engine("pe", eng)

            @block.scalar
            def _(eng):
                run_engine("act", eng)

            @block.vector
            def _(eng):
                run_engine("dve", eng)

            @block.gpsimd
            def _(eng):
                run_engine("pool", eng)

            @block.sync
            def _(eng):
                run_engine("sp", eng)
        return stats


class Builder:
    SB_BASE = 16512
    SB_LIMIT = 229344

    def __init__(self, nc):
        self.nc = nc
        self.P = Prog(nc)
        self.cur = self.SB_BASE
        self.uid = 0
        self.cache = {}

    def T(self, name, shape, dt):
        n = 1
        for s in shape[1:]:
            n *= s
        nbytes = n * mybir.dt.size(dt)
        nbytes = (nbytes + 31) // 32 * 32
        assert self.cur + nbytes <= self.SB_LIMIT, (name, self.cur, nbytes)
        key = (self.cur, tuple(shape), str(dt))
        t = self.cache.get(key)
        if t is None:
            self.uid += 1
            t = self.nc.alloc_sbuf_tensor_at(f"{name}_{self.uid}", shape, dt, offset=self.cur)
            self.cache[key] = t
        self.cur += nbytes
        return t

    def mark(self):
        return self.cur

    def release(self, m):
        self.cur = m

    def mm(self, out, lhsT, rhs, start, stop, reads, writes):
        self.P.add("pe", lambda e: e.matmul(out, lhsT=lhsT, rhs=rhs, start=start, stop=stop), reads, writes)

    def tr(self, out, in_, ident, reads, writes):
        self.P.add("pe", lambda e: e.transpose(out=out, in_=in_, identity=ident), reads, writes)

    def act(self, out, in_, func, reads, writes, scale=None, bias=None):
        kw = {}
        if scale is not None:
            kw["scale"] = scale
        if bias is not None:
            kw["bias"] = bias
        self.P.add("act", lambda e: e.activation(out=out, in_=in_, func=func, **kw), reads, writes)

    def tt(self, eng, out, in0, in1, op, reads, writes):
        self.P.add(eng, lambda e: e.tensor_tensor(out=out, in0=in0, in1=in1, op=op), reads, writes)

    def ts(self, eng, out, in0, s1, s2, op0, op1, reads, writes):
        if s2 is None:
            self.P.add(eng, lambda e: e.tensor_scalar(out=out, in0=in0, scalar1=s1, scalar2=None, op0=op0), reads, writes)
        else:
            self.P.add(eng, lambda e: e.tensor_scalar(out=out, in0=in0, scalar1=s1, scalar2=s2, op0=op0, op1=op1), reads, writes)

    def stt(self, out, in0, scalar, in1, op0, op1, reads, writes):
        self.P.add("dve", lambda e: e.scalar_tensor_tensor(out=out, in0=in0, scalar=scalar, in1=in1, op0=op0, op1=op1),
                   reads, writes)

    def cp(self, eng, out, in_, reads, writes):
        if eng == "act":
            self.P.add("act", lambda e: e.activation(out=out, in_=in_, func=AF.Copy), reads, writes)
        else:
            self.P.add(eng, lambda e: e.tensor_copy(out=out, in_=in_), reads, writes)

    def memset(self, eng, ap, val, writes):
        self.P.add(eng, lambda e: e.memset(ap, val), (), writes)

    def dma(self, q, out, in_, reads, writes, slow=False):
        if slow:
            self.P.add(q, lambda e: e.dma_start(out=out, in_=in_, allow_slow_non_contiguous=True), reads, writes, dma=True)
        else:
            self.P.add(q, lambda e: e.dma_start(out=out, in_=in_), reads, writes, dma=True)


def blk(i, n=512):
    return slice(i * n, (i + 1) * n)


def build_program():
    nc = bass.Bass("TRN2", target_bir_lowering=False)

    def din(name, shape, dt=F32):
        return nc.dram_tensor(name, shape, dt, kind="ExternalInput").ap()

    x = din("x", [NB, S, D])
    mem = din("mem", [NB, NMEM, D])
    pos = din("positions", [NB, S], I32)
    g_emb = din("g_emb", [D])
    b_emb = din("b_emb", [D])
    w_in = din("w_in", [D, D_IN])
    g_cq = din("g_cq", [256])
    g_ckv = din("g_ckv", [128])
    w_uq = din("w_uq", [256, 768])
    w_ukv = din("w_ukv", [128, 1024])
    w_mem_kv = din("w_mem_kv", [D, 1024])
    g_out_a = din("g_out_a", [1024])
    g_out_b = din("g_out_b", [512])
    g_out_m = din("g_out_m", [512])
    w_out = din("w_out", [2048, D])
    g_post = din("g_post", [D])
    b_post = din("b_post", [D])
    cst = din("cst", [128, 8])
    maskd = din("maskd", [128, MASK_W + 128])
    out = nc.dram_tensor("out", [NB, S, D], F32, kind="ExternalOutput").ap()
    hscr = nc.dram_tensor("hscr", [NB, S, D], F32, kind="Internal").ap()

    w_in_v = w_in.rearrange("(k p) c -> p k c", p=128)

    with ExitStack() as st:
        esems = {e: [st.enter_context(nc.semaphore(f"s_{e}_{i}")) for i in range(Prog.N_EPOCH_SEMS)]
                 for e in ENGS}
        dsems = [st.enter_context(nc.semaphore(f"d_{i}")) for i in range(Prog.N_DMA_SEMS)]
        psT = st.enter_context(nc.psum_tensor("psT", [128, 1024], BF16))
        ssqP = st.enter_context(nc.psum_tensor("ssqP", [128, 512], F32))
        W = [st.enter_context(nc.psum_tensor(f"W{i}", [128, 512], F32)) for i in range(6)]
        WN = [f"W{i}" for i in range(6)]

        B = Builder(nc)
        P = B.P

        hT = B.T("hT", [128, 8, S], BF16)
        yT = B.T("yT", [128, 16, S], BF16)
        ident = B.T("ident", [128, 128], BF16)
        ones_bf = B.T("ones_bf", [128, 128], BF16)
        onesf = B.T("onesf", [128, 128], F32)
        sel_o = B.T("sel_o", [128, 128], F32)
        mhalf = B.T("mhalf", [128, 512], F32)
        cst_t = B.T("cst", [128, 8], F32)
        gcq_t = B.T("gcq", [128, 2], F32)
        gckv_t = B.T("gckv", [128, 1], F32)
        goa_t = B.T("goa", [128, 8], F32)
        gob_t = B.T("gob", [128, 4], F32)
        gom_t = B.T("gom", [128, 4], F32)
        wst = [B.T(f"wst{i}", [128, 8, 512], BF16) for i in range(2)]

        B.memset("pool", ident[:], 0.0, ["ident"])
        P.add("pool", lambda e: e.affine_select(out=ident[:], in_=ident[:], compare_op=ALU.not_equal, fill=1.0,
                                                base=0, pattern=[[-1, 128]], channel_multiplier=1),
              ["ident"], ["ident"])
        B.memset("pool", ones_bf[:], 1.0, ["ones_bf"])
        B.memset("pool", onesf[:], 1.0, ["onesf"])
        B.memset("pool", sel_o[:, 0:64], 0.0, ["sel_o"])
        B.memset("pool", sel_o[:, 64:128], 1.0, ["sel_o"])
        B.memset("pool", mhalf[:], -0.5, ["mhalf"])
        B.dma("sp", cst_t[:], cst, [], ["cst"])
        B.dma("sp", gcq_t[:], g_cq.rearrange("(c p) -> p c", p=128), [], ["gvec"], slow=True)
        B.dma("sp", gckv_t[:], g_ckv.rearrange("(c p) -> p c", p=128), [], ["gvec"], slow=True)
        B.dma("sp", goa_t[:], g_out_a.rearrange("(c p) -> p c", p=128), [], ["gvec"], slow=True)
        B.dma("sp", gob_t[:], g_out_b.rearrange("(c p) -> p c", p=128), [], ["gvec"], slow=True)
        B.dma("sp", gom_t[:], g_out_m.rearrange("(c p) -> p c", p=128), [], ["gvec"], slow=True)
        P.barrier()

        base_mark = B.mark()

        def pipeline(T, stages):
            maxlag = max(l for l, _ in stages)
            stages = sorted(stages, key=lambda lf: -lf[0])
            for tau in range(T + maxlag):
                for lag, fn in stages:
                    t = tau - lag
                    if 0 <= t < T:
                        fn(t)

        def ln_alloc(nsm, nbig):
            return dict(st=[B.T("lst", [128, 2, 6], F32) for _ in range(nsm)],
                        mv=[B.T("lmv", [128, 2], F32) for _ in range(nsm)],
                        rs=[B.T("lrs", [128, 1], F32) for _ in range(nsm)],
                        nmr=[B.T("lnm", [128, 1], F32) for _ in range(nsm)],
                        xn=[B.T("lxn", [128, 1024], F32) for _ in range(nbig)],
                        t1=[B.T("lt1", [128, 1024], F32) for _ in range(nbig + 1)])

        def ln_stages(L, src_fn, gbc, gres, bbc, bres, lag0):
            nsm, nxn, nt1 = len(L["st"]), len(L["xn"]), len(L["t1"])

            def sB(t):
                k = t % nsm
                src, sres = src_fn(t)
                for hh in range(2):
                    P.add("dve", lambda e, hh=hh: e.bn_stats(out=L["st"][k][:, hh, :], in_=src[:, hh * 512:(hh + 1) * 512]),
                          [sres], [f"lst{k}"])
                P.add("dve", lambda e: e.bn_aggr(out=L["mv"][k][:], in_=L["st"][k][:].rearrange("p a b -> p (a b)")),
                      [f"lst{k}"], [f"lmv{k}"])
                B.ts("dve", L["rs"][k][:], L["mv"][k][:, 1:2], EPS, None, ALU.add, None, [f"lmv{k}"], [f"lrs{k}"])

            def sC(t):
                k = t % nsm
                B.tt("pool", L["rs"][k][:], L["rs"][k][:], mhalf[:, 0:1], ALU.pow, [f"lrs{k}", "mhalf"], [f"lrs{k}"])

            def sD(t):
                k = t % nsm
                B.stt(L["nmr"][k][:], L["mv"][k][:, 0:1], -1.0, L["rs"][k][:], ALU.mult, ALU.mult,
                      [f"lmv{k}", f"lrs{k}"], [f"lnm{k}"])

            def sE(t):
                k = t % nsm
                src, sres = src_fn(t)
                B.act(L["xn"][t % nxn][:], src, AF.Identity, [sres, f"lnm{k}", f"lrs{k}"], [f"lxn{t % nxn}"],
                      scale=L["rs"][k][:, 0:1], bias=L["nmr"][k][:, 0:1])

            def sF(t):
                B.tt("dve", L["t1"][t % nt1][:], L["xn"][t % nxn][:], gbc[:], ALU.mult, [f"lxn{t % nxn}", gres], [f"lt1_{t % nt1}"])

            def sG(t):
                B.tt("pool", L["t1"][t % nt1][:], L["t1"][t % nt1][:], bbc[:], ALU.add, [f"lt1_{t % nt1}", bres], [f"lt1_{t % nt1}"])

            return [(lag0, sB), (lag0, sC), (lag0 + 1, sD), (lag0 + 1, sE), (lag0 + 2, sF), (lag0 + 2, sG)]

        wcount = [0]
        wsel = [(0, 1, 2, 3, 4)]

        def next_w():
            sel = wsel[0]
            i = sel[wcount[0] % len(sel)]
            wcount[0] += 1
            return W[i], WN[i]

        def proj_fm(lhs_fn, K, rhs_fn, nrows, reads):
            bank, bn = next_w()
            for k in range(K):
                B.mm(bank[0:nrows, :], lhs_fn(k), rhs_fn(k), k == 0, k == K - 1, reads, [bn])
            return bank, bn

        def rope_tables(b, fr_col, sgn_col, cosT, sinT, tag):
            m = B.mark()
            posb = B.T("posb", [128, S], I32)
            ang = B.T("ang", [128, S], F32)
            ki = B.T("ki", [128, S], I32)
            kf = B.T("kf", [128, S], F32)
            a = B.T("a", [128, S], F32)
            B.dma("sp", posb[:], pos[b].partition_broadcast(128), [], ["posb"])
            B.cp("dve", ang[:], posb[:], ["posb"], ["ang"])
            B.ts("dve", ang[:], ang[:], cst_t[:, fr_col:fr_col + 1], None, ALU.mult, None, ["ang", "cst"], ["ang"])
            for which in range(2):
                if which == 1:
                    B.ts("dve", ang[:], ang[:], float(np.pi / 2), None, ALU.add, None, ["ang"], ["ang"])
                B.ts("dve", ki[:], ang[:], float(1.0 / TWO_PI), None, ALU.mult, None, ["ang"], ["ki"])
                B.cp("dve", kf[:], ki[:], ["ki"], ["kf"])
                B.stt(a[:], kf[:], -C1, ang[:], ALU.mult, ALU.add, ["kf", "ang"], ["a"])
                B.stt(a[:], kf[:], -C2, a[:], ALU.mult, ALU.add, ["kf", "a"], ["a"])
                B.ts("dve", a[:], a[:], float(-np.pi), float(np.pi), ALU.max, ALU.min, ["a"], ["a"])
                if which == 0:
                    B.act(sinT[:], a[:], AF.Sin, ["a", "cst"], [f"sin{tag}"], scale=cst_t[:, sgn_col:sgn_col + 1])
                else:
                    B.act(cosT[:], a[:], AF.Sin, ["a"], [f"cos{tag}"])
            P.barrier()
            B.release(m)

        def attn_bufs():
            d = {}
            d["PT"] = [B.T("PT", [128, 512], BF16) for _ in range(4)]
            d["RD"] = [B.T("RD", [128, 512], F32) for _ in range(1)]
            d["BCS"] = [B.T("BCS", [128, 512], F32) for _ in range(2)]
            d["ON"] = [B.T("ON", [128, 512], F32) for _ in range(2)]
            for t in d["RD"]:
                B.memset("pool", t[:], 1.0, ["RD0"])
            d["step"] = 0
            d["defer"] = []
            d["bj"] = 0
            d["epi"] = 0
            return d

        def attention(ab, KT, QT, V, vcols, nkt, r0, r1, den, scale, maskfn, sg, sgres, gvec, ychunk, sqT, reads, o3=None, bulk=None, flush=True):
            SB = (0, 1)
            OB = (2, 3)
            steps = []
            for qb in range(4):
                kts = [kt for kt in range(nkt) if maskfn is None or maskfn(kt, qb) is not None]
                rng = {}
                for kt in kts:
                    if maskfn is None:
                        rng[kt] = (0, 512)
                    else:
                        rng[kt] = (max(0, 128 * kt - 256 - 512 * qb), min(512, 128 * kt + 384 - 512 * qb))
                full = [kt for kt in kts if rng[kt] == (0, 512)]
                if maskfn is not None:
                    assert len(full) >= 2
                    kts = [full[0]] + [kt for kt in kts if kt not in (full[0], full[-1])] + [full[-1]]
                for j, kt in enumerate(kts):
                    steps.append((qb, kt, j == 0, j == len(kts) - 1, rng[kt][0], rng[kt][1]))
            base = ab["step"]

            def emit_S(i):
                qb, kt, _, _, c0, c1 = steps[i]
                s = SB[(base + i) % 2]
                if maskfn is None:
                    B.mm(W[s][:, :], KT(kt), QT(qb), True, True, reads, [WN[s]])
                else:
                    B.mm(W[s][:, c0:c1], KT(kt), QT(qb)[:, c0:c1], True, False, reads, [WN[s]])
                    B.mm(W[s][:, c0:c1], ident[:, :], maskfn(kt, qb)[:, c0:c1], False, True, ["ident", "maskT"], [WN[s]])

            def epilogue(qb, ob, cur):
                run_deferred(1 << 60)
                j = ab["epi"] % 2
                ab["epi"] += 1
                rd, bcs, on = ab["RD"][0], ab["BCS"][j], ab["ON"][j]
                if den[0] == "row":
                    dr = den[1]
                    den_ap = W[ob][dr:dr + 1, :]
                    den_res = WN[ob]
                else:
                    dr = 0
                    den_ap = W[4][0:1, :]
                    den_res = WN[4]
                if o3 is None:
                    P.add("dve", lambda e: e.reciprocal(out=rd[dr:dr + 1, :], in_=den_ap), [den_res], ["RD0"])
                else:
                    B.tt("dve", rd[dr:dr + 1, :], den_ap, o3[0][dr:dr + 1, blk(qb)], ALU.add, [den_res, o3[1]], ["RD0"])
                    P.add("dve", lambda e: e.reciprocal(out=rd[dr:dr + 1, :], in_=rd[dr:dr + 1, :]), ["RD0"], ["RD0"])

                def st1():
                    if r0 == 0 and r1 == 64:
                        B.mm(W[5][0:64, :], onesf[dr:dr + 1, 0:64], rd[dr:dr + 1, :], True, True, ["RD0", "onesf"], [WN[5]])
                    elif r0 == 64:
                        B.mm(W[5][0:128, :], sel_o[dr:dr + 1, 0:128], rd[dr:dr + 1, :], True, True, ["RD0", "sel_o"], [WN[5]])
                    else:
                        B.mm(W[5][0:128, :], onesf[dr:dr + 1, 0:128], rd[dr:dr + 1, :], True, True, ["RD0", "onesf"], [WN[5]])

                def st2():
                    B.cp("act", bcs[r0:r1, :], W[5][r0:r1, :], [WN[5]], [f"BCS{j}"])

                def st3():
                    if o3 is None:
                        B.tt("dve", on[r0:r1, :], W[ob][r0:r1, :], bcs[r0:r1, :], ALU.mult, [WN[ob], f"BCS{j}"], [f"ON{j}"])
                    else:
                        B.tt("dve", on[r0:r1, :], W[ob][r0:r1, :], o3[0][r0:r1, blk(qb)], ALU.add, [WN[ob], o3[1]], [f"ON{j}"])
                        B.tt("dve", on[r0:r1, :], on[r0:r1, :], bcs[r0:r1, :], ALU.mult, [f"ON{j}", f"BCS{j}"], [f"ON{j}"])

                def st4():
                    B.act(sqT[r0:r1, blk(qb)], on[r0:r1, :], AF.Square, [f"ON{j}"], ["sqT"])
                    B.stt(yT[r0:r1, ychunk, blk(qb)], on[r0:r1, :], gvec, sg[r0:r1, blk(qb)], ALU.mult, ALU.mult,
                          [f"ON{j}", "gvec", sgres], [f"yT{ychunk}"])

                for dly, fn in ((6, st1), (8, st2), (9, st3), (10, st4)):
                    ab["defer"].append((cur + dly, fn))

            def run_deferred(upto):
                q = ab["defer"]
                while q and q[0][0] <= upto:
                    q.pop(0)[1]()

            def emit_rest(i):
                qb, kt, first, last, c0, c1 = steps[i]
                s = SB[(base + i) % 2]
                pi = (base + i) % 4
                pt = ab["PT"][pi]
                ob = OB[(ab["epi"]) % 2]
                B.act(pt[:, c0:c1], W[s][:, c0:c1], AF.Exp, [WN[s]], [f"PT{pi}"], scale=scale)
                B.mm(W[ob][0:vcols, c0:c1], V(kt), pt[:, c0:c1], first, last, [f"PT{pi}"] + reads, [WN[ob]])
                if den[0] == "sep":
                    B.mm(W[4][0:1, :], ones_bf[:, 0:1], pt[:], first, last, [f"PT{pi}", "ones_bf"], [WN[4]])
                if last:
                    if bulk is None:
                        epilogue(qb, ob, base + i)
                    elif den[0] == "sep":
                        B.cp("dve", bulk[0][:, blk(qb)], W[ob][:, :], [WN[ob]], [bulk[1]])
                        B.cp("act", bulk[2][0:1, blk(qb)], W[4][0:1, :], [WN[4]], [bulk[3]])
                        ab["epi"] += 1
                    else:
                        brows = slice(0, 128) if r0 == 64 else slice(0, 65)
                        B.tt("dve", bulk[0][brows, blk(qb)], W[ob][brows, :], bulk[0][brows, blk(qb)], ALU.add,
                             [WN[ob], bulk[1]], [bulk[1]])
                        ab["epi"] += 1

            n = len(steps)
            emit_S(0)
            for i in range(n):
                if i + 1 < n:
                    emit_S(i + 1)
                emit_rest(i)
                run_deferred(base + i)
            if bulk is None and flush:
                run_deferred(1 << 60)
            ab["step"] = base + n

        def run_deferred_ab(ab, upto, maxn=1 << 30):
            q = ab["defer"]
            n = 0
            while q and q[0][0] <= upto and n < maxn:
                q.pop(0)[1]()
                n += 1

        def bulk_epilogue(ab, acc, accres, r0, r1, dr, sg, sgres, gvec, ychunk, sqT, start, dtile=None, dres=None):
            items = []
            if dtile is None:
                dtile, dres = acc, accres
            for qb in range(4):
                t0 = start + 6 * qb
                j = ab["bj"] % 2
                ab["bj"] += 1
                bcs, on = ab["BCS"][j], ab["ON"][j]

                def f_rec(qb=qb):
                    P.add("dve", lambda e: e.reciprocal(out=dtile[dr:dr + 1, blk(qb)], in_=dtile[dr:dr + 1, blk(qb)]),
                          [dres], [dres])

                def f_bc(qb=qb):
                    if r0 == 0 and r1 == 64:
                        B.mm(W[5][0:64, :], onesf[dr:dr + 1, 0:64], dtile[dr:dr + 1, blk(qb)], True, True, [dres, "onesf"], [WN[5]])
                    elif r0 == 64:
                        B.mm(W[5][0:128, :], sel_o[dr:dr + 1, 0:128], dtile[dr:dr + 1, blk(qb)], True, True, [dres, "sel_o"], [WN[5]])
                    else:
                        B.mm(W[5][0:128, :], onesf[dr:dr + 1, 0:128], dtile[dr:dr + 1, blk(qb)], True, True, [dres, "onesf"], [WN[5]])

                def f_cp(bcs=bcs, j=j):
                    B.cp("act", bcs[r0:r1, :], W[5][r0:r1, :], [WN[5]], [f"BCS{j}"])

                def f_mul(qb=qb, bcs=bcs, on=on, j=j):
                    B.tt("dve", on[r0:r1, :], acc[r0:r1, blk(qb)], bcs[r0:r1, :], ALU.mult, [accres, f"BCS{j}"], [f"ON{j}"])

                def f_fin(qb=qb, on=on, j=j):
                    B.act(sqT[r0:r1, blk(qb)], on[r0:r1, :], AF.Square, [f"ON{j}"], ["sqT"])
                    B.stt(yT[r0:r1, ychunk, blk(qb)], on[r0:r1, :], gvec, sg[r0:r1, blk(qb)], ALU.mult, ALU.mult,
                          [f"ON{j}", "gvec", sgres], [f"yT{ychunk}"])

                items += [(t0, f_rec), (t0 + 6, f_bc), (t0 + 8, f_cp), (t0 + 9, f_mul), (t0 + 10, f_fin)]
            ab["defer"] = sorted(ab["defer"] + items, key=lambda x: x[0])

        def attention_p3(ab, KTx, QTc_, V3, odd, mask3, O3s, reads):
            SB = (0, 1)
            OB = (2, 3)
            rows = slice(0, 128) if odd else slice(0, 65)
            vsl = slice(64, 192) if odd else slice(0, 128)
            base = ab["step"]

            def csl(r):
                return slice(r, r + 16 * 127 + 1, 16)

            def emit_S(r):
                s = SB[(base + r) % 2]
                B.mm(W[s][:, 0:128], KTx[:, csl(r)], QTc_[:, csl(r)], True, False, reads, [WN[s]])
                B.mm(W[s][:, 0:128], ident[:, :], mask3, False, True, ["ident", "maskT"], [WN[s]])

            def emit_rest(r):
                s = SB[(base + r) % 2]
                pi = (base + r) % 4
                pt = ab["PT"][pi]
                ob = OB[ab["epi"] % 2]
                B.act(pt[:, 0:128], W[s][:, 0:128], AF.Exp, [WN[s]], [f"PT{pi}"], scale=0.125)
                q = r % 4
                B.mm(W[ob][0:128, q * 128:(q + 1) * 128], V3[:, r, vsl], pt[:, 0:128], True, True, [f"PT{pi}", "V3"], [WN[ob]])
                if q == 3:
                    b3 = r // 4
                    dst = O3s[0][:].rearrange("p (j r) -> p r j", r=16)[rows, 4 * b3:4 * b3 + 4, :]
                    src = W[ob][rows, :].rearrange("p (q j) -> p q j", q=4)
                    B.cp("dve", dst, src, [WN[ob]], [O3s[1]])
                    ab["epi"] += 1

            emit_S(0)
            for r in range(16):
                if r + 1 < 16:
                    emit_S(r + 1)
                emit_rest(r)
                run_deferred_ab(ab, base + r)
            ab["step"] = base + 16

        def ssq_mms(sqT, chunk):
            for tt_ in range(16):
                col = tt_ * 16 + chunk
                B.mm(ssqP[:, col:col + 1], sqT[:, tt_ * 128:(tt_ + 1) * 128], ones_bf[:, 0:1], True, True,
                     ["sqT", "ones_bf"], ["ssqP"])

        def load_w_in(buf, col0, ncols, dst0=0):
            B.dma("pool", wst[buf][:, :, dst0:dst0 + ncols], w_in_v[:, :, col0:col0 + ncols], [], [f"wst{buf}"])

        for b in range(NB):
            m = B.mark()
            gbc = B.T("gbc", [128, 1024], F32)
            bbc = B.T("bbc", [128, 1024], F32)
            xt = [B.T("xt", [128, 1024], F32) for _ in range(4)]
            hb = [B.T("hb", [128, 1024], BF16) for _ in range(3)]
            ha = [B.T("ha", [128, 1024], F32) for _ in range(3)]
            L = ln_alloc(4, 2)
            B.dma("sp", gbc[:], g_emb.partition_broadcast(128), [], ["gbc"])
            B.dma("sp", bbc[:], b_emb.partition_broadcast(128), [], ["bbc"])

            def tsl_(t):
                return slice(t * 128, (t + 1) * 128)

            def p1_load(t):
                B.dma("sp", xt[t % 4][:], x[b, tsl_(t), :], [], [f"xt{t % 4}"])

            def p1_H(t):
                hfr = f"lt1_{t % 3}"
                hf_ = L["t1"][t % 3]
                B.act(hb[t % 3][:], hf_[:], AF.Copy, [hfr], [f"hb{t % 3}"])
                B.act(ha[t % 3][:], hf_[:], AF.Identity, [hfr], [f"ha{t % 3}"], scale=ALPHA)
                B.dma("sp", hscr[b, tsl_(t), :], ha[t % 3][:], [f"ha{t % 3}"], [f"hscr{t % 3}"])
                for k in range(8):
                    B.tr(psT[:, k * 128:(k + 1) * 128], hb[t % 3][:, k * 128:(k + 1) * 128], ident[:],
                         [f"hb{t % 3}", "ident"], ["psT"])

            def p1_J(t):
                B.cp("act", hT[:, :, tsl_(t)], psT[:].rearrange("p (k t) -> p k t", k=8), ["psT"], ["hT"])

            pipeline(16, [(0, p1_load)] + ln_stages(L, lambda t: (xt[t % 4][:], f"xt{t % 4}"), gbc, "gbc", bbc, "bbc", 1)
                     + [(4, p1_H), (5, p1_J)])
            P.barrier()
            B.release(m)

            m = B.mark()
            memb = B.T("memb", [128, 2, 1024], BF16)
            memT = B.T("memT", [128, 8, NMEM], BF16)
            mkT = B.T("mkT", [128, 4, NMEM], BF16)
            mv = B.T("mv", [128, 2, 512], BF16)
            qTm = [B.T("qTm", [128, S], BF16) for _ in range(2)]
            sgm = [B.T("sgm", [128, S], BF16) for _ in range(2)]
            accm = [B.T("accm", [128, S], F32) for _ in range(2)]
            denm = [B.T("denm", [128, S], F32) for _ in range(2)]
            sqT = B.T("sqT", [128, S], BF16)
            ab = attn_bufs()
            wmk = w_mem_kv.rearrange("(k p) c -> p k c", p=128)
            B.dma("pool", memb[:], mem[b].rearrange("(t p) d -> p t d", p=128), [], ["memb"])
            B.dma("pool", wst[0][:], wmk[:, :, 0:512], [], ["wst0"])
            B.dma("pool", wst[1][:], wmk[:, :, 512:1024], [], ["wst1"])
            for t in range(2):
                for k in range(8):
                    B.tr(psT[:, k * 128:(k + 1) * 128], memb[:, t, k * 128:(k + 1) * 128], ident[:], ["memb", "ident"], ["psT"])
                B.cp("act", memT[:, :, t * 128:(t + 1) * 128], psT[:].rearrange("p (k t) -> p k t", k=8), ["psT"], ["memT"])
            for h in range(4):
                bank, bn = next_w()
                for k in range(8):
                    B.mm(bank[:, 0:NMEM], wst[0][:, k, h * 128:(h + 1) * 128], memT[:, k, :], k == 0, k == 7,
                         ["wst0", "memT"], [bn])
                B.cp("dve", mkT[:, h, :], bank[:, 0:NMEM], [bn], ["mkT"])
            for t in range(2):
                bank, bn = next_w()
                for k in range(8):
                    B.mm(bank[:, :], memT[:, k, t * 128:(t + 1) * 128], wst[1][:, k, :], k == 0, k == 7, ["wst1", "memT"], [bn])
                B.cp("dve", mv[:, t, :], bank[:, :], [bn], ["mv"])
            load_w_in(0, 5024, 512)
            load_w_in(1, 5536, 512)
            for h in range(4):
                sl = h % 2
                for qb in range(4):
                    bank, bn = proj_fm(lambda k: wst[0][:, k, h * 128:(h + 1) * 128], 8, lambda k: hT[:, k, blk(qb)], 128,
                                       ["wst0", "hT"])
                    B.cp("dve", qTm[sl][:, blk(qb)], bank[:, :], [bn], [f"qTm{sl}"])
                    run_deferred_ab(ab, 1 << 60, maxn=3)
                    bank, bn = proj_fm(lambda k: wst[1][:, k, h * 128:(h + 1) * 128], 8, lambda k: hT[:, k, blk(qb)], 128,
                                       ["wst1", "hT"])
                    B.act(sgm[sl][:, blk(qb)], bank[:, :], AF.Silu, [bn], [f"sgm{sl}"])
                    run_deferred_ab(ab, 1 << 60, maxn=3)
                run_deferred_ab(ab, 1 << 60)
                if h > 0:
                    ssq_mms(sqT, 12 + h - 1)
                attention(ab, KT=lambda kt: mkT[:, h, kt * 128:(kt + 1) * 128], QT=lambda qb: qTm[sl][:, blk(qb)],
                          V=lambda kt: mv[:, kt, h * 128:(h + 1) * 128], vcols=128, nkt=2, r0=0, r1=128, den=("sep",),
                          scale=128.0 ** -0.5, maskfn=None, sg=sgm[sl], sgres=f"sgm{sl}", gvec=gom_t[:, h:h + 1],
                          ychunk=12 + h, sqT=sqT, reads=["mkT", "mv", f"qTm{sl}"],
                          bulk=(accm[sl], f"accm{sl}", denm[sl], f"denm{sl}"))
                bulk_epilogue(ab, accm[sl], f"accm{sl}", 0, 128, 0, sgm[sl], f"sgm{sl}", gom_t[:, h:h + 1], 12 + h, sqT,
                              ab["step"], dtile=denm[sl], dres=f"denm{sl}")
            run_deferred_ab(ab, 1 << 60)
            ssq_mms(sqT, 15)
            P.barrier()
            B.release(m)

            m = B.mark()
            cosB = B.T("cosB", [128, S], F32)
            sinB = B.T("sinB", [128, S], F32)
            wuq = B.T("wuq", [128, 2, 768], BF16)
            wuq_sw = B.T("wuq_sw", [128, 2, 768], BF16)
            wukv = B.T("wukv", [128, 1024], BF16)
            B.dma("pool", wuq[:], w_uq.rearrange("(k p) c -> p k c", p=128), [], ["wuq"])
            B.dma("pool", wukv[:], w_ukv, [], ["wukv"])
            load_w_in(0, 4096, 512)
            load_w_in(1, 4512, 512)
            rope_tables(b, 2, 3, cosB, sinB, "B")
            wkr_sw = B.T("wkr_sw", [128, 8, 96], BF16)
            cqn = B.T("cqn", [128, 2, S], BF16)
            ckvn = B.T("ckvn", [128, S], BF16)
            kr = B.T("kr", [128, S], BF16)
            V2B = B.T("V2B", [128, 16, 192], BF16)
            sqt = [B.T("sqt", [128, 512], BF16) for _ in range(2)]
            R = [B.T("R", [128, 512], F32) for _ in range(2)]
            rt1 = B.T("rt1", [128, 512], F32)
            rt2 = B.T("rt2", [128, 512], F32)
            QTh = B.T("QTh", [128, S], BF16)
            KTh = B.T("KTh", [128, S], BF16)
            sgb = B.T("sgb", [128, S], BF16)
            sqT = B.T("sqT", [128, S], BF16)
            ab = attn_bufs()
            B.memset("pool", wuq_sw[:], 0.0, ["wuq_sw"])
            wuq4 = wuq[:].rearrange("p k (h t) -> p k h t", t=96)
            wsw4 = wuq_sw[:].rearrange("p k (h t) -> p k h t", t=96)
            for kc in range(2):
                B.cp("pool", wsw4[:, kc, :, 64:80], wuq4[:, kc, :, 80:96], ["wuq", "wuq_sw"], ["wuq_sw"])
                B.cp("pool", wsw4[:, kc, :, 80:96], wuq4[:, kc, :, 64:80], ["wuq", "wuq_sw"], ["wuq_sw"])
            B.memset("pool", wkr_sw[:], 0.0, ["wkr_sw"])
            B.cp("pool", wkr_sw[:, :, 64:80], wst[0][:, :, 400:416], ["wst0", "wkr_sw"], ["wkr_sw"])
            B.cp("pool", wkr_sw[:, :, 80:96], wst[0][:, :, 384:400], ["wst0", "wkr_sw"], ["wkr_sw"])
            B.memset("pool", V2B[:, :, 64:65], 1.0, ["V2B"])
            B.memset("pool", V2B[:, :, 65:128], 0.0, ["V2B"])
            rc = 0
            for qb in range(4):
                hrhs = lambda k: hT[:, k, blk(qb)]
                cb = []
                for c in range(2):
                    cb.append(proj_fm(lambda k: wst[0][:, k, c * 128:(c + 1) * 128], 8, hrhs, 128, ["wst0", "hT"]))
                for c in range(2):
                    B.act(sqt[c][:], cb[c][0][:, :], AF.Square, [cb[c][1]], [f"sqt{c}"])
                sbank, sbn = next_w()
                for c in range(2):
                    B.mm(sbank[:, :], ones_bf[:, :], sqt[c][:], c == 0, c == 1, [f"sqt{c}", "ones_bf"], [sbn])
                j = rc % 2
                rc += 1
                B.ts("dve", R[j][:], sbank[:, :], 1.0 / 256, EPS, ALU.mult, ALU.add, [sbn], [f"R{j}"])
                B.act(R[j][:], R[j][:], AF.Sqrt, [f"R{j}"], [f"R{j}"])
                P.add("dve", lambda e, j=j: e.reciprocal(out=R[j][:], in_=R[j][:]), [f"R{j}"], [f"R{j}"])
                for c in range(2):
                    B.stt(cqn[:, c, blk(qb)], cb[c][0][:, :], gcq_t[:, c:c + 1], R[j][:], ALU.mult, ALU.mult,
                          [cb[c][1], "gvec", f"R{j}"], ["cqn"])
                kb, kbn = proj_fm(lambda k: wst[0][:, k, 256:384], 8, hrhs, 128, ["wst0", "hT"])
                B.act(sqt[0][:], kb[:, :], AF.Square, [kbn], ["sqt0"])
                sbank, sbn = next_w()
                B.mm(sbank[:, :], ones_bf[:, :], sqt[0][:], True, True, ["sqt0", "ones_bf"], [sbn])
                j = rc % 2
                rc += 1
                B.ts("dve", R[j][:], sbank[:, :], 1.0 / 128, EPS, ALU.mult, ALU.add, [sbn], [f"R{j}"])
                B.act(R[j][:], R[j][:], AF.Sqrt, [f"R{j}"], [f"R{j}"])
                P.add("dve", lambda e, j=j: e.reciprocal(out=R[j][:], in_=R[j][:]), [f"R{j}"], [f"R{j}"])
                B.stt(ckvn[:, blk(qb)], kb[:, :], gckv_t[:, 0:1], R[j][:], ALU.mult, ALU.mult, [kbn, "gvec", f"R{j}"], ["ckvn"])
                pa, pan = proj_fm(lambda k: wst[0][:, k, 320:416], 8, hrhs, 96, ["wst0", "hT"])
                pb, pbn = proj_fm(lambda k: wkr_sw[:, k, 0:96], 8, hrhs, 96, ["wkr_sw", "hT"])
                B.tt("dve", rt1[64:96, :], pa[64:96, :], cosB[64:96, blk(qb)], ALU.mult, [pan, "cosB"], ["rt1"])
                B.tt("dve", rt2[64:96, :], pb[64:96, :], sinB[64:96, blk(qb)], ALU.mult, [pbn, "sinB"], ["rt2"])
                B.tt("pool", kr[64:96, blk(qb)], rt1[64:96, :], rt2[64:96, :], ALU.add, ["rt1", "rt2"], ["kr"])
            wv3 = wukv[:].rearrange("p (h t) -> p h t", t=128)
            for h in range(8):
                pr = h // 2
                odd = h % 2
                wsel[0] = (0, 1, 4)
                for qb in range(4):
                    pa, pan = proj_fm(lambda k: wuq[:, k, h * 96:(h + 1) * 96], 2, lambda k: cqn[:, k, blk(qb)], 96, ["wuq", "cqn"])
                    pb, pbn = proj_fm(lambda k: wuq_sw[:, k, h * 96:(h + 1) * 96], 2, lambda k: cqn[:, k, blk(qb)], 96,
                                      ["wuq_sw", "cqn"])
                    B.cp("act", QTh[0:64, blk(qb)], pa[0:64, :], [pan], ["QTh"])
                    B.tt("dve", rt1[64:96, :], pa[64:96, :], cosB[64:96, blk(qb)], ALU.mult, [pan, "cosB"], ["rt1"])
                    B.tt("dve", rt2[64:96, :], pb[64:96, :], sinB[64:96, blk(qb)], ALU.mult, [pbn, "sinB"], ["rt2"])
                    B.tt("pool", QTh[64:96, blk(qb)], rt1[64:96, :], rt2[64:96, :], ALU.add, ["rt1", "rt2", "QTh"], ["QTh"])
                    pc, pcn = proj_fm(lambda k: wukv[:, h * 128:h * 128 + 64], 1, lambda k: ckvn[:, blk(qb)], 64, ["wukv", "ckvn"])
                    B.cp("act", KTh[0:64, blk(qb)], pc[0:64, :], [pcn], ["KTh"])
                B.cp("pool", KTh[64:96, :], kr[64:96, :], ["kr", "KTh"], ["KTh"])
                wsel[0] = (0, 1, 2, 3, 4)
                run_deferred_ab(ab, 1 << 60)
                if not odd:
                    if h > 0:
                        ssq_mms(sqT, 8 + pr - 1)
                    for qb in range(4):
                        bank, bn = proj_fm(lambda k: wst[1][:, k, pr * 128:(pr + 1) * 128], 8, lambda k: hT[:, k, blk(qb)], 128,
                                           ["wst1", "hT"])
                        B.act(sgb[:, blk(qb)], bank[:, :], AF.Silu, [bn], ["sgb"])
                    for t4 in range(4):
                        bank, bn = next_w()
                        for tq in range(4):
                            tt_ = t4 * 4 + tq
                            B.mm(bank[:, tq * 128:(tq + 1) * 128], ckvn[:, tt_ * 128:(tt_ + 1) * 128],
                                 wv3[:, 2 * pr:2 * pr + 2, 64:128], True, True, ["ckvn", "wukv"], [bn])
                        bv = bank[:, :].rearrange("p (q e t) -> p q e t", q=4, e=2)
                        B.cp("act", V2B[:, t4 * 4:(t4 + 1) * 4, 0:64], bv[:, :, 0, :], [bn, "V2B"], ["V2B"])
                        B.cp("dve", V2B[:, t4 * 4:(t4 + 1) * 4, 128:192], bv[:, :, 1, :], [bn, "V2B"], ["V2B"])
                if not odd:
                    V = lambda kt: V2B[:, kt, 0:128]
                    vcols, r0, r1, den = 128, 0, 64, ("row", 64)
                else:
                    V = lambda kt: V2B[:, kt, 64:192]
                    vcols, r0, r1, den = 128, 64, 128, ("row", 0)
                attention(ab, KT=lambda kt: KTh[0:96, kt * 128:(kt + 1) * 128], QT=lambda qb: QTh[0:96, blk(qb)],
                          V=V, vcols=vcols, nkt=16, r0=r0, r1=r1, den=den, scale=96.0 ** -0.5, maskfn=None,
                          sg=sgb, sgres="sgb", gvec=gob_t[r0:r1, pr:pr + 1], ychunk=8 + pr, sqT=sqT,
                          reads=["V2B", "QTh", "KTh"], flush=False)
            run_deferred_ab(ab, 1 << 60)
            ssq_mms(sqT, 11)
            P.barrier()
            B.release(m)

            m = B.mark()
            cosA = B.T("cosA", [128, S], F32)
            sinA = B.T("sinA", [128, S], F32)
            maskT = B.T("maskT", [128, MASK_W + 128], BF16)
            B.dma("pool", maskT[:], maskd, [], ["maskT"])
            for i in range(4):
                load_w_in(0, 1024 * i, 128, dst0=128 * i)
            rope_tables(b, 0, 1, cosA, sinA, "A")
            V3 = B.T("V3", [128, 16, 192], BF16)
            O3 = [B.T("O3s", [128, S], F32) for _ in range(2)]
            wswq = B.T("wswq", [128, 8, 128], BF16)
            wswk = B.T("wswk", [128, 8, 128], BF16)
            QTc = B.T("QTc", [128, S], BF16)
            KTc = B.T("KTc", [128, S], BF16)
            KTo = B.T("KTo", [128, S], BF16)
            V2A = B.T("V2A", [128, 16, 192], BF16)
            sga = [B.T("sga", [128, S], BF16) for _ in range(2)]
            rt1 = B.T("rt1", [128, 512], F32)
            rt2 = B.T("rt2", [128, 512], F32)
            sqT = B.T("sqT", [128, S], BF16)
            ab = dict(PT=[B.T("PT", [128, 512], BF16) for _ in range(4)],
                      BCS=[B.T("BCS", [128, 512], F32) for _ in range(2)],
                      ON=[B.T("ON", [128, 512], F32) for _ in range(2)],
                      step=0, epi=0, defer=[], bj=0)
            B.memset("pool", KTc[64:128, :], 0.0, ["KTc"])
            B.memset("pool", KTo[0:64, :], 0.0, ["KTc"])
            B.memset("pool", wswq[:], 0.0, ["wswq"])
            B.memset("pool", wswk[:], 0.0, ["wswk"])
            B.memset("pool", V2A[:, :, 64:65], 1.0, ["V2A"])
            B.memset("pool", V2A[:, :, 65:128], 0.0, ["V2A"])
            B.memset("pool", V3[:, :, 64:65], 1.0, ["V3"])
            B.memset("pool", V3[:, :, 65:128], 0.0, ["V3"])

            def maskfn(kt, qb):
                dmin = 128 * kt - 512 * qb - 511
                dmax = 128 * kt + 127 - 512 * qb
                if dmin > 256 or dmax < -256:
                    return None
                off = MASK_X0 - 128 * kt + 512 * qb
                assert 0 <= off and off + 512 <= MASK_W
                return maskT[:, off:off + 512]

            for c in range(8):
                sl = c % 2
                wb = wst[sl]
                sgc = sga[sl]
                sgn = f"sga{sl}"
                if c + 1 < 8:
                    for i in range(4):
                        load_w_in(1 - sl, 1024 * i + (c + 1) * 128, 128, dst0=128 * i)
                wq4 = wb[:, :, 0:128].rearrange("p k (e t) -> p k e t", e=2)
                wk4 = wb[:, :, 128:256].rearrange("p k (e t) -> p k e t", e=2)
                sq4 = wswq[:].rearrange("p k (e t) -> p k e t", e=2)
                sk4 = wswk[:].rearrange("p k (e t) -> p k e t", e=2)
                B.cp("pool", sq4[:, :, :, 0:8], wq4[:, :, :, 8:16], [f"wst{sl}", "wswq"], ["wswq"])
                B.cp("pool", sq4[:, :, :, 8:16], wq4[:, :, :, 0:8], [f"wst{sl}", "wswq"], ["wswq"])
                B.cp("pool", sk4[:, :, :, 0:8], wk4[:, :, :, 8:16], [f"wst{sl}", "wswk"], ["wswk"])
                B.cp("pool", sk4[:, :, :, 8:16], wk4[:, :, :, 0:8], [f"wst{sl}", "wswk"], ["wswk"])
                for qb in range(4):
                    hrhs = lambda k: hT[:, k, blk(qb)]
                    for (dst, dname, c0, wsw, wswn) in ((QTc, "QTc", 0, wswq, "wswq"), (KTc, "KTc", 128, wswk, "wswk")):
                        pa, pan = proj_fm(lambda k: wb[:, k, c0:c0 + 128], 8, hrhs, 128, [f"wst{sl}", "hT"])
                        pb, pbn = proj_fm(lambda k: wsw[:, k, :], 8, hrhs, 128, [wswn, "hT"])
                        B.tt("dve", rt1[:], pa[:, :], cosA[:, blk(qb)], ALU.mult, [pan, "cosA"], ["rt1"])
                        B.tt("dve", rt2[:], pb[:, :], sinA[:, blk(qb)], ALU.mult, [pbn, "sinA"], ["rt2"])
                        if dname == "QTc":
                            B.tt("pool", dst[:, blk(qb)], rt1[:], rt2[:], ALU.add, ["rt1", "rt2"], [dname])
                        else:
                            B.tt("pool", KTc[0:64, blk(qb)], rt1[0:64, :], rt2[0:64, :], ALU.add, ["rt1", "rt2"], [dname])
                            B.tt("pool", KTo[64:128, blk(qb)], rt1[64:128, :], rt2[64:128, :], ALU.add, ["rt1", "rt2"], [dname])
                        run_deferred_ab(ab, 1 << 60, maxn=3)
                    bank, bn = proj_fm(lambda k: wb[:, k, 384:512], 8, hrhs, 128, [f"wst{sl}", "hT"])
                    B.act(sgc[:, blk(qb)], bank[:, :], AF.Silu, [bn], [sgn])
                for t4 in range(4):
                    bank, bn = next_w()
                    for tq in range(4):
                        tt_ = t4 * 4 + tq
                        for k in range(8):
                            B.mm(bank[:, tq * 128:(tq + 1) * 128], hT[:, k, tt_ * 128:(tt_ + 1) * 128], wb[:, k, 256:384],
                                 k == 0, k == 7, [f"wst{sl}", "hT"], [bn])
                    bv = bank[:, :].rearrange("p (q e t) -> p q e t", q=4, e=2)
                    B.cp("act", V2A[:, t4 * 4:(t4 + 1) * 4, 0:64], bv[:, :, 0, :], [bn, "V2A"], ["V2A"])
                    B.cp("dve", V2A[:, t4 * 4:(t4 + 1) * 4, 128:192], bv[:, :, 1, :], [bn, "V2A"], ["V2A"])
                for r4 in range(4):
                    bank, bn = next_w()
                    for tq in range(4):
                        r = r4 * 4 + tq
                        for k in range(8):
                            B.mm(bank[:, tq * 128:(tq + 1) * 128], hT[:, k, r:r + 16 * 127 + 1:16], wb[:, k, 256:384],
                                 k == 0, k == 7, [f"wst{sl}", "hT"], [bn])
                    bv = bank[:, :].rearrange("p (q e t) -> p q e t", q=4, e=2)
                    B.cp("act", V3[:, r4 * 4:(r4 + 1) * 4, 0:64], bv[:, :, 0, :], [bn, "V3"], ["V3"])
                    B.cp("dve", V3[:, r4 * 4:(r4 + 1) * 4, 128:192], bv[:, :, 1, :], [bn, "V3"], ["V3"])
                run_deferred_ab(ab, 1 << 60)
                if c > 0:
                    ssq_mms(sqT, c - 1)
                for odd in range(2):
                    if not odd:
                        V = lambda kt: V2A[:, kt, 0:128]
                        vcols, r0, r1, den = 128, 0, 64, ("row", 64)
                    else:
                        V = lambda kt: V2A[:, kt, 64:192]
                        vcols, r0, r1, den = 128, 64, 128, ("row", 0)
                    KTx = KTo if odd else KTc
                    acc, accres = O3[odd], f"O3s{odd}"
                    attention_p3(ab, KTx, QTc, V3, odd, maskT[:, MASK_W:MASK_W + 128], (acc, accres), ["QTc", "KTc"])
                    attention(ab, KT=lambda kt: KTx[:, kt * 128:(kt + 1) * 128], QT=lambda qb: QTc[:, blk(qb)],
                              V=V, vcols=vcols, nkt=16, r0=r0, r1=r1, den=den, scale=0.125, maskfn=maskfn,
                              sg=sgc, sgres=sgn, gvec=goa_t[r0:r1, c:c + 1], ychunk=c, sqT=sqT,
                              reads=["V2A", "QTc", "KTc"], bulk=(acc, accres))
                    bulk_epilogue(ab, acc, accres, r0, r1, den[1], sgc, sgn, goa_t[r0:r1, c:c + 1], c, sqT,
                                  ab["step"] + (17 if not odd else 0))
            run_deferred_ab(ab, 1 << 60)
            ssq_mms(sqT, 7)
            P.barrier()
            B.release(m)

            m = B.mark()
            wo = B.T("wo", [128, 16, D], BF16)
            gpc = B.T("gpc", [128, 1024], F32)
            bpc = B.T("bpc", [128, 1024], F32)
            hf = [B.T("hf", [128, 1024], F32) for _ in range(3)]
            acc = [B.T("acc", [128, 1024], F32) for _ in range(3)]
            rg = [B.T("rg", [128, 4], F32) for _ in range(3)]
            L = ln_alloc(4, 2)
            wo_v = w_out.rearrange("(k p) c -> p k c", p=128)
            for q4 in range(4):
                B.dma("pool", wo[:, q4 * 4:(q4 + 1) * 4, :], wo_v[:, q4 * 4:(q4 + 1) * 4, :], [], [f"wo{q4}"])
            B.dma("sp", gpc[:], g_post.partition_broadcast(128), [], ["gpc"])
            B.dma("sp", bpc[:], b_post.partition_broadcast(128), [], ["bpc"])
            groups = ((0, 8, 1024.0), (8, 12, 512.0), (12, 16, 512.0))

            def tsl_(t):
                return slice(t * 128, (t + 1) * 128)

            def f_load(t):
                B.dma("sp", hf[t % 3][:], hscr[b, tsl_(t), :], [], [f"hf{t % 3}"])

            def f_mm(t):
                for n in range(2):
                    for gi, (c0, c1, width) in enumerate(groups):
                        bi = n * 3 + gi
                        for c in range(c0, c1):
                            B.mm(W[bi][:, :], yT[:, c, tsl_(t)], wo[:, c, blk(n)], c == c0, c == c1 - 1, ["yT", f"wo{c // 4}"],
                                 [f"{WN[bi]}"])

            def f_rg(t):
                k = t % 3
                for gi, (c0, c1, width) in enumerate(groups):
                    P.add("dve", lambda e, gi=gi, c0=c0, c1=c1: e.reduce_sum(
                        out=rg[k][:, gi:gi + 1], in_=ssqP[:, t * 16 + c0:t * 16 + c1], axis=mybir.AxisListType.X),
                        ["ssqP"], [f"rg{k}"])
                    B.ts("dve", rg[k][:, gi:gi + 1], rg[k][:, gi:gi + 1], 1.0 / width, EPS, ALU.mult, ALU.add,
                         [f"rg{k}"], [f"rg{k}"])
                B.tt("pool", rg[k][:, 0:3], rg[k][:, 0:3], mhalf[:, 0:3], ALU.pow, [f"rg{k}", "mhalf"], [f"rg{k}"])

            def f_stt(t):
                k = t % 3
                for n in range(2):
                    for gi in range(3):
                        bi = n * 3 + gi
                        src = hf[k][:, blk(n)] if gi == 0 else acc[k][:, blk(n)]
                        srcn = f"hf{k}" if gi == 0 else f"acc{k}"
                        B.stt(acc[k][:, blk(n)], W[bi][:, :], rg[k][:, gi:gi + 1], src, ALU.mult, ALU.add,
                              [WN[bi], f"rg{k}", srcn], [f"acc{k}"])

            def f_store(t):
                B.dma("sp", out[b, tsl_(t), :], L["t1"][t % 3][:], [f"lt1_{t % 3}"], [f"out{t % 3}"])

            pipeline(16, [(0, f_load), (0, f_mm), (0, f_rg), (1, f_stt)]
                     + ln_stages(L, lambda t: (acc[t % 3][:], f"acc{t % 3}"), gpc, "gpc", bpc, "bpc", 2)
                     + [(5, f_store)])
            P.barrier()
            B.release(m)

        stats = P.emit(esems, dsems)
    return nc, stats


def _host_consts():
    cst = np.zeros((128, 8), np.float32)
    for p in range(128):
        j = p % 64
        if j < 16:
            cst[p, 0] = THETA ** (-(2.0 * (j % 8)) / 16.0)
            cst[p, 1] = -1.0 if j < 8 else 1.0
        else:
            cst[p, 0] = 0.0
            cst[p, 1] = 1.0
        if 64 <= p < 96:
            jj = p - 64
            cst[p, 2] = THETA ** (-(2.0 * (jj % 16)) / 32.0)
            cst[p, 3] = -1.0 if jj < 16 else 1.0
        else:
            cst[p, 2] = 0.0
            cst[p, 3] = 1.0
    kl = np.arange(128)[:, None]
    xx = np.arange(MASK_W)[None, :]
    d = kl - xx + MASK_X0
    f = ((np.abs(d) <= 64).astype(np.float32)
         + ((d % 4 == 0) & (np.abs(d) <= 256)).astype(np.float32))
    mb = np.where(f > 0, np.log(np.maximum(f, 1.0)) / 0.125, -320.0)
    d3 = np.arange(128)[:, None] - np.arange(128)[None, :]
    m3 = np.where(np.abs(d3) <= 64, 0.0, -320.0)
    return cst, np.concatenate([mb, m3], axis=1).astype(np.float32)


_CACHE = {}


def kernel(x, mem, positions, g_emb, b_emb, w_in, g_cq, g_ckv, w_uq, w_ukv, w_mem_kv,
           g_out_a, g_out_b, g_out_m, w_out, g_post, b_post):
    f32 = lambda a: np.ascontiguousarray(np.asarray(a), dtype=np.float32)
    if "nc" not in _CACHE:
        _CACHE["nc"] = build_program()[0]
    nc = _CACHE["nc"]
    cst, maskd = _host_consts()
    x = f32(x)
    mem = f32(mem)
    positions = np.ascontiguousarray(np.asarray(positions), dtype=np.int32)
    shared = dict(
        g_emb=f32(g_emb), b_emb=f32(b_emb), w_in=f32(w_in).reshape(D, D_IN), g_cq=f32(g_cq).reshape(256),
        g_ckv=f32(g_ckv).reshape(128), w_uq=f32(w_uq).reshape(256, 768), w_ukv=f32(w_ukv).reshape(128, 1024),
        w_mem_kv=f32(w_mem_kv).reshape(D, 1024), g_out_a=f32(g_out_a).reshape(1024), g_out_b=f32(g_out_b).reshape(512),
        g_out_m=f32(g_out_m).reshape(512), w_out=f32(w_out).reshape(2048, D), g_post=f32(g_post).reshape(D),
        b_post=f32(b_post).reshape(D), cst=cst, maskd=maskd)
    in_maps = []
    for c in range(N_CORES):
        d = dict(shared)
        d["x"] = x[c * NB:(c + 1) * NB]
        d["mem"] = mem[c * NB:(c + 1) * NB]
        d["positions"] = positions[c * NB:(c + 1) * NB]
        in_maps.append(d)
    res = run_bass_kernel_spmd(nc, in_maps, core_ids=list(range(N_CORES)))
    return np.concatenate([r["out"] for r in res.results], axis=0)
```

```python
import numpy as np
from contextlib import ExitStack

import concourse.bass as bass
import concourse.mybir as mybir
from concourse.bass_utils import run_bass_kernel_spmd

F32 = mybir.dt.float32
BF16 = mybir.dt.bfloat16
I32 = mybir.dt.int32
ALU = mybir.AluOpType
AF = mybir.ActivationFunctionType

N_CORES = 8
NB = 2
S = 2048
D = 1024
D_IN = 6048
NMEM = 256
EPS = 1e-5
ALPHA = 2.0 ** 0.25
THETA = 500000.0
TWO_PI = float(2 * np.pi)
C1 = 6.28125
C2 = float(2 * np.pi - 6.28125)
MASK_W = 1408
MASK_X0 = 640

ENGS = ("pe", "act", "dve", "pool", "sp")


class Op:
    __slots__ = ("eng", "fn", "deps", "signal", "sigval", "is_dma", "dsem", "dval", "dprev",
                 "epoch", "esem")

    def __init__(self, eng, fn, is_dma):
        self.eng = eng
        self.fn = fn
        self.deps = {}
        self.signal = False
        self.sigval = 0
        self.is_dma = is_dma
        self.dsem = None
        self.dval = 0
        self.dprev = 0
        self.epoch = 0
        self.esem = None


class Prog:
    N_EPOCH_SEMS = 6
    N_DMA_SEMS = 28
    N_SW_SEMS = 10

    def __init__(self, nc):
        self.nc = nc
        self.items = {e: [] for e in ENGS}
        self.last_writer = {}
        self.readers = {}
        self.epoch = 0
        self.barriers = []
        self.dma_count = 0
        self.sw_count = 0
        self.dma_sem_counts = [0] * self.N_DMA_SEMS
        self.epoch_ops = {e: [] for e in ENGS}

    def add(self, eng, fn, reads=(), writes=(), dma=False):
        op = Op(eng, fn, dma)
        op.epoch = self.epoch
        for r in reads:
            w = self.last_writer.get(r)
            if w is not None:
                op.deps[w] = True
            lst = self.readers.setdefault(r, [])
            if not dma and eng in ("pe", "act", "dve"):
                lst[:] = [o for o in lst if o.is_dma or o.eng != eng]
            lst.append(op)
        for r in writes:
            w = self.last_writer.get(r)
            if w is not None and w is not op:
                op.deps.setdefault(w, False)
            for rd in self.readers.get(r, ()):
                if rd is not op:
                    op.deps.setdefault(rd, False)
            self.last_writer[r] = op
            self.readers[r] = []
        if dma:
            if eng == "pool":
                k = self.N_DMA_SEMS - self.N_SW_SEMS + (self.sw_count % self.N_SW_SEMS)
                self.sw_count += 1
            else:
                k = self.dma_count % (self.N_DMA_SEMS - self.N_SW_SEMS)
                self.dma_count += 1
            op.dsem = k
            op.dprev = 16 * self.dma_sem_counts[k]
            self.dma_sem_counts[k] += 1
            op.dval = 16 * self.dma_sem_counts[k]
        self.items[eng].append(op)
        self.epoch_ops[eng].append(op)
        return op

    def barrier(self):
        lasts = {}
        for e in ENGS:
            ops = [o for o in self.epoch_ops[e] if not o.is_dma]
            lasts[e] = ops[-1] if ops else None
            if lasts[e] is not None:
                lasts[e].signal = True
        self.barriers.append((lasts, list(self.dma_sem_counts)))
        for e in ENGS:
            self.items[e].append(("barrier", len(self.barriers) - 1))
            self.epoch_ops[e] = []
        self.last_writer = {}
        self.readers = {}
        self.epoch += 1

    def emit(self, esems, dsems):
        nc = self.nc
        for e in ENGS:
            for it in self.items[e]:
                if isinstance(it, tuple):
                    continue
                for d, raw in it.deps.items():
                    if d.is_dma or d.epoch != it.epoch:
                        continue
                    if d.eng == it.eng and not it.is_dma:
                        if it.eng == "pe" or not raw:
                            continue
                    d.signal = True
        cum = {e: [0] * self.N_EPOCH_SEMS for e in ENGS}
        for e in ENGS:
            for it in self.items[e]:
                if isinstance(it, tuple) or it.is_dma:
                    continue
                k = it.epoch % self.N_EPOCH_SEMS
                it.esem = esems[e][k]
                if it.signal:
                    cum[e][k] += 1
                    it.sigval = cum[e][k]
        stats = {e: [0, 0] for e in ENGS}

        def run_engine(e, eng):
            waited = {}

            def wait(sem, val):
                if val <= 0:
                    return
                key = id(sem)
                if waited.get(key, 0) >= val:
                    return
                eng.wait_ge(sem, val)
                waited[key] = val
                stats[e][1] += 1

            for it in self.items[e]:
                if isinstance(it, tuple):
                    lasts, dcounts = self.barriers[it[1]]
                    for e2 in ENGS:
                        lo = lasts[e2]
                        if lo is not None and e2 != e:
                            wait(lo.esem, lo.sigval)
                    for k, c in enumerate(dcounts):
                        wait(dsems[k], 16 * c)
                    continue
                op = it
                for d, raw in op.deps.items():
                    if d.epoch != op.epoch:
                        continue
                    if d.is_dma:
                        wait(dsems[d.dsem], d.dval)
                        continue
                    if d.eng == op.eng and not op.is_dma:
                        if op.eng == "pe" or not raw:
                            continue
                    wait(d.esem, d.sigval)
                if op.is_dma:
                    wait(dsems[op.dsem], op.dprev)
                ins = op.fn(eng)
                stats[e][0] += 1
                if op.is_dma:
                    ins.then_inc(dsems[op.dsem], 16)
                elif op.signal:
                    ins.then_inc(op.esem, 1)
            if e == "sp":
                for k, c in enumerate(self.dma_sem_counts):
                    wait(dsems[k], 16 * c)

        with nc.Block() as block:
            @block.tensor
            def _(eng):
                run_engine("pe", eng)

            @block.scalar
            def _(eng):
                run_engine("act", eng)

            @block.vector
            def _(eng):
                run_engine("dve", eng)

            @block.gpsimd
            def _(eng):
                run_engine("pool", eng)

            @block.sync
            def _(eng):
                run_engine("sp", eng)
        return stats


class Builder:
    SB_BASE = 16512
    SB_LIMIT = 229344

    def __init__(self, nc):
        self.nc = nc
        self.P = Prog(nc)
        self.cur = self.SB_BASE
        self.uid = 0
        self.cache = {}

    def T(self, name, shape, dt):
        n = 1
        for s in shape[1:]:
            n *= s
        nbytes = n * mybir.dt.size(dt)
        nbytes = (nbytes + 31) // 32 * 32
        assert self.cur + nbytes <= self.SB_LIMIT, (name, self.cur, nbytes)
        key = (self.cur, tuple(shape), str(dt))
        t = self.cache.get(key)
        if t is None:
            self.uid += 1
            t = self.nc.alloc_sbuf_tensor_at(f"{name}_{self.uid}", shape, dt, offset=self.cur)
            self.cache[key] = t
        self.cur += nbytes
        return t

    def mark(self):
        return self.cur

    def release(self, m):
        self.cur = m

    def mm(self, out, lhsT, rhs, start, stop, reads, writes):
        self.P.add("pe", lambda e: e.matmul(out, lhsT=lhsT, rhs=rhs, start=start, stop=stop), reads, writes)

    def tr(self, out, in_, ident, reads, writes):
        self.P.add("pe", lambda e: e.transpose(out=out, in_=in_, identity=ident), reads, writes)

    def act(self, out, in_, func, reads, writes, scale=None, bias=None):
        kw = {}
        if scale is not None:
            kw["scale"] = scale
        if bias is not None:
            kw["bias"] = bias
        self.P.add("act", lambda e: e.activation(out=out, in_=in_, func=func, **kw), reads, writes)

    def tt(self, eng, out, in0, in1, op, reads, writes):
        self.P.add(eng, lambda e: e.tensor_tensor(out=out, in0=in0, in1=in1, op=op), reads, writes)

    def ts(self, eng, out, in0, s1, s2, op0, op1, reads, writes):
        if s2 is None:
            self.P.add(eng, lambda e: e.tensor_scalar(out=out, in0=in0, scalar1=s1, scalar2=None, op0=op0), reads, writes)
        else:
            self.P.add(eng, lambda e: e.tensor_scalar(out=out, in0=in0, scalar1=s1, scalar2=s2, op0=op0, op1=op1), reads, writes)

    def stt(self, out, in0, scalar, in1, op0, op1, reads, writes):
        self.P.add("dve", lambda e: e.scalar_tensor_tensor(out=out, in0=in0, scalar=scalar, in1=in1, op0=op0, op1=op1),
                   reads, writes)

    def cp(self, eng, out, in_, reads, writes):
        if eng == "act":
            self.P.add("act", lambda e: e.activation(out=out, in_=in_, func=AF.Copy), reads, writes)
        else:
            self.P.add(eng, lambda e: e.tensor_copy(out=out, in_=in_), reads, writes)

    def memset(self, eng, ap, val, writes):
        self.P.add(eng, lambda e: e.memset(ap, val), (), writes)

    def dma(self, q, out, in_, reads, writes, slow=False):
        if slow:
            self.P.add(q, lambda e: e.dma_start(out=out, in_=in_, allow_slow_non_contiguous=True), reads, writes, dma=True)
        else:
            self.P.add(q, lambda e: e.dma_start(out=out, in_=in_), reads, writes, dma=True)


def blk(i, n=512):
    return slice(i * n, (i + 1) * n)


def build_program():
    nc = bass.Bass("TRN2", target_bir_lowering=False)

    def din(name, shape, dt=F32):
        return nc.dram_tensor(name, shape, dt, kind="ExternalInput").ap()

    x = din("x", [NB, S, D])
    mem = din("mem", [NB, NMEM, D])
    pos = din("positions", [NB, S], I32)
    g_emb = din("g_emb", [D])
    b_emb = din("b_emb", [D])
    w_in = din("w_in", [D, D_IN])
    g_cq = din("g_cq", [256])
    g_ckv = din("g_ckv", [128])
    w_uq = din("w_uq", [256, 768])
    w_ukv = din("w_ukv", [128, 1024])
    w_mem_kv = din("w_mem_kv", [D, 1024])
    g_out_a = din("g_out_a", [1024])
    g_out_b = din("g_out_b", [512])
    g_out_m = din("g_out_m", [512])
    w_out = din("w_out", [2048, D])
    g_post = din("g_post", [D])
    b_post = din("b_post", [D])
    cst = din("cst", [128, 8])
    maskd = din("maskd", [128, MASK_W + 128])
    out = nc.dram_tensor("out", [NB, S, D], F32, kind="ExternalOutput").ap()
    hscr = nc.dram_tensor("hscr", [NB, S, D], F32, kind="Internal").ap()

    w_in_v = w_in.rearrange("(k p) c -> p k c", p=128)

    with ExitStack() as st:
        esems = {e: [st.enter_context(nc.semaphore(f"s_{e}_{i}")) for i in range(Prog.N_EPOCH_SEMS)]
                 for e in ENGS}
        dsems = [st.enter_context(nc.semaphore(f"d_{i}")) for i in range(Prog.N_DMA_SEMS)]
        psT = st.enter_context(nc.psum_tensor("psT", [128, 1024], BF16))
        ssqP = st.enter_context(nc.psum_tensor("ssqP", [128, 512], F32))
        W = [st.enter_context(nc.psum_tensor(f"W{i}", [128, 512], F32)) for i in range(6)]
        WN = [f"W{i}" for i in range(6)]

        B = Builder(nc)
        P = B.P

        hT = B.T("hT", [128, 8, S], BF16)
        yT = B.T("yT", [128, 16, S], BF16)
        ident = B.T("ident", [128, 128], BF16)
        ones_bf = B.T("ones_bf", [128, 128], BF16)
        onesf = B.T("onesf", [128, 128], F32)
        sel_o = B.T("sel_o", [128, 128], F32)
        mhalf = B.T("mhalf", [128, 512], F32)
        cst_t = B.T("cst", [128, 8], F32)
        gcq_t = B.T("gcq", [128, 2], F32)
        gckv_t = B.T("gckv", [128, 1], F32)
        goa_t = B.T("goa", [128, 8], F32)
        gob_t = B.T("gob", [128, 4], F32)
        gom_t = B.T("gom", [128, 4], F32)
        wst = [B.T(f"wst{i}", [128, 8, 512], BF16) for i in range(2)]

        B.memset("pool", ident[:], 0.0, ["ident"])
        P.add("pool", lambda e: e.affine_select(out=ident[:], in_=ident[:], compare_op=ALU.not_equal, fill=1.0,
                                                base=0, pattern=[[-1, 128]], channel_multiplier=1),
              ["ident"], ["ident"])
        B.memset("pool", ones_bf[:], 1.0, ["ones_bf"])
        B.memset("pool", onesf[:], 1.0, ["onesf"])
        B.memset("pool", sel_o[:, 0:64], 0.0, ["sel_o"])
        B.memset("pool", sel_o[:, 64:128], 1.0, ["sel_o"])
        B.memset("pool", mhalf[:], -0.5, ["mhalf"])
        B.dma("sp", cst_t[:], cst, [], ["cst"])
        B.dma("sp", gcq_t[:], g_cq.rearrange("(c p) -> p c", p=128), [], ["gvec"], slow=True)
        B.dma("sp", gckv_t[:], g_ckv.rearrange("(c p) -> p c", p=128), [], ["gvec"], slow=True)
        B.dma("sp", goa_t[:], g_out_a.rearrange("(c p) -> p c", p=128), [], ["gvec"], slow=True)
        B.dma("sp", gob_t[:], g_out_b.rearrange("(c p) -> p c", p=128), [], ["gvec"], slow=True)
        B.dma("sp", gom_t[:], g_out_m.rearrange("(c p) -> p c", p=128), [], ["gvec"], slow=True)
        P.barrier()

        base_mark = B.mark()

        def pipeline(T, stages):
            maxlag = max(l for l, _ in stages)
            stages = sorted(stages, key=lambda lf: -lf[0])
            for tau in range(T + maxlag):
                for lag, fn in stages:
                    t = tau - lag
                    if 0 <= t < T:
                        fn(t)

        def ln_alloc(nsm, nbig):
            return dict(st=[B.T("lst", [128, 2, 6], F32) for _ in range(nsm)],
                        mv=[B.T("lmv", [128, 2], F32) for _ in range(nsm)],
                        rs=[B.T("lrs", [128, 1], F32) for _ in range(nsm)],
                        nmr=[B.T("lnm", [128, 1], F32) for _ in range(nsm)],
                        xn=[B.T("lxn", [128, 1024], F32) for _ in range(nbig)],
                        t1=[B.T("lt1", [128, 1024], F32) for _ in range(nbig + 1)])

        def ln_stages(L, src_fn, gbc, gres, bbc, bres, lag0):
            nsm, nxn, nt1 = len(L["st"]), len(L["xn"]), len(L["t1"])

            def sB(t):
                k = t % nsm
                src, sres = src_fn(t)
                for hh in range(2):
                    P.add("dve", lambda e, hh=hh: e.bn_stats(out=L["st"][k][:, hh, :], in_=src[:, hh * 512:(hh + 1) * 512]),
                          [sres], [f"lst{k}"])
                P.add("dve", lambda e: e.bn_aggr(out=L["mv"][k][:], in_=L["st"][k][:].rearrange("p a b -> p (a b)")),
                      [f"lst{k}"], [f"lmv{k}"])
                B.ts("dve", L["rs"][k][:], L["mv"][k][:, 1:2], EPS, None, ALU.add, None, [f"lmv{k}"], [f"lrs{k}"])

            def sC(t):
                k = t % nsm
                B.tt("pool", L["rs"][k][:], L["rs"][k][:], mhalf[:, 0:1], ALU.pow, [f"lrs{k}", "mhalf"], [f"lrs{k}"])

            def sD(t):
                k = t % nsm
                B.stt(L["nmr"][k][:], L["mv"][k][:, 0:1], -1.0, L["rs"][k][:], ALU.mult, ALU.mult,
                      [f"lmv{k}", f"lrs{k}"], [f"lnm{k}"])

            def sE(t):
                k = t % nsm
                src, sres = src_fn(t)
                B.act(L["xn"][t % nxn][:], src, AF.Identity, [sres, f"lnm{k}", f"lrs{k}"], [f"lxn{t % nxn}"],
                      scale=L["rs"][k][:, 0:1], bias=L["nmr"][k][:, 0:1])

            def sF(t):
                B.tt("dve", L["t1"][t % nt1][:], L["xn"][t % nxn][:], gbc[:], ALU.mult, [f"lxn{t % nxn}", gres], [f"lt1_{t % nt1}"])

            def sG(t):
                B.tt("pool", L["t1"][t % nt1][:], L["t1"][t % nt1][:], bbc[:], ALU.add, [f"lt1_{t % nt1}", bres], [f"lt1_{t % nt1}"])

            return [(lag0, sB), (lag0, sC), (lag0 + 1, sD), (lag0 + 1, sE), (lag0 + 2, sF), (lag0 + 2, sG)]

        wcount = [0]
        wsel = [(0, 1, 2, 3, 4)]

        def next_w():
            sel = wsel[0]
            i = sel[wcount[0] % len(sel)]
            wcount[0] += 1
            return W[i], WN[i]

        def proj_fm(lhs_fn, K, rhs_fn, nrows, reads):
            bank, bn = next_w()
            for k in range(K):
                B.mm(bank[0:nrows, :], lhs_fn(k), rhs_fn(k), k == 0, k == K - 1, reads, [bn])
            return bank, bn

        def rope_tables(b, fr_col, sgn_col, cosT, sinT, tag):
            m = B.mark()
            posb = B.T("posb", [128, S], I32)
            ang = B.T("ang", [128, S], F32)
            ki = B.T("ki", [128, S], I32)
            kf = B.T("kf", [128, S], F32)
            a = B.T("a", [128, S], F32)
            B.dma("sp", posb[:], pos[b].partition_broadcast(128), [], ["posb"])
            B.cp("dve", ang[:], posb[:], ["posb"], ["ang"])
            B.ts("dve", ang[:], ang[:], cst_t[:, fr_col:fr_col + 1], None, ALU.mult, None, ["ang", "cst"], ["ang"])
            for which in range(2):
                if which == 1:
                    B.ts("dve", ang[:], ang[:], float(np.pi / 2), None, ALU.add, None, ["ang"], ["ang"])
                B.ts("dve", ki[:], ang[:], float(1.0 / TWO_PI), None, ALU.mult, None, ["ang"], ["ki"])
                B.cp("dve", kf[:], ki[:], ["ki"], ["kf"])
                B.stt(a[:], kf[:], -C1, ang[:], ALU.mult, ALU.add, ["kf", "ang"], ["a"])
                B.stt(a[:], kf[:], -C2, a[:], ALU.mult, ALU.add, ["kf", "a"], ["a"])
                B.ts("dve", a[:], a[:], float(-np.pi), float(np.pi), ALU.max, ALU.min, ["a"], ["a"])
                if which == 0:
                    B.act(sinT[:], a[:], AF.Sin, ["a", "cst"], [f"sin{tag}"], scale=cst_t[:, sgn_col:sgn_col + 1])
                else:
                    B.act(cosT[:], a[:], AF.Sin, ["a"], [f"cos{tag}"])
            P.barrier()
            B.release(m)

        def attn_bufs():
            d = {}
            d["PT"] = [B.T("PT", [128, 512], BF16) for _ in range(4)]
            d["RD"] = [B.T("RD", [128, 512], F32) for _ in range(1)]
            d["BCS"] = [B.T("BCS", [128, 512], F32) for _ in range(2)]
            d["ON"] = [B.T("ON", [128, 512], F32) for _ in range(2)]
            for t in d["RD"]:
                B.memset("pool", t[:], 1.0, ["RD0"])
            d["step"] = 0
            d["defer"] = []
            d["bj"] = 0
            d["epi"] = 0
            return d

        def attention(ab, KT, QT, V, vcols, nkt, r0, r1, den, scale, maskfn, sg, sgres, gvec, ychunk, sqT, reads, o3=None, bulk=None, flush=True):
            SB = (0, 1)
            OB = (2, 3)
            steps = []
            for qb in range(4):
                kts = [kt for kt in range(nkt) if maskfn is None or maskfn(kt, qb) is not None]
                rng = {}
                for kt in kts:
                    if maskfn is None:
                        rng[kt] = (0, 512)
                    else:
                        rng[kt] = (max(0, 128 * kt - 256 - 512 * qb), min(512, 128 * kt + 384 - 512 * qb))
                full = [kt for kt in kts if rng[kt] == (0, 512)]
                if maskfn is not None:
                    assert len(full) >= 2
                    kts = [full[0]] + [kt for kt in kts if kt not in (full[0], full[-1])] + [full[-1]]
                for j, kt in enumerate(kts):
                    steps.append((qb, kt, j == 0, j == len(kts) - 1, rng[kt][0], rng[kt][1]))
            base = ab["step"]

            def emit_S(i):
                qb, kt, _, _, c0, c1 = steps[i]
                s = SB[(base + i) % 2]
                if maskfn is None:
                    B.mm(W[s][:, :], KT(kt), QT(qb), True, True, reads, [WN[s]])
                else:
                    B.mm(W[s][:, c0:c1], KT(kt), QT(qb)[:, c0:c1], True, False, reads, [WN[s]])
                    B.mm(W[s][:, c0:c1], ident[:, :], maskfn(kt, qb)[:, c0:c1], False, True, ["ident", "maskT"], [WN[s]])

            def epilogue(qb, ob, cur):
                run_deferred(1 << 60)
                j = ab["epi"] % 2
                ab["epi"] += 1
                rd, bcs, on = ab["RD"][0], ab["BCS"][j], ab["ON"][j]
                if den[0] == "row":
                    dr = den[1]
                    den_ap = W[ob][dr:dr + 1, :]
                    den_res = WN[ob]
                else:
                    dr = 0
                    den_ap = W[4][0:1, :]
                    den_res = WN[4]
                if o3 is None:
                    P.add("dve", lambda e: e.reciprocal(out=rd[dr:dr + 1, :], in_=den_ap), [den_res], ["RD0"])
                else:
                    B.tt("dve", rd[dr:dr + 1, :], den_ap, o3[0][dr:dr + 1, blk(qb)], ALU.add, [den_res, o3[1]], ["RD0"])
                    P.add("dve", lambda e: e.reciprocal(out=rd[dr:dr + 1, :], in_=rd[dr:dr + 1, :]), ["RD0"], ["RD0"])

                def st1():
                    if r0 == 0 and r1 == 64:
                        B.mm(W[5][0:64, :], onesf[dr:dr + 1, 0:64], rd[dr:dr + 1, :], True, True, ["RD0", "onesf"], [WN[5]])
                    elif r0 == 64:
                        B.mm(W[5][0:128, :], sel_o[dr:dr + 1, 0:128], rd[dr:dr + 1, :], True, True, ["RD0", "sel_o"], [WN[5]])
                    else:
                        B.mm(W[5][0:128, :], onesf[dr:dr + 1, 0:128], rd[dr:dr + 1, :], True, True, ["RD0", "onesf"], [WN[5]])

                def st2():
                    B.cp("act", bcs[r0:r1, :], W[5][r0:r1, :], [WN[5]], [f"BCS{j}"])

                def st3():
                    if o3 is None:
                        B.tt("dve", on[r0:r1, :], W[ob][r0:r1, :], bcs[r0:r1, :], ALU.mult, [WN[ob], f"BCS{j}"], [f"ON{j}"])
                    else:
                        B.tt("dve", on[r0:r1, :], W[ob][r0:r1, :], o3[0][r0:r1, blk(qb)], ALU.add, [WN[ob], o3[1]], [f"ON{j}"])
                        B.tt("dve", on[r0:r1, :], on[r0:r1, :], bcs[r0:r1, :], ALU.mult, [f"ON{j}", f"BCS{j}"], [f"ON{j}"])

                def st4():
                    B.act(sqT[r0:r1, blk(qb)], on[r0:r1, :], AF.Square, [f"ON{j}"], ["sqT"])
                    B.stt(yT[r0:r1, ychunk, blk(qb)], on[r0:r1, :], gvec, sg[r0:r1, blk(qb)], ALU.mult, ALU.mult,
                          [f"ON{j}", "gvec", sgres], [f"yT{ychunk}"])

                for dly, fn in ((6, st1), (8, st2), (9, st3), (10, st4)):
                    ab["defer"].append((cur + dly, fn))

            def run_deferred(upto):
                q = ab["defer"]
                while q and q[0][0] <= upto:
                    q.pop(0)[1]()

            def emit_rest(i):
                qb, kt, first, last, c0, c1 = steps[i]
                s = SB[(base + i) % 2]
                pi = (base + i) % 4
                pt = ab["PT"][pi]
                ob = OB[(ab["epi"]) % 2]
                B.act(pt[:, c0:c1], W[s][:, c0:c1], AF.Exp, [WN[s]], [f"PT{pi}"], scale=scale)
                B.mm(W[ob][0:vcols, c0:c1], V(kt), pt[:, c0:c1], first, last, [f"PT{pi}"] + reads, [WN[ob]])
                if den[0] == "sep":
                    B.mm(W[4][0:1, :], ones_bf[:, 0:1], pt[:], first, last, [f"PT{pi}", "ones_bf"], [WN[4]])
                if last:
                    if bulk is None:
                        epilogue(qb, ob, base + i)
                    elif den[0] == "sep":
                        B.cp("dve", bulk[0][:, blk(qb)], W[ob][:, :], [WN[ob]], [bulk[1]])
                        B.cp("act", bulk[2][0:1, blk(qb)], W[4][0:1, :], [WN[4]], [bulk[3]])
                        ab["epi"] += 1
                    else:
                        brows = slice(0, 128) if r0 == 64 else slice(0, 65)
                        B.tt("dve", bulk[0][brows, blk(qb)], W[ob][brows, :], bulk[0][brows, blk(qb)], ALU.add,
                             [WN[ob], bulk[1]], [bulk[1]])
                        ab["epi"] += 1

            n = len(steps)
            emit_S(0)
            for i in range(n):
                if i + 1 < n:
                    emit_S(i + 1)
                emit_rest(i)
                run_deferred(base + i)
            if bulk is None and flush:
                run_deferred(1 << 60)
            ab["step"] = base + n

        def run_deferred_ab(ab, upto, maxn=1 << 30):
            q = ab["defer"]
            n = 0
            while q and q[0][0] <= upto and n < maxn:
                q.pop(0)[1]()
                n += 1

        def bulk_epilogue(ab, acc, accres, r0, r1, dr, sg, sgres, gvec, ychunk, sqT, start, dtile=None, dres=None):
            items = []
            if dtile is None:
                dtile, dres = acc, accres
            for qb in range(4):
                t0 = start + 6 * qb
                j = ab["bj"] % 2
                ab["bj"] += 1
                bcs, on = ab["BCS"][j], ab["ON"][j]

                def f_rec(qb=qb):
                    P.add("dve", lambda e: e.reciprocal(out=dtile[dr:dr + 1, blk(qb)], in_=dtile[dr:dr + 1, blk(qb)]),
                          [dres], [dres])

                def f_bc(qb=qb):
                    if r0 == 0 and r1 == 64:
                        B.mm(W[5][0:64, :], onesf[dr:dr + 1, 0:64], dtile[dr:dr + 1, blk(qb)], True, True, [dres, "onesf"], [WN[5]])
                    elif r0 == 64:
                        B.mm(W[5][0:128, :], sel_o[dr:dr + 1, 0:128], dtile[dr:dr + 1, blk(qb)], True, True, [dres, "sel_o"], [WN[5]])
                    else:
                        B.mm(W[5][0:128, :], onesf[dr:dr + 1, 0:128], dtile[dr:dr + 1, blk(qb)], True, True, [dres, "onesf"], [WN[5]])

                def f_cp(bcs=bcs, j=j):
                    B.cp("act", bcs[r0:r1, :], W[5][r0:r1, :], [WN[5]], [f"BCS{j}"])

                def f_mul(qb=qb, bcs=bcs, on=on, j=j):
                    B.tt("dve", on[r0:r1, :], acc[r0:r1, blk(qb)], bcs[r0:r1, :], ALU.mult, [accres, f"BCS{j}"], [f"ON{j}"])

                def f_fin(qb=qb, on=on, j=j):
                    B.act(sqT[r0:r1, blk(qb)], on[r0:r1, :], AF.Square, [f"ON{j}"], ["sqT"])
                    B.stt(yT[r0:r1, ychunk, blk(qb)], on[r0:r1, :], gvec, sg[r0:r1, blk(qb)], ALU.mult, ALU.mult,
                          [f"ON{j}", "gvec", sgres], [f"yT{ychunk}"])

                items += [(t0, f_rec), (t0 + 6, f_bc), (t0 + 8, f_cp), (t0 + 9, f_mul), (t0 + 10, f_fin)]
            ab["defer"] = sorted(ab["defer"] + items, key=lambda x: x[0])

        def attention_p3(ab, KTx, QTc_, V3, odd, mask3, O3s, reads):
            SB = (0, 1)
            OB = (2, 3)
            rows = slice(0, 128) if odd else slice(0, 65)
            vsl = slice(64, 192) if odd else slice(0, 128)
            base = ab["step"]

            def csl(r):
                return slice(r, r + 16 * 127 + 1, 16)

            def emit_S(r):
                s = SB[(base + r) % 2]
                B.mm(W[s][:, 0:128], KTx[:, csl(r)], QTc_[:, csl(r)], True, False, reads, [WN[s]])
                B.mm(W[s][:, 0:128], ident[:, :], mask3, False, True, ["ident", "maskT"], [WN[s]])

            def emit_rest(r):
                s = SB[(base + r) % 2]
                pi = (base + r) % 4
                pt = ab["PT"][pi]
                ob = OB[ab["epi"] % 2]
                B.act(pt[:, 0:128], W[s][:, 0:128], AF.Exp, [WN[s]], [f"PT{pi}"], scale=0.125)
                q = r % 4
                B.mm(W[ob][0:128, q * 128:(q + 1) * 128], V3[:, r, vsl], pt[:, 0:128], True, True, [f"PT{pi}", "V3"], [WN[ob]])
                if q == 3:
                    b3 = r // 4
                    dst = O3s[0][:].rearrange("p (j r) -> p r j", r=16)[rows, 4 * b3:4 * b3 + 4, :]
                    src = W[ob][rows, :].rearrange("p (q j) -> p q j", q=4)
                    B.cp("dve", dst, src, [WN[ob]], [O3s[1]])
                    ab["epi"] += 1

            emit_S(0)
            for r in range(16):
                if r + 1 < 16:
                    emit_S(r + 1)
                emit_rest(r)
                run_deferred_ab(ab, base + r)
            ab["step"] = base + 16

        def ssq_mms(sqT, chunk):
            for tt_ in range(16):
                col = tt_ * 16 + chunk
                B.mm(ssqP[:, col:col + 1], sqT[:, tt_ * 128:(tt_ + 1) * 128], ones_bf[:, 0:1], True, True,
                     ["sqT", "ones_bf"], ["ssqP"])

        def load_w_in(buf, col0, ncols, dst0=0):
            B.dma("pool", wst[buf][:, :, dst0:dst0 + ncols], w_in_v[:, :, col0:col0 + ncols], [], [f"wst{buf}"])

        for b in range(NB):
            m = B.mark()
            gbc = B.T("gbc", [128, 1024], F32)
            bbc = B.T("bbc", [128, 1024], F32)
            xt = [B.T("xt", [128, 1024], F32) for _ in range(4)]
            hb = [B.T("hb", [128, 1024], BF16) for _ in range(3)]
            ha = [B.T("ha", [128, 1024], F32) for _ in range(3)]
            L = ln_alloc(4, 2)
            B.dma("sp", gbc[:], g_emb.partition_broadcast(128), [], ["gbc"])
            B.dma("sp", bbc[:], b_emb.partition_broadcast(128), [], ["bbc"])

            def tsl_(t):
                return slice(t * 128, (t + 1) * 128)

            def p1_load(t):
                B.dma("sp", xt[t % 4][:], x[b, tsl_(t), :], [], [f"xt{t % 4}"])

            def p1_H(t):
                hfr = f"lt1_{t % 3}"
                hf_ = L["t1"][t % 3]
                B.act(hb[t % 3][:], hf_[:], AF.Copy, [hfr], [f"hb{t % 3}"])
                B.act(ha[t % 3][:], hf_[:], AF.Identity, [hfr], [f"ha{t % 3}"], scale=ALPHA)
                B.dma("sp", hscr[b, tsl_(t), :], ha[t % 3][:], [f"ha{t % 3}"], [f"hscr{t % 3}"])
                for k in range(8):
                    B.tr(psT[:, k * 128:(k + 1) * 128], hb[t % 3][:, k * 128:(k + 1) * 128], ident[:],
                         [f"hb{t % 3}", "ident"], ["psT"])

            def p1_J(t):
                B.cp("act", hT[:, :, tsl_(t)], psT[:].rearrange("p (k t) -> p k t", k=8), ["psT"], ["hT"])

            pipeline(16, [(0, p1_load)] + ln_stages(L, lambda t: (xt[t % 4][:], f"xt{t % 4}"), gbc, "gbc", bbc, "bbc", 1)
                     + [(4, p1_H), (5, p1_J)])
            P.barrier()
            B.release(m)

            m = B.mark()
            memb = B.T("memb", [128, 2, 1024], BF16)
            memT = B.T("memT", [128, 8, NMEM], BF16)
            mkT = B.T("mkT", [128, 4, NMEM], BF16)
            mv = B.T("mv", [128, 2, 512], BF16)
            qTm = [B.T("qTm", [128, S], BF16) for _ in range(2)]
            sgm = [B.T("sgm", [128, S], BF16) for _ in range(2)]
            accm = [B.T("accm", [128, S], F32) for _ in range(2)]
            denm = [B.T("denm", [128, S], F32) for _ in range(2)]
            sqT = B.T("sqT", [128, S], BF16)
            ab = attn_bufs()
            wmk = w_mem_kv.rearrange("(k p) c -> p k c", p=128)
            B.dma("pool", memb[:], mem[b].rearrange("(t p) d -> p t d", p=128), [], ["memb"])
            B.dma("pool", wst[0][:], wmk[:, :, 0:512], [], ["wst0"])
            B.dma("pool", wst[1][:], wmk[:, :, 512:1024], [], ["wst1"])
            for t in range(2):
                for k in range(8):
                    B.tr(psT[:, k * 128:(k + 1) * 128], memb[:, t, k * 128:(k + 1) * 128], ident[:], ["memb", "ident"], ["psT"])
                B.cp("act", memT[:, :, t * 128:(t + 1) * 128], psT[:].rearrange("p (k t) -> p k t", k=8), ["psT"], ["memT"])
            for h in range(4):
                bank, bn = next_w()
                for k in range(8):
                    B.mm(bank[:, 0:NMEM], wst[0][:, k, h * 128:(h + 1) * 128], memT[:, k, :], k == 0, k == 7,
                         ["wst0", "memT"], [bn])
                B.cp("dve", mkT[:, h, :], bank[:, 0:NMEM], [bn], ["mkT"])
            for t in range(2):
                bank, bn = next_w()
                for k in range(8):
                    B.mm(bank[:, :], memT[:, k, t * 128:(t + 1) * 128], wst[1][:, k, :], k == 0, k == 7, ["wst1", "memT"], [bn])
                B.cp("dve", mv[:, t, :], bank[:, :], [bn], ["mv"])
            load_w_in(0, 5024, 512)
            load_w_in(1, 5536, 512)
            for h in range(4):
                sl = h % 2
                for qb in range(4):
                    bank, bn = proj_fm(lambda k: wst[0][:, k, h * 128:(h + 1) * 128], 8, lambda k: hT[:, k, blk(qb)], 128,
                                       ["wst0", "hT"])
                    B.cp("dve", qTm[sl][:, blk(qb)], bank[:, :], [bn], [f"qTm{sl}"])
                    run_deferred_ab(ab, 1 << 60, maxn=3)
                    bank, bn = proj_fm(lambda k: wst[1][:, k, h * 128:(h + 1) * 128], 8, lambda k: hT[:, k, blk(qb)], 128,
                                       ["wst1", "hT"])
                    B.act(sgm[sl][:, blk(qb)], bank[:, :], AF.Silu, [bn], [f"sgm{sl}"])
                    run_deferred_ab(ab, 1 << 60, maxn=3)
                run_deferred_ab(ab, 1 << 60)
                if h > 0:
                    ssq_mms(sqT, 12 + h - 1)
                attention(ab, KT=lambda kt: mkT[:, h, kt * 128:(kt + 1) * 128], QT=lambda qb: qTm[sl][:, blk(qb)],
                          V=lambda kt: mv[:, kt, h * 128:(h + 1) * 128], vcols=128, nkt=2, r0=0, r1=128, den=("sep",),
                          scale=128.0 ** -0.5, maskfn=None, sg=sgm[sl], sgres=f"sgm{sl}", gvec=gom_t[:, h:h + 1],
                          ychunk=12 + h, sqT=sqT, reads=["mkT", "mv", f"qTm{sl}"],
                          bulk=(accm[sl], f"accm{sl}", denm[sl], f"denm{sl}"))
                bulk_epilogue(ab, accm[sl], f"accm{sl}", 0, 128, 0, sgm[sl], f"sgm{sl}", gom_t[:, h:h + 1], 12 + h, sqT,
                              ab["step"], dtile=denm[sl], dres=f"denm{sl}")
            run_deferred_ab(ab, 1 << 60)
            ssq_mms(sqT, 15)
            P.barrier()
            B.release(m)

            m = B.mark()
            cosB = B.T("cosB", [128, S], F32)
            sinB = B.T("sinB", [128, S], F32)
            wuq = B.T("wuq", [128, 2, 768], BF16)
            wuq_sw = B.T("wuq_sw", [128, 2, 768], BF16)
            wukv = B.T("wukv", [128, 1024], BF16)
            B.dma("pool", wuq[:], w_uq.rearrange("(k p) c -> p k c", p=128), [], ["wuq"])
            B.dma("pool", wukv[:], w_ukv, [], ["wukv"])
            load_w_in(0, 4096, 512)
            load_w_in(1, 4512, 512)
            rope_tables(b, 2, 3, cosB, sinB, "B")
            wkr_sw = B.T("wkr_sw", [128, 8, 96], BF16)
            cqn = B.T("cqn", [128, 2, S], BF16)
            ckvn = B.T("ckvn", [128, S], BF16)
            kr = B.T("kr", [128, S], BF16)
            V2B = B.T("V2B", [128, 16, 192], BF16)
            sqt = [B.T("sqt", [128, 512], BF16) for _ in range(2)]
            R = [B.T("R", [128, 512], F32) for _ in range(2)]
            rt1 = B.T("rt1", [128, 512], F32)
            rt2 = B.T("rt2", [128, 512], F32)
            QTh = B.T("QTh", [128, S], BF16)
            KTh = B.T("KTh", [128, S], BF16)
            sgb = B.T("sgb", [128, S], BF16)
            sqT = B.T("sqT", [128, S], BF16)
            ab = attn_bufs()
            B.memset("pool", wuq_sw[:], 0.0, ["wuq_sw"])
            wuq4 = wuq[:].rearrange("p k (h t) -> p k h t", t=96)
            wsw4 = wuq_sw[:].rearrange("p k (h t) -> p k h t", t=96)
            for kc in range(2):
                B.cp("pool", wsw4[:, kc, :, 64:80], wuq4[:, kc, :, 80:96], ["wuq", "wuq_sw"], ["wuq_sw"])
                B.cp("pool", wsw4[:, kc, :, 80:96], wuq4[:, kc, :, 64:80], ["wuq", "wuq_sw"], ["wuq_sw"])
            B.memset("pool", wkr_sw[:], 0.0, ["wkr_sw"])
            B.cp("pool", wkr_sw[:, :, 64:80], wst[0][:, :, 400:416], ["wst0", "wkr_sw"], ["wkr_sw"])
            B.cp("pool", wkr_sw[:, :, 80:96], wst[0][:, :, 384:400], ["wst0", "wkr_sw"], ["wkr_sw"])
            B.memset("pool", V2B[:, :, 64:65], 1.0, ["V2B"])
            B.memset("pool", V2B[:, :, 65:128], 0.0, ["V2B"])
            rc = 0
            for qb in range(4):
                hrhs = lambda k: hT[:, k, blk(qb)]
                cb = []
                for c in range(2):
                    cb.append(proj_fm(lambda k: wst[0][:, k, c * 128:(c + 1) * 128], 8, hrhs, 128, ["wst0", "hT"]))
                for c in range(2):
                    B.act(sqt[c][:], cb[c][0][:, :], AF.Square, [cb[c][1]], [f"sqt{c}"])
                sbank, sbn = next_w()
                for c in range(2):
                    B.mm(sbank[:, :], ones_bf[:, :], sqt[c][:], c == 0, c == 1, [f"sqt{c}", "ones_bf"], [sbn])
                j = rc % 2
                rc += 1
                B.ts("dve", R[j][:], sbank[:, :], 1.0 / 256, EPS, ALU.mult, ALU.add, [sbn], [f"R{j}"])
                B.act(R[j][:], R[j][:], AF.Sqrt, [f"R{j}"], [f"R{j}"])
                P.add("dve", lambda e, j=j: e.reciprocal(out=R[j][:], in_=R[j][:]), [f"R{j}"], [f"R{j}"])
                for c in range(2):
                    B.stt(cqn[:, c, blk(qb)], cb[c][0][:, :], gcq_t[:, c:c + 1], R[j][:], ALU.mult, ALU.mult,
                          [cb[c][1], "gvec", f"R{j}"], ["cqn"])
                kb, kbn = proj_fm(lambda k: wst[0][:, k, 256:384], 8, hrhs, 128, ["wst0", "hT"])
                B.act(sqt[0][:], kb[:, :], AF.Square, [kbn], ["sqt0"])
                sbank, sbn = next_w()
                B.mm(sbank[:, :], ones_bf[:, :], sqt[0][:], True, True, ["sqt0", "ones_bf"], [sbn])
                j = rc % 2
                rc += 1
                B.ts("dve", R[j][:], sbank[:, :], 1.0 / 128, EPS, ALU.mult, ALU.add, [sbn], [f"R{j}"])
                B.act(R[j][:], R[j][:], AF.Sqrt, [f"R{j}"], [f"R{j}"])
                P.add("dve", lambda e, j=j: e.reciprocal(out=R[j][:], in_=R[j][:]), [f"R{j}"], [f"R{j}"])
                B.stt(ckvn[:, blk(qb)], kb[:, :], gckv_t[:, 0:1], R[j][:], ALU.mult, ALU.mult, [kbn, "gvec", f"R{j}"], ["ckvn"])
                pa, pan = proj_fm(lambda k: wst[0][:, k, 320:416], 8, hrhs, 96, ["wst0", "hT"])
                pb, pbn = proj_fm(lambda k: wkr_sw[:, k, 0:96], 8, hrhs, 96, ["wkr_sw", "hT"])
                B.tt("dve", rt1[64:96, :], pa[64:96, :], cosB[64:96, blk(qb)], ALU.mult, [pan, "cosB"], ["rt1"])
                B.tt("dve", rt2[64:96, :], pb[64:96, :], sinB[64:96, blk(qb)], ALU.mult, [pbn, "sinB"], ["rt2"])
                B.tt("pool", kr[64:96, blk(qb)], rt1[64:96, :], rt2[64:96, :], ALU.add, ["rt1", "rt2"], ["kr"])
            wv3 = wukv[:].rearrange("p (h t) -> p h t", t=128)
            for h in range(8):
                pr = h // 2
                odd = h % 2
                wsel[0] = (0, 1, 4)
                for qb in range(4):
                    pa, pan = proj_fm(lambda k: wuq[:, k, h * 96:(h + 1) * 96], 2, lambda k: cqn[:, k, blk(qb)], 96, ["wuq", "cqn"])
                    pb, pbn = proj_fm(lambda k: wuq_sw[:, k, h * 96:(h + 1) * 96], 2, lambda k: cqn[:, k, blk(qb)], 96,
                                      ["wuq_sw", "cqn"])
                    B.cp("act", QTh[0:64, blk(qb)], pa[0:64, :], [pan], ["QTh"])
                    B.tt("dve", rt1[64:96, :], pa[64:96, :], cosB[64:96, blk(qb)], ALU.mult, [pan, "cosB"], ["rt1"])
                    B.tt("dve", rt2[64:96, :], pb[64:96, :], sinB[64:96, blk(qb)], ALU.mult, [pbn, "sinB"], ["rt2"])
                    B.tt("pool", QTh[64:96, blk(qb)], rt1[64:96, :], rt2[64:96, :], ALU.add, ["rt1", "rt2", "QTh"], ["QTh"])
                    pc, pcn = proj_fm(lambda k: wukv[:, h * 128:h * 128 + 64], 1, lambda k: ckvn[:, blk(qb)], 64, ["wukv", "ckvn"])
                    B.cp("act", KTh[0:64, blk(qb)], pc[0:64, :], [pcn], ["KTh"])
                B.cp("pool", KTh[64:96, :], kr[64:96, :], ["kr", "KTh"], ["KTh"])
                wsel[0] = (0, 1, 2, 3, 4)
                run_deferred_ab(ab, 1 << 60)
                if not odd:
                    if h > 0:
                        ssq_mms(sqT, 8 + pr - 1)
                    for qb in range(4):
                        bank, bn = proj_fm(lambda k: wst[1][:, k, pr * 128:(pr + 1) * 128], 8, lambda k: hT[:, k, blk(qb)], 128,
                                           ["wst1", "hT"])
                        B.act(sgb[:, blk(qb)], bank[:, :], AF.Silu, [bn], ["sgb"])
                    for t4 in range(4):
                        bank, bn = next_w()
                        for tq in range(4):
                            tt_ = t4 * 4 + tq
                            B.mm(bank[:, tq * 128:(tq + 1) * 128], ckvn[:, tt_ * 128:(tt_ + 1) * 128],
                                 wv3[:, 2 * pr:2 * pr + 2, 64:128], True, True, ["ckvn", "wukv"], [bn])
                        bv = bank[:, :].rearrange("p (q e t) -> p q e t", q=4, e=2)
                        B.cp("act", V2B[:, t4 * 4:(t4 + 1) * 4, 0:64], bv[:, :, 0, :], [bn, "V2B"], ["V2B"])
                        B.cp("dve", V2B[:, t4 * 4:(t4 + 1) * 4, 128:192], bv[:, :, 1, :], [bn, "V2B"], ["V2B"])
                if not odd:
                    V = lambda kt: V2B[:, kt, 0:128]
                    vcols, r0, r1, den = 128, 0, 64, ("row", 64)
                else:
                    V = lambda kt: V2B[:, kt, 64:192]
                    vcols, r0, r1, den = 128, 64, 128, ("row", 0)
                attention(ab, KT=lambda kt: KTh[0:96, kt * 128:(kt + 1) * 128], QT=lambda qb: QTh[0:96, blk(qb)],
                          V=V, vcols=vcols, nkt=16, r0=r0, r1=r1, den=den, scale=96.0 ** -0.5, maskfn=None,
                          sg=sgb, sgres="sgb", gvec=gob_t[r0:r1, pr:pr + 1], ychunk=8 + pr, sqT=sqT,
                          reads=["V2B", "QTh", "KTh"], flush=False)
            run_deferred_ab(ab, 1 << 60)
            ssq_mms(sqT, 11)
            P.barrier()
            B.release(m)

            m = B.mark()
            cosA = B.T("cosA", [128, S], F32)
            sinA = B.T("sinA", [128, S], F32)
            maskT = B.T("maskT", [128, MASK_W + 128], BF16)
            B.dma("pool", maskT[:], maskd, [], ["maskT"])
            for i in range(4):
                load_w_in(0, 1024 * i, 128, dst0=128 * i)
            rope_tables(b, 0, 1, cosA, sinA, "A")
            V3 = B.T("V3", [128, 16, 192], BF16)
            O3 = [B.T("O3s", [128, S], F32) for _ in range(2)]
            wswq = B.T("wswq", [128, 8, 128], BF16)
            wswk = B.T("wswk", [128, 8, 128], BF16)
            QTc = B.T("QTc", [128, S], BF16)
            KTc = B.T("KTc", [128, S], BF16)
            KTo = B.T("KTo", [128, S], BF16)
            V2A = B.T("V2A", [128, 16, 192], BF16)
            sga = [B.T("sga", [128, S], BF16) for _ in range(2)]
            rt1 = B.T("rt1", [128, 512], F32)
            rt2 = B.T("rt2", [128, 512], F32)
            sqT = B.T("sqT", [128, S], BF16)
            ab = dict(PT=[B.T("PT", [128, 512], BF16) for _ in range(4)],
                      BCS=[B.T("BCS", [128, 512], F32) for _ in range(2)],
                      ON=[B.T("ON", [128, 512], F32) for _ in range(2)],
                      step=0, epi=0, defer=[], bj=0)
            B.memset("pool", KTc[64:128, :], 0.0, ["KTc"])
            B.memset("pool", KTo[0:64, :], 0.0, ["KTc"])
            B.memset("pool", wswq[:], 0.0, ["wswq"])
            B.memset("pool", wswk[:], 0.0, ["wswk"])
            B.memset("pool", V2A[:, :, 64:65], 1.0, ["V2A"])
            B.memset("pool", V2A[:, :, 65:128], 0.0, ["V2A"])
            B.memset("pool", V3[:, :, 64:65], 1.0, ["V3"])
            B.memset("pool", V3[:, :, 65:128], 0.0, ["V3"])

            def maskfn(kt, qb):
                dmin = 128 * kt - 512 * qb - 511
                dmax = 128 * kt + 127 - 512 * qb
                if dmin > 256 or dmax < -256:
                    return None
                off = MASK_X0 - 128 * kt + 512 * qb
                assert 0 <= off and off + 512 <= MASK_W
                return maskT[:, off:off + 512]

            for c in range(8):
                sl = c % 2
                wb = wst[sl]
                sgc = sga[sl]
                sgn = f"sga{sl}"
                if c + 1 < 8:
                    for i in range(4):
                        load_w_in(1 - sl, 1024 * i + (c + 1) * 128, 128, dst0=128 * i)
                wq4 = wb[:, :, 0:128].rearrange("p k (e t) -> p k e t", e=2)
                wk4 = wb[:, :, 128:256].rearrange("p k (e t) -> p k e t", e=2)
                sq4 = wswq[:].rearrange("p k (e t) -> p k e t", e=2)
                sk4 = wswk[:].rearrange("p k (e t) -> p k e t", e=2)
                B.cp("pool", sq4[:, :, :, 0:8], wq4[:, :, :, 8:16], [f"wst{sl}", "wswq"], ["wswq"])
                B.cp("pool", sq4[:, :, :, 8:16], wq4[:, :, :, 0:8], [f"wst{sl}", "wswq"], ["wswq"])
                B.cp("pool", sk4[:, :, :, 0:8], wk4[:, :, :, 8:16], [f"wst{sl}", "wswk"], ["wswk"])
                B.cp("pool", sk4[:, :, :, 8:16], wk4[:, :, :, 0:8], [f"wst{sl}", "wswk"], ["wswk"])
                for qb in range(4):
                    hrhs = lambda k: hT[:, k, blk(qb)]
                    for (dst, dname, c0, wsw, wswn) in ((QTc, "QTc", 0, wswq, "wswq"), (KTc, "KTc", 128, wswk, "wswk")):
                        pa, pan = proj_fm(lambda k: wb[:, k, c0:c0 + 128], 8, hrhs, 128, [f"wst{sl}", "hT"])
                        pb, pbn = proj_fm(lambda k: wsw[:, k, :], 8, hrhs, 128, [wswn, "hT"])
                        B.tt("dve", rt1[:], pa[:, :], cosA[:, blk(qb)], ALU.mult, [pan, "cosA"], ["rt1"])
                        B.tt("dve", rt2[:], pb[:, :], sinA[:, blk(qb)], ALU.mult, [pbn, "sinA"], ["rt2"])
                        if dname == "QTc":
                            B.tt("pool", dst[:, blk(qb)], rt1[:], rt2[:], ALU.add, ["rt1", "rt2"], [dname])
                        else:
                            B.tt("pool", KTc[0:64, blk(qb)], rt1[0:64, :], rt2[0:64, :], ALU.add, ["rt1", "rt2"], [dname])
                            B.tt("pool", KTo[64:128, blk(qb)], rt1[64:128, :], rt2[64:128, :], ALU.add, ["rt1", "rt2"], [dname])
                        run_deferred_ab(ab, 1 << 60, maxn=3)
                    bank, bn = proj_fm(lambda k: wb[:, k, 384:512], 8, hrhs, 128, [f"wst{sl}", "hT"])
                    B.act(sgc[:, blk(qb)], bank[:, :], AF.Silu, [bn], [sgn])
                for t4 in range(4):
                    bank, bn = next_w()
                    for tq in range(4):
                        tt_ = t4 * 4 + tq
                        for k in range(8):
                            B.mm(bank[:, tq * 128:(tq + 1) * 128], hT[:, k, tt_ * 128:(tt_ + 1) * 128], wb[:, k, 256:384],
                                 k == 0, k == 7, [f"wst{sl}", "hT"], [bn])
                    bv = bank[:, :].rearrange("p (q e t) -> p q e t", q=4, e=2)
                    B.cp("act", V2A[:, t4 * 4:(t4 + 1) * 4, 0:64], bv[:, :, 0, :], [bn, "V2A"], ["V2A"])
                    B.cp("dve", V2A[:, t4 * 4:(t4 + 1) * 4, 128:192], bv[:, :, 1, :], [bn, "V2A"], ["V2A"])
                for r4 in range(4):
                    bank, bn = next_w()
                    for tq in range(4):
                        r = r4 * 4 + tq
                        for k in range(8):
                            B.mm(bank[:, tq * 128:(tq + 1) * 128], hT[:, k, r:r + 16 * 127 + 1:16], wb[:, k, 256:384],
                                 k == 0, k == 7, [f"wst{sl}", "hT"], [bn])
                    bv = bank[:, :].rearrange("p (q e t) -> p q e t", q=4, e=2)
                    B.cp("act", V3[:, r4 * 4:(r4 + 1) * 4, 0:64], bv[:, :, 0, :], [bn, "V3"], ["V3"])
                    B.cp("dve", V3[:, r4 * 4:(r4 + 1) * 4, 128:192], bv[:, :, 1, :], [bn, "V3"], ["V3"])
                run_deferred_ab(ab, 1 << 60)
                if c > 0:
                    ssq_mms(sqT, c - 1)
                for odd in range(2):
                    if not odd:
                        V = lambda kt: V2A[:, kt, 0:128]
                        vcols, r0, r1, den = 128, 0, 64, ("row", 64)
                    else:
                        V = lambda kt: V2A[:, kt, 64:192]
                        vcols, r0, r1, den = 128, 64, 128, ("row", 0)
                    KTx = KTo if odd else KTc
                    acc, accres = O3[odd], f"O3s{odd}"
                    attention_p3(ab, KTx, QTc, V3, odd, maskT[:, MASK_W:MASK_W + 128], (acc, accres), ["QTc", "KTc"])
                    attention(ab, KT=lambda kt: KTx[:, kt * 128:(kt + 1) * 128], QT=lambda qb: QTc[:, blk(qb)],
                              V=V, vcols=vcols, nkt=16, r0=r0, r1=r1, den=den, scale=0.125, maskfn=maskfn,
                              sg=sgc, sgres=sgn, gvec=goa_t[r0:r1, c:c + 1], ychunk=c, sqT=sqT,
                              reads=["V2A", "QTc", "KTc"], bulk=(acc, accres))
                    bulk_epilogue(ab, acc, accres, r0, r1, den[1], sgc, sgn, goa_t[r0:r1, c:c + 1], c, sqT,
                                  ab["step"] + (17 if not odd else 0))
            run_deferred_ab(ab, 1 << 60)
            ssq_mms(sqT, 7)
            P.barrier()
            B.release(m)

            m = B.mark()
            wo = B.T("wo", [128, 16, D], BF16)
            gpc = B.T("gpc", [128, 1024], F32)
            bpc = B.T("bpc", [128, 1024], F32)
            hf = [B.T("hf", [128, 1024], F32) for _ in range(3)]
            acc = [B.T("acc", [128, 1024], F32) for _ in range(3)]
            rg = [B.T("rg", [128, 4], F32) for _ in range(3)]
            L = ln_alloc(4, 2)
            wo_v = w_out.rearrange("(k p) c -> p k c", p=128)
            for q4 in range(4):
                B.dma("pool", wo[:, q4 * 4:(q4 + 1) * 4, :], wo_v[:, q4 * 4:(q4 + 1) * 4, :], [], [f"wo{q4}"])
            B.dma("sp", gpc[:], g_post.partition_broadcast(128), [], ["gpc"])
            B.dma("sp", bpc[:], b_post.partition_broadcast(128), [], ["bpc"])
            groups = ((0, 8, 1024.0), (8, 12, 512.0), (12, 16, 512.0))

            def tsl_(t):
                return slice(t * 128, (t + 1) * 128)

            def f_load(t):
                B.dma("sp", hf[t % 3][:], hscr[b, tsl_(t), :], [], [f"hf{t % 3}"])

            def f_mm(t):
                for n in range(2):
                    for gi, (c0, c1, width) in enumerate(groups):
                        bi = n * 3 + gi
                        for c in range(c0, c1):
                            B.mm(W[bi][:, :], yT[:, c, tsl_(t)], wo[:, c, blk(n)], c == c0, c == c1 - 1, ["yT", f"wo{c // 4}"],
                                 [f"{WN[bi]}"])

            def f_rg(t):
                k = t % 3
                for gi, (c0, c1, width) in enumerate(groups):
                    P.add("dve", lambda e, gi=gi, c0=c0, c1=c1: e.reduce_sum(
                        out=rg[k][:, gi:gi + 1], in_=ssqP[:, t * 16 + c0:t * 16 + c1], axis=mybir.AxisListType.X),
                        ["ssqP"], [f"rg{k}"])
                    B.ts("dve", rg[k][:, gi:gi + 1], rg[k][:, gi:gi + 1], 1.0 / width, EPS, ALU.mult, ALU.add,
                         [f"rg{k}"], [f"rg{k}"])
                B.tt("pool", rg[k][:, 0:3], rg[k][:, 0:3], mhalf[:, 0:3], ALU.pow, [f"rg{k}", "mhalf"], [f"rg{k}"])

            def f_stt(t):
                k = t % 3
                for n in range(2):
                    for gi in range(3):
                        bi = n * 3 + gi
                        src = hf[k][:, blk(n)] if gi == 0 else acc[k][:, blk(n)]
                        srcn = f"hf{k}" if gi == 0 else f"acc{k}"
                        B.stt(acc[k][:, blk(n)], W[bi][:, :], rg[k][:, gi:gi + 1], src, ALU.mult, ALU.add,
                              [WN[bi], f"rg{k}", srcn], [f"acc{k}"])

            def f_store(t):
                B.dma("sp", out[b, tsl_(t), :], L["t1"][t % 3][:], [f"lt1_{t % 3}"], [f"out{t % 3}"])

            pipeline(16, [(0, f_load), (0, f_mm), (0, f_rg), (1, f_stt)]
                     + ln_stages(L, lambda t: (acc[t % 3][:], f"acc{t % 3}"), gpc, "gpc", bpc, "bpc", 2)
                     + [(5, f_store)])
            P.barrier()
            B.release(m)

        stats = P.emit(esems, dsems)
    return nc, stats


def _host_consts():
    cst = np.zeros((128, 8), np.float32)
    for p in range(128):
        j = p % 64
        if j < 16:
            cst[p, 0] = THETA ** (-(2.0 * (j % 8)) / 16.0)
            cst[p, 1] = -1.0 if j < 8 else 1.0
        else:
            cst[p, 0] = 0.0
            cst[p, 1] = 1.0
        if 64 <= p < 96:
            jj = p - 64
            cst[p, 2] = THETA ** (-(2.0 * (jj % 16)) / 32.0)
            cst[p, 3] = -1.0 if jj < 16 else 1.0
        else:
            cst[p, 2] = 0.0
            cst[p, 3] = 1.0
    kl = np.arange(128)[:, None]
    xx = np.arange(MASK_W)[None, :]
    d = kl - xx + MASK_X0
    f = ((np.abs(d) <= 64).astype(np.float32)
         + ((d % 4 == 0) & (np.abs(d) <= 256)).astype(np.float32))
    mb = np.where(f > 0, np.log(np.maximum(f, 1.0)) / 0.125, -320.0)
    d3 = np.arange(128)[:, None] - np.arange(128)[None, :]
    m3 = np.where(np.abs(d3) <= 64, 0.0, -320.0)
    return cst, np.concatenate([mb, m3], axis=1).astype(np.float32)


_CACHE = {}


def kernel(x, mem, positions, g_emb, b_emb, w_in, g_cq, g_ckv, w_uq, w_ukv, w_mem_kv,
           g_out_a, g_out_b, g_out_m, w_out, g_post, b_post):
    f32 = lambda a: np.ascontiguousarray(np.asarray(a), dtype=np.float32)
    if "nc" not in _CACHE:
        _CACHE["nc"] = build_program()[0]
    nc = _CACHE["nc"]
    cst, maskd = _host_consts()
    x = f32(x)
    mem = f32(mem)
    positions = np.ascontiguousarray(np.asarray(positions), dtype=np.int32)
    shared = dict(
        g_emb=f32(g_emb), b_emb=f32(b_emb), w_in=f32(w_in).reshape(D, D_IN), g_cq=f32(g_cq).reshape(256),
        g_ckv=f32(g_ckv).reshape(128), w_uq=f32(w_uq).reshape(256, 768), w_ukv=f32(w_ukv).reshape(128, 1024),
        w_mem_kv=f32(w_mem_kv).reshape(D, 1024), g_out_a=f32(g_out_a).reshape(1024), g_out_b=f32(g_out_b).reshape(512),
        g_out_m=f32(g_out_m).reshape(512), w_out=f32(w_out).reshape(2048, D), g_post=f32(g_post).reshape(D),
        b_post=f32(b_post).reshape(D), cst=cst, maskd=maskd)
    in_maps = []
    for c in range(N_CORES):
        d = dict(shared)
        d["x"] = x[c * NB:(c + 1) * NB]
        d["mem"] = mem[c * NB:(c + 1) * NB]
        d["positions"] = positions[c * NB:(c + 1) * NB]
        in_maps.append(d)
    res = run_bass_kernel_spmd(nc, in_maps, core_ids=list(range(N_CORES)))
    return np.concatenate([r["out"] for r in res.results], axis=0)
```
